# Optimizing a Trainium2 kernel written in Bass

```python
import math
import jax, jax.numpy as jnp
from jax import lax
import numpy as np

D_MODEL = 1024
BATCH = 16
SEQ = 4096
DEPTH = 1

D_MIX = D_MODEL
D_CONV = D_MIX // 2
D_SSM = D_MIX - D_CONV
CONV_HEADS = 8
CONV_HEAD_DIM = D_CONV // CONV_HEADS
CONV_WIDTH = 31
SSM_GROUP = 16
SSM_GROUPS = D_SSM // SSM_GROUP
SSM_STATE = 64
D_FF = 128 * ((8 * D_MODEL // 3 + 127) // 128)
D_IN = 2 * D_CONV + D_SSM
FFN_RES = 0.5
EPS = 1e-6

kernel_name = "macaron_conv_s5_hybrid_layer"


def rmsnorm(x, g):
    xf = x.astype(jnp.float32)
    xf = xf * lax.rsqrt(jnp.mean(xf * xf, axis=-1, keepdims=True) + EPS)
    return (xf * g.astype(jnp.float32)).astype(x.dtype)


def layernorm(x, g, b):
    xf = x.astype(jnp.float32)
    mu = jnp.mean(xf, axis=-1, keepdims=True)
    xc = xf - mu
    var = jnp.mean(xc * xc, axis=-1, keepdims=True)
    y = xc * lax.rsqrt(var + EPS) * g.astype(jnp.float32) + b.astype(jnp.float32)
    return y.astype(x.dtype)


def swiglu(h, w1, w3, w2):
    return (jax.nn.silu(h @ w1) * (h @ w3)) @ w2


def conv_module(a_val, a_gate, conv_w, conv_b, ln_g, ln_b):
    a = a_val * jax.nn.sigmoid(a_gate)
    a = lax.conv_general_dilated(
        a, conv_w[:, None, :].astype(a.dtype),
        window_strides=(1,), padding=[(CONV_WIDTH - 1, 0)],
        dimension_numbers=("NWC", "WIO", "NWC"),
        feature_group_count=D_CONV) + conv_b
    a = layernorm(a, ln_g, ln_b)
    return jax.nn.silu(a)


def _complex_affine_combine(e1, e2):
    a1r, a1i, b1r, b1i = e1
    a2r, a2i, b2r, b2i = e2
    ar = a2r * a1r - a2i * a1i
    ai = a2r * a1i + a2i * a1r
    br = a2r * b1r - a2i * b1i + b2r
    bi = a2r * b1i + a2i * b1r + b2i
    return (ar, ai, br, bi)


def s5_layer(u, A_re, A_im, log_dt, B_re, B_im, C_re, C_im, D_skip, glu_w, glu_b):
    bsz, seq = u.shape[0], u.shape[1]
    ug = u.astype(jnp.float32).reshape(bsz, seq, SSM_GROUPS, SSM_GROUP)
    dt = jnp.exp(log_dt.astype(jnp.float32))[:, None]
    lr = A_re.astype(jnp.float32)
    li = A_im.astype(jnp.float32)
    zr, zi = lr * dt, li * dt
    mag = jnp.exp(zr)
    abar_r, abar_i = mag * jnp.cos(zi), mag * jnp.sin(zi)
    den = lr * lr + li * li
    nr = abar_r - 1.0
    coef_r = (nr * lr + abar_i * li) / den
    coef_i = (abar_i * lr - nr * li) / den
    br_, bi_ = B_re.astype(jnp.float32), B_im.astype(jnp.float32)
    bb_r = coef_r[..., None] * br_ - coef_i[..., None] * bi_
    bb_i = coef_r[..., None] * bi_ + coef_i[..., None] * br_
    bu_r = jnp.einsum("bsgh,gph->bsgp", ug, bb_r)
    bu_i = jnp.einsum("bsgh,gph->bsgp", ug, bb_i)
    ar_all = jnp.broadcast_to(abar_r, bu_r.shape)
    ai_all = jnp.broadcast_to(abar_i, bu_r.shape)
    _, _, xr, xi = lax.associative_scan(
        _complex_affine_combine, (ar_all, ai_all, bu_r, bu_i), axis=1)
    y = (jnp.einsum("bsgp,ghp->bsgh", xr, C_re.astype(jnp.float32))
         - jnp.einsum("bsgp,ghp->bsgh", xi, C_im.astype(jnp.float32)))
    y = y + D_skip.astype(jnp.float32).reshape(SSM_GROUPS, SSM_GROUP) * ug
    y = y.reshape(bsz, seq, D_SSM).astype(u.dtype)
    y = jax.nn.gelu(y)
    return y * jax.nn.sigmoid(y @ glu_w + glu_b)


def setup_inputs(seed: int = 0) -> dict:
    key = jax.random.key(seed)
    ks = iter(jax.random.split(key, 48))
    L = DEPTH
    f32 = jnp.float32

    def nrm(shape, scale):
        return scale * jax.random.normal(next(ks), shape, f32)

    def gain(shape):
        return 1.0 + nrm(shape, 0.02)

    n_idx = jnp.arange(SSM_STATE, dtype=f32)
    A_re = -0.5 + nrm((L, SSM_GROUPS, SSM_STATE), 0.01)
    A_im = jnp.pi * n_idx[None, None, :] + nrm((L, SSM_GROUPS, SSM_STATE), 0.01)
    log_dt = jax.random.uniform(next(ks), (L, SSM_GROUPS), f32,
                                minval=math.log(1e-3), maxval=math.log(1e-1))
    b_scale = (SSM_GROUP ** -0.5) / math.sqrt(2.0)
    c_scale = (SSM_STATE ** -0.5) / math.sqrt(2.0)
    return {
        "x": nrm((BATCH, SEQ, D_MODEL), 1.0),
        "norm_ffn1": gain((L, D_MODEL)),
        "ffn1_w1": nrm((L, D_MODEL, D_FF), D_MODEL ** -0.5),
        "ffn1_w3": nrm((L, D_MODEL, D_FF), D_MODEL ** -0.5),
        "ffn1_w2": nrm((L, D_FF, D_MODEL), D_FF ** -0.5),
        "norm_mix": gain((L, D_MODEL)),
        "w_in": nrm((L, D_MODEL, D_IN), D_MODEL ** -0.5),
        "conv_w": nrm((L, CONV_WIDTH, D_CONV), CONV_WIDTH ** -0.5),
        "conv_b": nrm((L, D_CONV), 0.02),
        "conv_ln_g": gain((L, D_CONV)),
        "conv_ln_b": nrm((L, D_CONV), 0.02),
        "conv_out_g": gain((L, D_CONV)),
        "ssm_A_re": A_re,
        "ssm_A_im": A_im,
        "ssm_log_dt": log_dt,
        "ssm_B_re": nrm((L, SSM_GROUPS, SSM_STATE, SSM_GROUP), b_scale),
        "ssm_B_im": nrm((L, SSM_GROUPS, SSM_STATE, SSM_GROUP), b_scale),
        "ssm_C_re": nrm((L, SSM_GROUPS, SSM_GROUP, SSM_STATE), c_scale),
        "ssm_C_im": nrm((L, SSM_GROUPS, SSM_GROUP, SSM_STATE), c_scale),
        "ssm_D": 1.0 + nrm((L, D_SSM), 0.1),
        "ssm_glu_w": nrm((L, D_SSM, D_SSM), D_SSM ** -0.5),
        "ssm_glu_b": nrm((L, D_SSM), 0.02),
        "ssm_out_g": gain((L, D_SSM)),
        "w_out": nrm((L, D_MIX, D_MODEL), D_MIX ** -0.5),
        "norm_ffn2": gain((L, D_MODEL)),
        "ffn2_w1": nrm((L, D_MODEL, D_FF), D_MODEL ** -0.5),
        "ffn2_w3": nrm((L, D_MODEL, D_FF), D_MODEL ** -0.5),
        "ffn2_w2": nrm((L, D_FF, D_MODEL), D_FF ** -0.5),
        "norm_final": gain((D_MODEL,)),
    }


def reference(x, norm_ffn1, ffn1_w1, ffn1_w3, ffn1_w2, norm_mix, w_in,
              conv_w, conv_b, conv_ln_g, conv_ln_b, conv_out_g,
              ssm_A_re, ssm_A_im, ssm_log_dt, ssm_B_re, ssm_B_im, ssm_C_re, ssm_C_im,
              ssm_D, ssm_glu_w, ssm_glu_b, ssm_out_g, w_out,
              norm_ffn2, ffn2_w1, ffn2_w3, ffn2_w2, norm_final):
    for l in range(DEPTH):
        x = x + FFN_RES * swiglu(rmsnorm(x, norm_ffn1[l]), ffn1_w1[l], ffn1_w3[l], ffn1_w2[l])

        h = rmsnorm(x, norm_mix[l])
        proj = h @ w_in[l]
        a_val = proj[..., :D_CONV]
        a_gate = proj[..., D_CONV:2 * D_CONV]
        u = proj[..., 2 * D_CONV:]

        a = conv_module(a_val, a_gate, conv_w[l], conv_b[l], conv_ln_g[l], conv_ln_b[l])
        a = rmsnorm(a, conv_out_g[l])

        s = s5_layer(u, ssm_A_re[l], ssm_A_im[l], ssm_log_dt[l], ssm_B_re[l], ssm_B_im[l],
                     ssm_C_re[l], ssm_C_im[l], ssm_D[l], ssm_glu_w[l], ssm_glu_b[l])
        s = rmsnorm(s, ssm_out_g[l])

        mixed = jnp.concatenate([a, s], axis=-1)
        x = x + mixed @ w_out[l]

        x = x + FFN_RES * swiglu(rmsnorm(x, norm_ffn2[l]), ffn2_w1[l], ffn2_w3[l], ffn2_w2[l])
    return rmsnorm(x, norm_final)
```

```python
import contextlib
import math
import numpy as np
import concourse.bass as bass
import concourse.mybir as mybir
from concourse.bass_utils import run_bass_kernel_spmd

F32 = mybir.dt.float32
BF16 = mybir.dt.bfloat16
I32 = mybir.dt.int32
AF = mybir.ActivationFunctionType
ALU = mybir.AluOpType

NCORES = 8
D = 1024
DC = 8
DFF = 2816
FC = 22
T = 512
SEQ = 4096
TPS = SEQ // T
NSEQ = 2
NT = TPS * NSEQ
TF = 128
NF = T // TF
EPS = 1e-6
RS = 3
RB = 2
NSMALL = 55
NBIG = 16
WIN_PAIRS = [(4, 0), (5, 1), (6, 2), (7, 3), (8, 9), (10, 11)]


class Tok:
    __slots__ = ("w", "r", "name")

    def __init__(self, name=""):
        self.w = None
        self.r = {}
        self.name = name


class Prog:
    ENG = ("pe", "act", "dve", "pool", "sp")
    NDSEM = 10

    def __init__(self, nc, stack):
        self.nc = nc
        self.q = {e: [] for e in self.ENG}
        self.cnt = {e: 0 for e in self.ENG}
        self.seen = {e: {} for e in self.ENG}
        self.sems = {}
        for e in self.ENG:
            self.sems[e] = stack.enter_context(nc.semaphore("s_" + e))
        for qn in ("sp", "pool"):
            for i in range(self.NDSEM):
                self.sems[("d" + qn, i)] = stack.enter_context(nc.semaphore("d%s_%d" % (qn, i)))
        self.dma_i = {"sp": 0, "pool": 0}

    def _deps(self, eng, reads, writes):
        deps = {}

        def add(t):
            if t is None:
                return
            k, v = t
            if deps.get(k, 0) < v:
                deps[k] = v

        for b in reads:
            add(b.w)
        for b in writes:
            if b.w is not None and not (b.w[0] == eng == "pe"):
                add(b.w)
            for k, v in b.r.items():
                if not (k == eng == "pe"):
                    add((k, v))
        return deps

    def _commit(self, eng, deps, tok, reads, writes):
        waits = []
        seen = self.seen[eng]
        for k, v in deps.items():
            if seen.get(k, 0) < v:
                seen[k] = v
                waits.append((k, v))
        k, v = tok
        for b in reads:
            if b.r.get(k, 0) < v:
                b.r[k] = v
        for b in writes:
            b.w = tok
            b.r = {}
        return waits

    def op(self, eng, fn, reads=(), writes=()):
        deps = self._deps(eng, reads, writes)
        self.cnt[eng] += 1
        tok = (eng, self.cnt[eng])
        waits = self._commit(eng, deps, tok, reads, writes)
        self.q[eng].append((waits, fn, (eng, 1)))
        return tok

    def dma(self, queue, fn, reads=(), writes=()):
        deps = self._deps(queue, reads, writes)
        i = self.dma_i[queue]
        self.dma_i[queue] += 1
        key = ("d" + queue, i % self.NDSEM)
        val = 16 * (i // self.NDSEM + 1)
        if val > 16 and deps.get(key, 0) < val - 16:
            deps[key] = val - 16
        tok = (key, val)
        waits = self._commit(queue, deps, tok, reads, writes)
        self.q[queue].append((waits, fn, (key, 16)))
        return tok

    def wait_all(self, eng, toks):
        deps = {}
        for k, v in toks:
            if deps.get(k, 0) < v:
                deps[k] = v
        waits = []
        for k, v in deps.items():
            if self.seen[eng].get(k, 0) < v:
                self.seen[eng][k] = v
                waits.append((k, v))
        self.q[eng].append((waits, None, None))

    def emit(self, block):
        sems = self.sems

        def replay(name):
            def body(e):
                for waits, fn, inc in self.q[name]:
                    for k, v in waits:
                        e.wait_ge(sems[k], v)
                    if fn is None:
                        continue
                    ins = fn(e)
                    ins.then_inc(sems[inc[0]], inc[1])
            return body

        block.tensor(replay("pe"))
        block.scalar(replay("act"))
        block.vector(replay("dve"))
        block.gpsimd(replay("pool"))
        block.sync(replay("sp"))


def toks(n, name):
    return [Tok("%s%d" % (name, i)) for i in range(n)]


def build_nc(ntiles=NT, stages=None):
    nc = bass.Bass("TRN2", target_bir_lowering=False)
    NTOK = NT * T
    xT = nc.dram_tensor("xT", [D, NTOK], F32, kind="ExternalInput").ap()
    oT = nc.dram_tensor("oT", [D, NTOK], F32, kind="ExternalOutput").ap()
    wsm_d = nc.dram_tensor("wsmall", [NSMALL, 128, 2048], F32, kind="ExternalInput").ap()
    wbg_d = nc.dram_tensor("wbig", [NBIG, 128, DFF], F32, kind="ExternalInput").ap()
    gains_d = nc.dram_tensor("gains", [128, 4 * DC], F32, kind="ExternalInput").ap()
    vecs_d = nc.dram_tensor("vecs", [128, 8 * 4], F32, kind="ExternalInput").ap()
    cw_d = nc.dram_tensor("cw", [128, 4 * 31], F32, kind="ExternalInput").ap()
    ident_d = nc.dram_tensor("ident", [128, 32], F32, kind="ExternalInput").ap()
    aq_d = nc.dram_tensor("aq", [128, 3 * 16], F32, kind="ExternalInput").ap()
    sc_d = nc.dram_tensor("sc", [128, 5 * 512], F32, kind="ExternalInput").ap()
    cq_d = nc.dram_tensor("cq", [128, 2 * 512], F32, kind="ExternalInput").ap()
    idf_d = nc.dram_tensor("idf", [128, 256], F32, kind="ExternalInput").ap()

    with contextlib.ExitStack() as st:
        def sb(name, shape, dt):
            return st.enter_context(nc.sbuf_tensor(name, shape, dt))

        def ps(name):
            return st.enter_context(nc.psum_tensor(name, [128, 512], F32))

        xbs = [sb("xb%d" % i, [128, DC, T], F32) for i in range(3)]
        hb = sb("hb", [128, DC, T], BF16)
        sq = sb("sq", [128, 4, T], BF16)
        stt_ = sb("stt", [128, 3, T], F32)
        sta = sb("sta", [128, 2, T], F32)
        act = sb("act", [128, FC, T], BF16)
        ms = sb("ms", [128, 12, T], BF16)
        stmp2 = sb("stmp2", [128, 2, T], BF16)
        rtmp = sb("rtmp", [128, 2, T], F32)
        stmp = sb("stmp", [128, 2, T], BF16)
        ws = sb("ws", [128, RS, 2048], BF16)
        wb = sb("wb", [128, RB, DFF], BF16)
        diag = sb("diag", [128, 4 * 31, 32], BF16)
        abuf = sb("abuf", [128, 4, 30 + T], BF16)
        co = sb("co", [128, 4, T], F32)
        u32 = sb("u32", [128, 4, T], F32)
        s5tb = sb("s5tb", [128, 2, T], BF16)
        s5v = sb("s5v", [128, 2, T], F32)
        s5w = sb("s5w", [128, 2, T], F32)
        s5x = sb("s5x", [128, 4, T], BF16)
        ctabb = sb("ctabb", [128, 16, TF], BF16)
        stabb = sb("stabb", [128, 16, TF], BF16)
        s5wb = sb("s5wb", [128, 2, T], BF16)
        rtab = sb("rtab", [128, 16, TF], F32)
        btabF = sb("btabF", [128, 2, 16 * 128], BF16)
        ctb = sb("ctb", [128, 2, 512], BF16)
        gains = sb("gains_s", [128, 4 * DC], F32)
        vecs = sb("vecs_s", [128, 32], F32)
        cw = sb("cw_s", [128, 4 * 31], F32)
        ident = sb("ident_s", [128, 32], F32)
        ones_bf = sb("ones_bf", [128, 128], BF16)
        aq = sb("aq_s", [128, 48], F32)
        qs = sb("qs", [128, 12, 16], F32)
        carry = sb("carry", [128, 4, 8], F32)
        ctmp = sb("ctmp", [128, 4, 4], F32)
        btab = act[:, 0:4, :].rearrange("p a b -> p (a b)").bitcast(F32).rearrange("p (a b) -> p a b", a=2)
        xb0 = xbs[0]
        hbm = ms[:, 4:12, :]
        ctab = xbs[1][:, 0:4, :].rearrange("p a (b c) -> p (a b) c", c=TF)
        stab = xbs[1][:, 4:8, :].rearrange("p a (b c) -> p (a b) c", c=TF)
        pA = [ps("pA%d" % i) for i in range(4)]
        pB = [ps("pB%d" % i) for i in range(2)]
        pS = ps("pS")
        pX = ps("pX")

        P = Prog(nc, st)
        block = st.enter_context(nc.Block())

        t_xs = [toks(DC, "x%d_" % i) for i in range(3)]
        t_h = toks(DC, "h")
        t_sq = toks(4, "sq")
        t_st = toks(3, "st")
        t_sta = toks(2, "sta")
        t_act = toks(FC, "act")
        t_ms = toks(12, "ms")
        t_hm = t_ms[4:12]
        t_stmp2 = toks(2, "stmp2")
        t_rtmp = toks(2, "rtmp")
        t_stmp = toks(2, "stmp")
        t_ws2 = [toks(2, "ws%d_" % i) for i in range(RS)]
        t_wb2 = [toks(2, "wb%d_" % i) for i in range(RB)]
        t_diag = Tok("diag")
        t_abuf = toks(4, "abuf")
        t_halo = toks(4, "halo")
        t_co = toks(4, "co")
        t_u32 = toks(4, "u32")
        t_s5t = toks(4, "s5t")
        t_s5w = toks(2, "s5w")
        t_s5wb = toks(2, "s5wb")
        t_s5x = toks(4, "s5x")
        t_tab = Tok("tab")
        t_const = Tok("const")
        t_pA = toks(4, "pA")
        t_pB = toks(2, "pB")
        t_pS = Tok("pS")
        t_pX = Tok("pX")
        t_carry = toks(4, "carry")
        t_ctmp = Tok("ctmp")
        t_setup = Tok("setup")

        def g_ap(n, c):
            return gains[:, n * DC + c:n * DC + c + 1]

        def vec_ap(n, c):
            return vecs[:, n * 4 + c:n * 4 + c + 1]

        LEVEL = stages if stages is not None else 9

        for (dst, src) in ((gains, gains_d), (vecs, vecs_d), (cw, cw_d), (ident, ident_d), (aq, aq_d)):
            P.dma("sp", lambda e, dst=dst, src=src: e.dma_start(out=dst[:], in_=src), writes=[t_setup])

        P.dma("sp", lambda e: e.dma_start(out=xb0[:, 0:5, :].rearrange("p a b -> p (a b)"), in_=sc_d), writes=[t_setup])
        P.dma("sp", lambda e: e.dma_start(out=xb0[:, 5:7, :].rearrange("p a b -> p (a b)"), in_=cq_d), writes=[t_setup])
        P.op("dve", lambda e: e.memset(ones_bf[:], 1.0), writes=[t_const])

        def S(fn, eng="dve"):
            P.op(eng, fn, reads=[t_setup, t_const], writes=[t_setup])

        for c in range(4):
            for k in range(31):
                S(lambda e, c=c, k=k: e.tensor_scalar(
                    out=diag[:, c * 31 + k, :], in0=ident[:], scalar1=cw[:, c * 31 + k:c * 31 + k + 1],
                    scalar2=None, op0=ALU.mult))

        PI = math.pi

        def lam_setup(are, aim, ldt, w):
            dt_, zr, th, r_, tmp, ang, sn, cs_, k_i = [w(i) for i in range(9)]
            S(lambda e: e.activation(out=dt_, in_=ldt, func=AF.Exp), "act")
            S(lambda e: e.tensor_tensor(out=zr, in0=are, in1=dt_, op=ALU.mult))
            S(lambda e: e.tensor_tensor(out=th, in0=aim, in1=dt_, op=ALU.mult))
            S(lambda e: e.activation(out=r_, in_=zr, func=AF.Exp), "act")
            for which, out_ in ((0, sn), (1, cs_)):
                off = 0.0 if which == 0 else PI / 2
                S(lambda e, off=off: e.tensor_scalar(out=ang, in0=th, scalar1=off, scalar2=None, op0=ALU.add))
                S(lambda e: e.tensor_scalar(out=tmp, in0=ang, scalar1=1.0 / (2 * PI), scalar2=None, op0=ALU.mult))
                S(lambda e: e.tensor_copy(out=k_i.bitcast(I32), in_=tmp))
                S(lambda e: e.tensor_copy(out=tmp, in_=k_i.bitcast(I32)))
                S(lambda e: e.scalar_tensor_tensor(out=ang, in0=tmp, scalar=-2 * PI, in1=ang, op0=ALU.mult, op1=ALU.add))
                S(lambda e: e.tensor_scalar(out=tmp, in0=ang, scalar1=-PI, scalar2=2 * PI, op0=ALU.is_lt, op1=ALU.mult))
                S(lambda e: e.tensor_tensor(out=ang, in0=ang, in1=tmp, op=ALU.add))
                S(lambda e: e.tensor_scalar(out=tmp, in0=ang, scalar1=PI, scalar2=-2 * PI, op0=ALU.is_gt, op1=ALU.mult))
                S(lambda e: e.tensor_tensor(out=ang, in0=ang, in1=tmp, op=ALU.add))
                S(lambda e: e.tensor_scalar(out=ang, in0=ang, scalar1=-3.1415925, scalar2=3.1415925, op0=ALU.max, op1=ALU.min))
                S(lambda e, out_=out_: e.activation(out=out_, in_=ang, func=AF.Sin), "act")
            return dict(r=r_, cos=cs_, sin=sn, dt=dt_)

        lq = lam_setup(aq[:, 0:16], aq[:, 16:32], aq[:, 32:48], lambda i: qs[:, i, :])
        S(lambda e: e.memset(ctab[:, :, 0:1], 1.0))
        S(lambda e: e.memset(stab[:, :, 0:1], 0.0))
        S(lambda e: e.tensor_copy(out=ctab[:, :, 1], in_=lq["cos"]))
        S(lambda e: e.tensor_copy(out=stab[:, :, 1], in_=lq["sin"]))
        k = 1
        while k < TF:
            for gc in range(16):
                er = ctab[:, gc, k:k + 1]
                ei = stab[:, gc, k:k + 1]
                n = min(k, TF - k)
                if n > 1:
                    src_c = ctab[:, gc, 1:n]
                    src_s = stab[:, gc, 1:n]
                    dst_c = ctab[:, gc, k + 1:k + n]
                    dst_s = stab[:, gc, k + 1:k + n]
                    tmpv = co[:, 0, 0:n - 1]
                    S(lambda e, src_s=src_s, ei=ei, tmpv=tmpv: e.tensor_scalar(out=tmpv, in0=src_s, scalar1=ei, scalar2=None, op0=ALU.mult))
                    S(lambda e, src_c=src_c, er=er, tmpv=tmpv, dst_c=dst_c: e.scalar_tensor_tensor(out=dst_c, in0=src_c, scalar=er, in1=tmpv, op0=ALU.mult, op1=ALU.subtract))
                    S(lambda e, src_s=src_s, er=er, tmpv=tmpv: e.tensor_scalar(out=tmpv, in0=src_s, scalar1=er, scalar2=None, op0=ALU.mult))
                    S(lambda e, src_c=src_c, ei=ei, tmpv=tmpv, dst_s=dst_s: e.scalar_tensor_tensor(out=dst_s, in0=src_c, scalar=ei, in1=tmpv, op0=ALU.mult, op1=ALU.add))
            if 2 * k < TF:
                for gc in range(16):
                    er = ctab[:, gc, k:k + 1]
                    ei = stab[:, gc, k:k + 1]
                    tmpv = co[:, 0, 0:1]
                    S(lambda e, ei=ei, tmpv=tmpv: e.tensor_scalar(out=tmpv, in0=ei, scalar1=ei, scalar2=None, op0=ALU.mult))
                    S(lambda e, er=er, tmpv=tmpv, gc=gc, k=k: e.scalar_tensor_tensor(out=ctab[:, gc, 2 * k:2 * k + 1], in0=er, scalar=er, in1=tmpv, op0=ALU.mult, op1=ALU.subtract))
                    S(lambda e, er=er, ei=ei, gc=gc, k=k: e.tensor_scalar(out=stab[:, gc, 2 * k:2 * k + 1], in0=er, scalar1=ei, scalar2=2.0, op0=ALU.mult, op1=ALU.mult))
            k *= 2
        S(lambda e: e.tensor_tensor(out=qs[:, 9, :], in0=ctab[:, :, TF - 1], in1=lq["cos"], op=ALU.mult))
        S(lambda e: e.tensor_tensor(out=qs[:, 11, :], in0=stab[:, :, TF - 1], in1=lq["sin"], op=ALU.mult))
        S(lambda e: e.tensor_tensor(out=qs[:, 9, :], in0=qs[:, 9, :], in1=qs[:, 11, :], op=ALU.subtract))
        S(lambda e: e.tensor_tensor(out=qs[:, 10, :], in0=ctab[:, :, TF - 1], in1=lq["sin"], op=ALU.mult))
        S(lambda e: e.tensor_tensor(out=qs[:, 11, :], in0=stab[:, :, TF - 1], in1=lq["cos"], op=ALU.mult))
        S(lambda e: e.tensor_tensor(out=qs[:, 10, :], in0=qs[:, 10, :], in1=qs[:, 11, :], op=ALU.add))
        S(lambda e: e.tensor_tensor(out=qs[:, 9, :], in0=qs[:, 9, :], in1=lq["r"], op=ALU.mult))
        S(lambda e: e.tensor_tensor(out=qs[:, 10, :], in0=qs[:, 10, :], in1=lq["r"], op=ALU.mult))
        rEr = qs[:, 9, :]
        rEi = qs[:, 10, :]
        S(lambda e: e.tensor_copy(out=ctabb[:], in_=ctab))
        S(lambda e: e.tensor_copy(out=stabb[:], in_=stab))
        S(lambda e: e.memset(rtab[:], 1.0))
        for gc in range(16):
            S(lambda e, gc=gc: e.tensor_scalar(out=rtab[:, gc, :], in0=rtab[:, gc, :], scalar1=qs[:, 3, gc:gc + 1], scalar2=None, op0=ALU.mult))
        S(lambda e: e.memset(rtab[:, :, 0:1], 0.0))

        def cv(i):
            return co[:, i, :] if i < 4 else (u32[:, i - 4, :] if i < 8 else s5v[:, 0, :])
        lc = lam_setup(xb0[:, 0, :], xb0[:, 1, :], xb0[:, 2, :], cv)
        lr_, li_ = xb0[:, 0, :], xb0[:, 1, :]
        br_, bi_ = xb0[:, 3, :], xb0[:, 4, :]
        ar_, ai_, den, cr_, ci_, tq = cv(0), cv(1), cv(2), cv(4), cv(5), s5v[:, 1, :]
        S(lambda e: e.tensor_tensor(out=ar_, in0=lc["r"], in1=lc["cos"], op=ALU.mult))
        S(lambda e: e.tensor_tensor(out=ai_, in0=lc["r"], in1=lc["sin"], op=ALU.mult))
        S(lambda e: e.tensor_scalar(out=ar_, in0=ar_, scalar1=-1.0, scalar2=None, op0=ALU.add))
        S(lambda e: e.tensor_tensor(out=den, in0=lr_, in1=lr_, op=ALU.mult))
        S(lambda e: e.tensor_tensor(out=tq, in0=li_, in1=li_, op=ALU.mult))
        S(lambda e: e.tensor_tensor(out=den, in0=den, in1=tq, op=ALU.add))
        S(lambda e: e.reciprocal(out=den, in_=den))
        S(lambda e: e.tensor_tensor(out=cr_, in0=ar_, in1=lr_, op=ALU.mult))
        S(lambda e: e.tensor_tensor(out=tq, in0=ai_, in1=li_, op=ALU.mult))
        S(lambda e: e.tensor_tensor(out=cr_, in0=cr_, in1=tq, op=ALU.add))
        S(lambda e: e.tensor_tensor(out=cr_, in0=cr_, in1=den, op=ALU.mult))
        S(lambda e: e.tensor_tensor(out=ci_, in0=ai_, in1=lr_, op=ALU.mult))
        S(lambda e: e.tensor_tensor(out=tq, in0=ar_, in1=li_, op=ALU.mult))
        S(lambda e: e.tensor_tensor(out=ci_, in0=ci_, in1=tq, op=ALU.subtract))
        S(lambda e: e.tensor_tensor(out=ci_, in0=ci_, in1=den, op=ALU.mult))
        S(lambda e: e.tensor_tensor(out=tq, in0=ci_, in1=bi_, op=ALU.mult))
        S(lambda e: e.tensor_tensor(out=den, in0=cr_, in1=br_, op=ALU.mult))
        S(lambda e: e.tensor_tensor(out=btab[:, 0, :], in0=den, in1=tq, op=ALU.subtract))
        S(lambda e: e.tensor_tensor(out=tq, in0=ci_, in1=br_, op=ALU.mult))
        S(lambda e: e.tensor_tensor(out=den, in0=cr_, in1=bi_, op=ALU.mult))
        S(lambda e: e.tensor_tensor(out=btab[:, 1, :], in0=den, in1=tq, op=ALU.add))
        for ri in range(2):
            for gc in range(16):
                uc_, j_ = divmod(gc, 4)
                S(lambda e, ri=ri, gc=gc, uc_=uc_, j_=j_: e.tensor_scalar(
                    out=btabF[:, ri, gc * 128:(gc + 1) * 128], in0=btab[:, ri, uc_ * 128:(uc_ + 1) * 128],
                    scalar1=vecs[:, 28 + j_:29 + j_], scalar2=None, op0=ALU.mult))
        S(lambda e: e.tensor_copy(out=ctb[:, 0, :], in_=xb0[:, 5, :]))
        S(lambda e: e.tensor_scalar(out=ctb[:, 1, :], in0=xb0[:, 6, :], scalar1=-1.0, scalar2=None, op0=ALU.mult))
        P.op("dve", lambda e: e.memset(ctmp[:], 0.0), reads=[t_setup],
             writes=[t_tab, t_diag, t_ctmp] + t_xs[0] + t_xs[1] + t_st + t_co + t_u32 + t_s5t + t_act[0:4])

        wstate = {"ns": 0, "nb": 0, "cs": 0, "cb": 0}
        total_small = ntiles * NSMALL
        total_big = ntiles * NBIG
        small_seq = []
        big_seq = []

        def issue_small(piece):
            n = wstate["ns"]
            slot = n % RS
            for hh in range(2):
                P.dma("pool", lambda e, hh=hh, slot=slot, piece=piece: e.dma_start(
                    out=ws[:, slot, hh * 1024:(hh + 1) * 1024], in_=wsm_d[piece][:, hh * 1024:(hh + 1) * 1024]),
                    writes=[t_ws2[slot][hh]])
            wstate["ns"] += 1

        def issue_big(piece):
            n = wstate["nb"]
            slot = n % RB
            HB = DFF // 2
            for hh in range(2):
                P.dma("pool", lambda e, hh=hh, slot=slot, piece=piece: e.dma_start(
                    out=wb[:, slot, hh * HB:(hh + 1) * HB], in_=wbg_d[piece][:, hh * HB:(hh + 1) * HB]),
                    writes=[t_wb2[slot][hh]])
            wstate["nb"] += 1

        def need_small(piece):
            n = wstate["cs"]
            assert small_plan[n] == piece, (n, small_plan[n], piece)
            while wstate["ns"] < min(n + RS, len(small_plan)):
                issue_small(small_plan[wstate["ns"]])
            wstate["cs"] += 1
            return n % RS

        def need_big(piece):
            n = wstate["cb"]
            assert big_plan[n] == piece, (n, big_plan[n], piece)
            while wstate["nb"] < min(n + RB, len(big_plan)):
                issue_big(big_plan[wstate["nb"]])
            wstate["cb"] += 1
            return n % RB

        def rms_stats(src_fn, src_toks, nchunks, dim, dst, t_dst):
            for c in range(nchunks):
                s = c % 4
                P.op("act", lambda e, c=c, s=s: e.activation(out=sq[:, s, :], in_=src_fn(c), func=AF.Square),
                     reads=[src_toks[c]], writes=[t_sq[s]])
                P.op("pe", lambda e, c=c, s=s: e.matmul(pS[:], lhsT=ones_bf[:], rhs=sq[:, s, :],
                                                        start=(c == 0), stop=(c == nchunks - 1)),
                     reads=[t_sq[s], t_const], writes=[t_pS])
            P.op("act", lambda e: e.activation(out=dst[:, 0, :], in_=pS[:], func=AF.Ln, scale=1.0 / dim, bias=EPS),
                 reads=[t_pS], writes=[t_dst[0]])
            P.op("act", lambda e: e.activation(out=dst[:, 1, :], in_=dst[:, 0, :], func=AF.Exp, scale=-0.5),
                 reads=[t_dst[0]], writes=[t_dst[1]])

        def norm_to_h(gi, xb, t_x, dst, t_dst, hdst=None, t_hdst=None):
            hdst = hb if hdst is None else hdst
            t_hdst = t_h if t_hdst is None else t_hdst
            rms_stats(lambda c: xb[:, c, :], t_x, DC, D, dst, t_dst)
            for c in range(DC):
                P.op("dve", lambda e, c=c: e.scalar_tensor_tensor(
                    out=hdst[:, c, :], in0=xb[:, c, :], scalar=g_ap(gi, c), in1=dst[:, 1, :],
                    op0=ALU.mult, op1=ALU.mult), reads=[t_x[c], t_dst[1]], writes=[t_hdst[c]])

        def ffn_A(fc, sbase):
            slot = need_small(sbase + fc)
            b = 2 * (fc % 2)

            def mm(e, slot=slot, b=b):
                for kc in range(DC):
                    e.matmul(pA[b][:], lhsT=ws[:, slot, kc * 128:(kc + 1) * 128], rhs=hb[:, kc, :],
                             start=(kc == 0), stop=(kc == DC - 1))
                for kc in range(DC):
                    ins = e.matmul(pA[b + 1][:], lhsT=ws[:, slot, 1024 + kc * 128:1024 + (kc + 1) * 128],
                                   rhs=hb[:, kc, :], start=(kc == 0), stop=(kc == DC - 1))
                return ins
            P.op("pe", mm, reads=t_ws2[slot] + t_h, writes=[t_pA[b], t_pA[b + 1]])
            s = fc % 2
            P.op("act", lambda e, s=s, b=b: e.activation(out=stmp[:, s, :], in_=pA[b][:], func=AF.Silu),
                 reads=[t_pA[b]], writes=[t_stmp[s]])
            P.op("act", lambda e, s=s, b=b: e.activation(out=stmp2[:, s, :], in_=pA[b + 1][:], func=AF.Copy),
                 reads=[t_pA[b + 1]], writes=[t_stmp2[s]])
            P.op("pool", lambda e, s=s, fc=fc: e.tensor_tensor(out=act[:, fc, :], in0=stmp2[:, s, :], in1=stmp[:, s, :], op=ALU.mult),
                 reads=[t_stmp2[s], t_stmp[s]], writes=[t_act[fc]])

        def ffn_B(dc, bbase, xb, t_x):
            slot = need_big(bbase + dc)
            b = 2 * (dc % 2)

            def mm2(e, slot=slot, b=b, dc=dc):
                for fc in range(FC):
                    ins = e.matmul(pA[b][:], lhsT=wb[:, slot, fc * 128:(fc + 1) * 128], rhs=act[:, fc, :],
                                   start=(fc == 0), stop=(fc == FC - 1))
                return ins
            P.op("pe", mm2, reads=t_wb2[slot] + t_act, writes=[t_pA[b]])
            r = dc % 2
            P.op("act", lambda e, r=r, b=b: e.activation(out=rtmp[:, r, :], in_=pA[b][:], func=AF.Copy, scale=0.5),
                 reads=[t_pA[b]], writes=[t_rtmp[r]])
            P.op("pool", lambda e, dc=dc, r=r: e.tensor_tensor(out=xb[:, dc, :], in0=xb[:, dc, :], in1=rtmp[:, r, :], op=ALU.add),
                 reads=[t_rtmp[r], t_x[dc]], writes=[t_x[dc]])

        def ffn_gen(gi, xb, t_x, sbase, bbase):
            norm_to_h(gi, xb, t_x, sta, t_sta)
            yield
            for fc in range(FC):
                ffn_A(fc, sbase)
                yield
            for dc in range(DC):
                ffn_B(dc, bbase, xb, t_x)
                yield

        def final_store(ti, xb, t_x):
            rms_stats(lambda c: xb[:, c, :], t_x, DC, D, sta, t_sta)
            for c in range(DC):
                P.op("dve", lambda e, c=c: e.scalar_tensor_tensor(
                    out=xb[:, c, :], in0=xb[:, c, :], scalar=g_ap(3, c), in1=sta[:, 1, :],
                    op0=ALU.mult, op1=ALU.mult), reads=[t_x[c], t_sta[1]], writes=[t_x[c]])
            cols = slice(ti * T, (ti + 1) * T)
            out_toks.append(P.dma("sp", lambda e, cols=cols, xb=xb: e.dma_start(out=oT_v[:, :, cols], in_=xb[:]), reads=t_x))
            if ti + 3 < ntiles:
                load_x(ti + 3)

        def ffn_pair_gen(t2, t1):
            xb2, tx2 = xbs[t2 % 3], t_xs[t2 % 3]
            xb1, tx1 = xbs[t1 % 3], t_xs[t1 % 3]
            norm_to_h(2, xb2, tx2, sta, t_sta)
            yield
            for fc in range(FC):
                ffn_A(fc, 33)
                yield
            norm_to_h(0, xb1, tx1, sta, t_sta)
            yield
            for dc in range(DC):
                ffn_B(dc, 8, xb2, tx2)
                yield
            for fc in range(2):
                ffn_A(fc, 0)
                yield
            final_store(t2, xb2, tx2)
            yield
            for fc in range(2, FC):
                ffn_A(fc, 0)
                yield
            for dc in range(DC):
                ffn_B(dc, 0, xb1, tx1)
                yield

        UBF = (0, 1, 2, 3)
        GLB = UBF
        MIX = tuple(range(4, 12))

        def conv_gen(ti):
            seq_end = (ti % TPS == TPS - 1)
            for c in range(4):
                def mmc(e, c=c):
                    for k in range(31):
                        for i in range(4):
                            ins = e.matmul(pS[32 * i:32 * i + 32, :], lhsT=diag[32 * i:32 * i + 32, c * 31 + k, :],
                                           rhs=abuf[32 * i:32 * i + 32, c, k:k + T],
                                           start=(k == 0), stop=(k == 30), tile_position=(32 * i, 32 * i))
                    return ins
                P.op("pe", mmc, reads=[t_diag, t_abuf[c], t_halo[c]], writes=[t_pS])
                P.op("act", lambda e, c=c: e.activation(out=co[:, c, :], in_=pS[:], func=AF.Identity, bias=vec_ap(0, c)),
                     reads=[t_pS, t_tab], writes=[t_co[c]])
                if not seq_end:
                    P.op("pool", lambda e, c=c: e.tensor_copy(out=abuf[:, c, 0:30], in_=abuf[:, c, T:T + 30]),
                         reads=[t_abuf[c]], writes=[t_halo[c]])
                P.op("act", lambda e, c=c: e.activation(out=sq[:, c, :], in_=co[:, c, :], func=AF.Square),
                     reads=[t_co[c]], writes=[t_sq[c]])
                P.op("act", lambda e, c=c: e.activation(out=ms[:, MIX[c], :], in_=co[:, c, :], func=AF.Copy),
                     reads=[t_co[c]], writes=[t_ms[MIX[c]]])
                yield
            for c in range(4):
                P.op("pe", lambda e, c=c: e.matmul(pS[:], lhsT=ones_bf[:], rhs=ms[:, MIX[c], :], start=(c == 0), stop=(c == 3)),
                     reads=[t_ms[MIX[c]], t_const], writes=[t_pS])
            P.op("act", lambda e: e.activation(out=stt_[:, 0, :], in_=pS[:], func=AF.Copy, scale=1.0 / 512),
                 reads=[t_pS], writes=[t_st[0]])
            P.op("dve", lambda e: e.tensor_tensor(out=stt_[:, 2, :], in0=stt_[:, 0, :], in1=stt_[:, 0, :], op=ALU.mult),
                 reads=[t_st[0]], writes=[t_st[2]])
            yield
            for c in range(4):
                P.op("pe", lambda e, c=c: e.matmul(pS[:], lhsT=ones_bf[:], rhs=sq[:, c, :], start=(c == 0), stop=(c == 3)),
                     reads=[t_sq[c], t_const], writes=[t_pS])
            P.op("dve", lambda e: e.scalar_tensor_tensor(out=stt_[:, 2, :], in0=pS[:], scalar=1.0 / 512, in1=stt_[:, 2, :],
                                                         op0=ALU.mult, op1=ALU.subtract),
                 reads=[t_pS, t_st[2]], writes=[t_st[2]])
            P.op("act", lambda e: e.activation(out=stt_[:, 2, :], in_=stt_[:, 2, :], func=AF.Ln, bias=EPS),
                 reads=[t_st[2]], writes=[t_st[2]])
            P.op("act", lambda e: e.activation(out=stt_[:, 1, :], in_=stt_[:, 2, :], func=AF.Exp, scale=-0.5),
                 reads=[t_st[2]], writes=[t_st[1]])
            P.op("dve", lambda e: e.tensor_tensor(out=stt_[:, 2, :], in0=stt_[:, 0, :], in1=stt_[:, 1, :], op=ALU.mult),
                 reads=[t_st[0], t_st[1], t_st[2]], writes=[t_st[2]])
            yield
            for c in range(4):
                P.op("dve", lambda e, c=c: e.tensor_tensor(out=co[:, c, :], in0=co[:, c, :], in1=stt_[:, 1, :], op=ALU.mult),
                     reads=[t_co[c], t_st[1]], writes=[t_co[c]])
                P.op("dve", lambda e, c=c: e.tensor_tensor(out=co[:, c, :], in0=co[:, c, :], in1=stt_[:, 2, :], op=ALU.subtract),
                     reads=[t_co[c], t_st[2]], writes=[t_co[c]])
                P.op("act", lambda e, c=c: e.activation(out=co[:, c, :], in_=co[:, c, :], func=AF.Silu,
                                                        scale=vec_ap(1, c), bias=vec_ap(2, c)),
                     reads=[t_co[c], t_tab], writes=[t_co[c]])
                yield
            rms_stats(lambda c: co[:, c, :], t_co, 4, 512, stt_, t_st)
            yield
            for c in range(4):
                P.op("dve", lambda e, c=c: e.scalar_tensor_tensor(
                    out=ms[:, MIX[c], :], in0=co[:, c, :], scalar=vec_ap(3, c), in1=stt_[:, 1, :],
                    op0=ALU.mult, op1=ALU.mult), reads=[t_co[c], t_st[1], t_tab], writes=[t_ms[MIX[c]]])

        def mixer_gen(ti, xb, t_x):
            seq_start = (ti % TPS == 0)
            seq_end = (ti % TPS == TPS - 1)
            norm_to_h(1, xb, t_x, stt_, t_st, hbm, t_hm)
            if seq_start:
                for c in range(4):
                    P.op("dve", lambda e, c=c: e.memset(abuf[:, c, 0:30], 0.0), writes=[t_halo[c]])
            yield
            for m, pair in enumerate(WIN_PAIRS):
                slot = need_small(22 + m)
                for o2, oc in enumerate(pair):
                    b = o2

                    def mm(e, slot=slot, o2=o2, b=b):
                        for kc in range(DC):
                            ins = e.matmul(pB[b][:], lhsT=ws[:, slot, (o2 * 8 + kc) * 128:(o2 * 8 + kc + 1) * 128],
                                           rhs=hbm[:, kc, :], start=(kc == 0), stop=(kc == DC - 1))
                        return ins
                    P.op("pe", mm, reads=t_ws2[slot] + t_hm, writes=[t_pB[b]])
                    if 4 <= oc < 8:
                        c = oc - 4
                        P.op("act", lambda e, b=b, c=c: e.activation(out=s5x[:, c % 2, :], in_=pB[b][:], func=AF.Sigmoid),
                             reads=[t_pB[b]], writes=[t_s5x[c % 2]])
                    elif oc < 4:
                        c = oc
                        P.op("dve", lambda e, b=b, c=c: e.tensor_tensor(out=abuf[:, c, 30:30 + T], in0=pB[b][:], in1=s5x[:, c % 2, :], op=ALU.mult),
                             reads=[t_pB[b], t_s5x[c % 2]], writes=[t_abuf[c]])
                    else:
                        c = oc - 8
                        P.op("act", lambda e, b=b, c=c: e.activation(out=u32[:, c, :], in_=pB[b][:], func=AF.Copy),
                             reads=[t_pB[b]], writes=[t_u32[c]])
                        P.op("act", lambda e, c=c: e.activation(out=ms[:, UBF[c], :], in_=u32[:, c, :], func=AF.Copy),
                             reads=[t_u32[c]], writes=[t_ms[UBF[c]]])
            yield

            cg = conv_gen(ti)
            steps = [(uc, f) for uc in range(4) for f in range(NF)]
            NS = len(steps)
            D_ = lambda fn, r, w: P.op("dve", fn, reads=r, writes=w)
            G_ = lambda fn, r, w: P.op("dve", fn, reads=r, writes=w)
            tA, tB, vr, vi = s5tb[:, 0, :], s5tb[:, 1, :], s5v[:, 0, :], s5v[:, 1, :]
            wr, wi = s5w[:, 0, :], s5w[:, 1, :]
            bre, bim = pB[0], pB[1]

            def tabs(uc):
                return (ctabb[:, 4 * uc:4 * uc + 4, :].rearrange("p a b -> p (a b)"),
                        stabb[:, 4 * uc:4 * uc + 4, :].rearrange("p a b -> p (a b)"),
                        rtab[:, 4 * uc:4 * uc + 4, :].rearrange("p a b -> p (a b)"))

            def mmb_op(si):
                uc, f = steps[si]
                fs = slice(f * TF, (f + 1) * TF)

                def mmb(e, uc=uc, fs=fs):
                    for ri in range(2):
                        for j in range(4):
                            gc = 4 * uc + j
                            ins = e.matmul(pB[ri][:, j * TF:(j + 1) * TF],
                                           lhsT=btabF[:, ri, gc * 128:(gc + 1) * 128],
                                           rhs=ms[:, UBF[uc], fs], start=True, stop=True)
                    return ins
                P.op("pe", mmb, reads=[t_tab, t_ms[UBF[uc]]], writes=[t_pB[0], t_pB[1]])

            def fwd(si):
                uc, f = steps[si]
                cT, sT, rT = tabs(uc)
                D_(lambda e: e.tensor_tensor(out=tA, in0=bre[:], in1=cT, op=ALU.mult), [t_pB[0], t_tab], [t_s5t[0]])
                D_(lambda e: e.tensor_tensor(out=tB, in0=bim[:], in1=sT, op=ALU.mult), [t_pB[1], t_tab], [t_s5t[1]])
                D_(lambda e: e.tensor_tensor(out=vr, in0=tA, in1=tB, op=ALU.add), [t_s5t[0], t_s5t[1]], [t_s5t[2]])
                D_(lambda e: e.tensor_tensor(out=tA, in0=bim[:], in1=cT, op=ALU.mult), [t_pB[1], t_tab], [t_s5t[0]])
                D_(lambda e: e.tensor_tensor(out=tB, in0=bre[:], in1=sT, op=ALU.mult), [t_pB[0], t_tab], [t_s5t[1]])
                D_(lambda e: e.tensor_tensor(out=vi, in0=tA, in1=tB, op=ALU.subtract), [t_s5t[0], t_s5t[1]], [t_s5t[3]])

            def scans(si):
                uc, f = steps[si]
                cT, sT, rT = tabs(uc)
                first = seq_start and f == 0
                if not first:
                    vr0 = s5v[:, 0, :].rearrange("p (a b) -> p a b", a=4)[:, :, 0]
                    vi0 = s5v[:, 1, :].rearrange("p (a b) -> p a b", a=4)[:, :, 0]
                    D_(lambda e: e.tensor_tensor(out=vr0, in0=vr0, in1=carry[:, uc, 0:4], op=ALU.add), [t_s5t[2], t_carry[uc]], [t_s5t[2]])
                    D_(lambda e: e.tensor_tensor(out=vi0, in0=vi0, in1=carry[:, uc, 4:8], op=ALU.add), [t_s5t[3], t_carry[uc]], [t_s5t[3]])
                D_(lambda e: e.tensor_tensor_scan(out=wr, data0=rT, data1=vr, initial=0.0, op0=ALU.mult, op1=ALU.add), [t_s5t[2], t_tab], [t_s5w[0]])
                D_(lambda e: e.tensor_tensor_scan(out=wi, data0=rT, data1=vi, initial=0.0, op0=ALU.mult, op1=ALU.add), [t_s5t[3], t_tab], [t_s5w[1]])
                P.op("act", lambda e: e.activation(out=s5wb[:, 0, :], in_=wr, func=AF.Copy), reads=[t_s5w[0]], writes=[t_s5wb[0]])
                P.op("act", lambda e: e.activation(out=s5wb[:, 1, :], in_=wi, func=AF.Copy), reads=[t_s5w[1]], writes=[t_s5wb[1]])
                last = seq_end and f == NF - 1
                if not last:
                    wr1 = s5w[:, 0, :].rearrange("p (a b) -> p a b", a=4)[:, :, TF - 1]
                    wi1 = s5w[:, 1, :].rearrange("p (a b) -> p a b", a=4)[:, :, TF - 1]
                    er4 = rEr[:, 4 * uc:4 * uc + 4]
                    ei4 = rEi[:, 4 * uc:4 * uc + 4]
                    D_(lambda e: e.tensor_tensor(out=ctmp[:, 0, :], in0=wr1, in1=er4, op=ALU.mult), [t_s5w[0], t_tab, t_ctmp], [t_ctmp])
                    D_(lambda e: e.tensor_tensor(out=ctmp[:, 1, :], in0=wi1, in1=ei4, op=ALU.mult), [t_s5w[1], t_tab, t_ctmp], [t_ctmp])
                    D_(lambda e: e.tensor_tensor(out=ctmp[:, 2, :], in0=wi1, in1=er4, op=ALU.mult), [t_s5w[1], t_tab, t_ctmp], [t_ctmp])
                    D_(lambda e: e.tensor_tensor(out=ctmp[:, 3, :], in0=wr1, in1=ei4, op=ALU.mult), [t_s5w[0], t_tab, t_ctmp], [t_ctmp])
                    D_(lambda e: e.tensor_tensor(out=carry[:, uc, 0:4], in0=ctmp[:, 0, :], in1=ctmp[:, 1, :], op=ALU.subtract), [t_ctmp], [t_carry[uc]])
                    D_(lambda e: e.tensor_tensor(out=carry[:, uc, 4:8], in0=ctmp[:, 2, :], in1=ctmp[:, 3, :], op=ALU.add), [t_ctmp], [t_carry[uc]])

            def combos(si):
                uc, f = steps[si]
                cT, sT, rT = tabs(uc)
                xs = 2 * (si % 2)
                xr, xi = s5x[:, xs, :], s5x[:, xs + 1, :]
                wrb, wib = s5wb[:, 0, :], s5wb[:, 1, :]
                D_(lambda e: e.tensor_tensor(out=tA, in0=wrb, in1=cT, op=ALU.mult), [t_s5wb[0], t_tab], [t_s5t[0]])
                D_(lambda e: e.tensor_tensor(out=tB, in0=wib, in1=sT, op=ALU.mult), [t_s5wb[1], t_tab], [t_s5t[1]])
                D_(lambda e: e.tensor_tensor(out=xr, in0=tA, in1=tB, op=ALU.subtract), [t_s5t[0], t_s5t[1]], [t_s5x[xs]])
                D_(lambda e: e.tensor_tensor(out=tA, in0=wib, in1=cT, op=ALU.mult), [t_s5wb[1], t_tab], [t_s5t[0]])
                D_(lambda e: e.tensor_tensor(out=tB, in0=wrb, in1=sT, op=ALU.mult), [t_s5wb[0], t_tab], [t_s5t[1]])
                D_(lambda e: e.tensor_tensor(out=xi, in0=tA, in1=tB, op=ALU.add), [t_s5t[0], t_s5t[1]], [t_s5x[xs + 1]])

            def mmy_op(si):
                uc, f = steps[si]
                xs = 2 * (si % 2)
                fs = slice(f * TF, (f + 1) * TF)

                def mmy(e):
                    for j in range(4):
                        for ri in range(2):
                            gc = 4 * uc + j
                            ins = e.matmul(pX[32 * j:32 * j + 32, fs],
                                           lhsT=ctb[:, ri, gc * 32:(gc + 1) * 32],
                                           rhs=s5x[:, xs + ri, j * TF:(j + 1) * TF],
                                           start=(ri == 0), stop=(ri == 1), tile_position=(0, 32 * j))
                    return ins
                P.op("pe", mmy, reads=[t_tab, t_s5x[xs], t_s5x[xs + 1]], writes=[t_pX])
                if f == NF - 1:
                    yv = u32[:, uc, :]
                    P.op("dve", lambda e: e.scalar_tensor_tensor(
                        out=yv, in0=yv, scalar=vec_ap(4, uc), in1=pX[:], op0=ALU.mult, op1=ALU.add),
                        reads=[t_u32[uc], t_pX, t_tab], writes=[t_u32[uc]])
                    s = uc % 2
                    tq_ = sta[:, s, :]
                    P.op("act", lambda e: e.activation(out=tq_, in_=yv, func=AF.Square),
                         reads=[t_u32[uc]], writes=[t_sta[s]])
                    P.op("dve", lambda e: e.tensor_scalar(out=tq_, in0=tq_, scalar1=0.044715, scalar2=1.0, op0=ALU.mult, op1=ALU.add),
                         reads=[t_sta[s]], writes=[t_sta[s]])
                    P.op("dve", lambda e: e.tensor_tensor(out=tq_, in0=tq_, in1=yv, op=ALU.mult),
                         reads=[t_sta[s], t_u32[uc]], writes=[t_sta[s]])
                    P.op("act", lambda e: e.activation(out=tq_, in_=tq_, func=AF.Sigmoid, scale=1.5957691216057308),
                         reads=[t_sta[s]], writes=[t_sta[s]])
                    P.op("dve", lambda e: e.tensor_tensor(out=yv, in0=yv, in1=tq_, op=ALU.mult),
                         reads=[t_sta[s], t_u32[uc]], writes=[t_u32[uc]])
                    P.op("act", lambda e: e.activation(out=ms[:, GLB[uc], :], in_=yv, func=AF.Copy),
                         reads=[t_u32[uc]], writes=[t_ms[GLB[uc]]])

            mmb_op(0)
            fwd(0)
            scans(0)
            for si in range(NS):
                combos(si)
                if si + 1 < NS:
                    mmb_op(si + 1)
                yield
                if si + 1 < NS:
                    fwd(si + 1)
                    scans(si + 1)
                try:
                    next(cg)
                except StopIteration:
                    pass
                yield
                mmy_op(si)
            for _ in cg:
                pass
            slot = need_small(28)
            for oc in range(4):
                b = oc % 2

                def mmg(e, slot=slot, oc=oc, b=b):
                    for kc in range(4):
                        ins = e.matmul(pB[b][:], lhsT=ws[:, slot, (oc * 4 + kc) * 128:(oc * 4 + kc + 1) * 128],
                                       rhs=ms[:, GLB[kc], :], start=(kc == 0), stop=(kc == 3))
                    return ins
                P.op("pe", mmg, reads=t_ws2[slot] + [t_ms[g] for g in GLB], writes=[t_pB[b]])
                s = oc % 2
                tq_ = sta[:, s, :]
                P.op("act", lambda e, b=b, oc=oc, tq_=tq_: e.activation(out=tq_, in_=pB[b][:], func=AF.Sigmoid, bias=vec_ap(5, oc)),
                     reads=[t_pB[b], t_tab], writes=[t_sta[s]])
                P.op("dve", lambda e, oc=oc, tq_=tq_: e.tensor_tensor(out=u32[:, oc, :], in0=u32[:, oc, :], in1=tq_, op=ALU.mult),
                     reads=[t_sta[s], t_u32[oc]], writes=[t_u32[oc]])
            yield
            rms_stats(lambda c: u32[:, c, :], t_u32, 4, 512, stt_, t_st)
            for c in range(4):
                P.op("dve", lambda e, c=c: e.scalar_tensor_tensor(
                    out=ms[:, MIX[4 + c], :], in0=u32[:, c, :], scalar=vec_ap(6, c), in1=stt_[:, 1, :],
                    op0=ALU.mult, op1=ALU.mult), reads=[t_u32[c], t_st[1], t_tab], writes=[t_ms[MIX[4 + c]]])
            yield
            for m in range(4):
                slot = need_small(29 + m)
                for d2 in range(2):
                    dc = 2 * m + d2
                    b = dc % 2

                    def mmo(e, slot=slot, d2=d2, b=b, dc=dc):
                        for kc in range(8):
                            ins = e.matmul(pB[b][:], lhsT=ws[:, slot, (d2 * 8 + kc) * 128:(d2 * 8 + kc + 1) * 128],
                                           rhs=ms[:, MIX[kc], :], start=(kc == 0), stop=(kc == 7))
                        return ins
                    P.op("pe", mmo, reads=t_ws2[slot] + [t_ms[g] for g in MIX], writes=[t_pB[b]])
                    P.op("dve", lambda e, dc=dc, b=b: e.tensor_tensor(out=xb[:, dc, :], in0=pB[b][:], in1=xb[:, dc, :], op=ALU.add),
                         reads=[t_pB[b], t_x[dc]], writes=[t_x[dc]])
                yield

        small_plan, big_plan = [], []
        MIXP = list(range(22, 28)) + [28] + list(range(29, 33))
        small_plan += list(range(0, 22))
        big_plan += list(range(0, 8))
        for ti in range(ntiles):
            pass

        out_toks = []
        xT_v = xT.rearrange("(c p) t -> p c t", p=128)
        oT_v = oT.rearrange("(c p) t -> p c t", p=128)

        def load_x(ti):
            cols = slice(ti * T, (ti + 1) * T)
            b = ti % 3
            P.dma("sp", lambda e, cols=cols, b=b: e.dma_start(out=xbs[b][:], in_=xT_v[:, :, cols]), writes=t_xs[b])

        def finalize_gen(ti):
            xb, t_x = xbs[ti % 3], t_xs[ti % 3]
            for _ in ffn_gen(2, xb, t_x, 33, 8):
                yield
            final_store(ti, xb, t_x)
            yield

        def chain(*gens):
            for g in gens:
                if g is not None:
                    for _ in g:
                        yield

        def interleave(gm, ga, per):
            k = 0
            m_done = a_done = False
            while not (m_done and a_done):
                if not m_done:
                    try:
                        next(gm)
                    except StopIteration:
                        m_done = True
                n = per(k) if not m_done else 10 ** 9
                k += 1
                for _ in range(n):
                    if a_done:
                        break
                    try:
                        next(ga)
                    except StopIteration:
                        a_done = True

        def per_full(k):
            if k < 2:
                return 3
            if k < 2 + 32:
                return 1 if k % 2 == 0 else 2
            return 2

        def per_half(k):
            if k < 2:
                return 1
            if k < 2 + 32:
                return 1 if k % 2 == 1 else 0
            return 2

        def run_schedule(dry):
            load_x(0)
            for _ in ffn_gen(0, xbs[0], t_xs[0], 0, 0):
                pass
            if ntiles > 1:
                load_x(1)
            if ntiles > 2:
                load_x(2)
            for ti in range(ntiles):
                b = ti % 3
                has2 = ti >= 1
                has1 = ti + 1 < ntiles
                if has2 and has1:
                    ga = ffn_pair_gen(ti - 1, ti + 1)
                elif has2:
                    ga = finalize_gen(ti - 1)
                elif has1:
                    ga = ffn_gen(0, xbs[(ti + 1) % 3], t_xs[(ti + 1) % 3], 0, 0)
                else:
                    ga = None
                interleave(mixer_gen(ti, xbs[b], t_xs[b]), ga if ga is not None else iter(()), per_full if (has2 and has1) else per_half)
            for _ in finalize_gen(ntiles - 1):
                pass

        real_need_small, real_need_big = need_small, need_big
        real_P = P
        plan_s, plan_b = [], []

        class _Null:
            def op(self, *a, **k):
                return None

            def dma(self, *a, **k):
                return ("x", 0)
        P = _Null()

        def need_small(piece):
            plan_s.append(piece)
            return 0

        def need_big(piece):
            plan_b.append(piece)
            return 0
        run_schedule(True)
        small_plan, big_plan = plan_s, plan_b
        out_toks.clear()
        P = real_P
        need_small, need_big = real_need_small, real_need_big
        run_schedule(False)
        P.wait_all("sp", out_toks)
        P.emit(block)
    return nc


def _host_layout(inp):
    f = lambda a: np.ascontiguousarray(a, dtype=np.float32)
    L = 0
    small = np.empty((NSMALL, 128, 2048), np.float32)

    def w13(w1, w3):
        a = w1.reshape(8, 128, FC, 128).transpose(2, 1, 0, 3)
        b = w3.reshape(8, 128, FC, 128).transpose(2, 1, 0, 3)
        return np.stack([a, b], axis=2).reshape(FC, 128, 2048)
    small[0:22] = w13(inp["ffn1_w1"][L], inp["ffn1_w3"][L])
    win = inp["w_in"][L].reshape(8, 128, 12, 128)
    for m, pair in enumerate(WIN_PAIRS):
        small[22 + m] = np.stack([win[:, :, oc, :].transpose(1, 0, 2) for oc in pair], axis=1).reshape(128, 2048)
    glu = inp["ssm_glu_w"][L].reshape(4, 128, 4, 128)
    small[28] = glu.transpose(1, 2, 0, 3).reshape(128, 2048)
    wo = inp["w_out"][L].reshape(8, 128, 4, 2, 128)
    small[29:33] = wo.transpose(2, 1, 3, 0, 4).reshape(4, 128, 2048)
    small[33:55] = w13(inp["ffn2_w1"][L], inp["ffn2_w3"][L])
    big = np.empty((NBIG, 128, DFF), np.float32)
    big[0:8] = inp["ffn1_w2"][L].reshape(FC, 128, 8, 128).transpose(2, 1, 0, 3).reshape(8, 128, DFF)
    big[8:16] = inp["ffn2_w2"][L].reshape(FC, 128, 8, 128).transpose(2, 1, 0, 3).reshape(8, 128, DFF)

    def pc(v, n):
        return v.reshape(n, 128).T
    gains = np.concatenate([pc(inp["norm_ffn1"][L], 8), pc(inp["norm_mix"][L], 8),
                            pc(inp["norm_ffn2"][L], 8), pc(inp["norm_final"], 8)], axis=1)
    vecs = np.concatenate([pc(inp["conv_b"][L], 4), pc(inp["conv_ln_g"][L], 4), pc(inp["conv_ln_b"][L], 4),
                           pc(inp["conv_out_g"][L], 4), pc(inp["ssm_D"][L], 4), pc(inp["ssm_glu_b"][L], 4),
                           pc(inp["ssm_out_g"][L], 4),
                           (np.arange(128)[:, None] // 32 == np.arange(4)[None, :]).astype(np.float32)], axis=1)
    cw = inp["conv_w"][L].reshape(31, 4, 128).transpose(2, 1, 0).reshape(128, 124)
    ident = (np.arange(128)[:, None] % 32 == np.arange(32)[None, :]).astype(np.float32)
    A_re, A_im, ldt = inp["ssm_A_re"][L], inp["ssm_A_im"][L], inp["ssm_log_dt"][L]

    def ql(a):
        return a.reshape(16, 2, 64).transpose(1, 2, 0).reshape(128, 16)
    aq = np.concatenate([ql(A_re), ql(A_im), ql(np.broadcast_to(ldt[:, None], (32, 64)))], axis=1)
    B_re, B_im = inp["ssm_B_re"][L], inp["ssm_B_im"][L]
    sc = np.zeros((5, 128, 4, 128), np.float32)
    for g in range(32):
        uc, g8 = divmod(g, 8)
        rows = slice(16 * g8, 16 * g8 + 16)
        sc[0, rows, uc, :] = np.tile(A_re[g], 2)[None, :]
        sc[1, rows, uc, :] = np.tile(A_im[g], 2)[None, :]
        sc[2, rows, uc, :] = ldt[g]
        qo = 64 * (g % 2)
        sc[3, rows, uc, qo:qo + 64] = B_re[g].T
        sc[4, rows, uc, qo:qo + 64] = B_im[g].T
    sc = sc.transpose(1, 0, 2, 3).reshape(128, 5 * 512)
    C_re, C_im = inp["ssm_C_re"][L], inp["ssm_C_im"][L]
    cq = np.zeros((2, 128, 16, 32), np.float32)
    for g in range(32):
        gc, g2 = divmod(g, 2)
        cq[0, 64 * g2:64 * g2 + 64, gc, 16 * g2:16 * g2 + 16] = C_re[g].T
        cq[1, 64 * g2:64 * g2 + 64, gc, 16 * g2:16 * g2 + 16] = C_im[g].T
    cq = cq.transpose(1, 0, 2, 3).reshape(128, 2 * 512)
    idf = np.concatenate([np.eye(128, dtype=np.float32), 2.0 * np.eye(128, dtype=np.float32)], axis=1)
    return dict(wsmall=small, wbig=big, gains=f(gains), vecs=f(vecs), cw=f(cw), ident=ident, idf=idf,
                aq=f(aq), sc=f(sc), cq=f(cq))


_NC_CACHE = {}


def kernel(**inputs):
    inp = {k: np.asarray(v) for k, v in inputs.items()}
    x = inp["x"]
    shared = _host_layout(inp)
    in_maps = []
    for c in range(NCORES):
        xc = x[NSEQ * c:NSEQ * (c + 1)].reshape(NSEQ * SEQ, D)
        m = dict(shared)
        m["xT"] = np.ascontiguousarray(xc.T)
        in_maps.append(m)
    if "nc" not in _NC_CACHE:
        _NC_CACHE["nc"] = build_nc()
    res = run_bass_kernel_spmd(_NC_CACHE["nc"], in_maps, core_ids=list(range(NCORES)))
    out = np.empty((16, SEQ, D), np.float32)
    for c in range(NCORES):
        o = res.results[c]["oT"]
        out[NSEQ * c:NSEQ * (c + 1)] = np.ascontiguousarray(o.T).reshape(NSEQ, SEQ, D)
    return out
```

```python
import contextlib
import math
import numpy as np
import concourse.bass as bass
import concourse.mybir as mybir
from concourse.bass_utils import run_bass_kernel_spmd

F32 = mybir.dt.float32
BF16 = mybir.dt.bfloat16
I32 = mybir.dt.int32
AF = mybir.ActivationFunctionType
ALU = mybir.AluOpType

NCORES = 8
D = 1024
DC = 8
DFF = 2816
FC = 22
T = 512
SEQ = 4096
TPS = SEQ // T
NSEQ = 2
NT = TPS * NSEQ
TF = 128
NF = T // TF
EPS = 1e-6
RS = 3
RB = 2
NSMALL = 55
NBIG = 16
WIN_PAIRS = [(4, 0), (5, 1), (6, 2), (7, 3), (8, 9), (10, 11)]


class Tok:
    __slots__ = ("w", "r", "name")

    def __init__(self, name=""):
        self.w = None
        self.r = {}
        self.name = name


class Prog:
    ENG = ("pe", "act", "dve", "pool", "sp")
    NDSEM = 10

    def __init__(self, nc, stack):
        self.nc = nc
        self.q = {e: [] for e in self.ENG}
        self.cnt = {e: 0 for e in self.ENG}
        self.seen = {e: {} for e in self.ENG}
        self.sems = {}
        for e in self.ENG:
            self.sems[e] = stack.enter_context(nc.semaphore("s_" + e))
        for qn in ("sp", "pool"):
            for i in range(self.NDSEM):
                self.sems[("d" + qn, i)] = stack.enter_context(nc.semaphore("d%s_%d" % (qn, i)))
        self.dma_i = {"sp": 0, "pool": 0}

    def _deps(self, eng, reads, writes):
        deps = {}

        def add(t):
            if t is None:
                return
            k, v = t
            if deps.get(k, 0) < v:
                deps[k] = v

        for b in reads:
            add(b.w)
        for b in writes:
            if b.w is not None and not (b.w[0] == eng == "pe"):
                add(b.w)
            for k, v in b.r.items():
                if not (k == eng == "pe"):
                    add((k, v))
        return deps

    def _commit(self, eng, deps, tok, reads, writes):
        waits = []
        seen = self.seen[eng]
        for k, v in deps.items():
            if seen.get(k, 0) < v:
                seen[k] = v
                waits.append((k, v))
        k, v = tok
        for b in reads:
            if b.r.get(k, 0) < v:
                b.r[k] = v
        for b in writes:
            b.w = tok
            b.r = {}
        return waits

    def op(self, eng, fn, reads=(), writes=()):
        deps = self._deps(eng, reads, writes)
        self.cnt[eng] += 1
        tok = (eng, self.cnt[eng])
        waits = self._commit(eng, deps, tok, reads, writes)
        self.q[eng].append((waits, fn, (eng, 1)))
        return tok

    def dma(self, queue, fn, reads=(), writes=()):
        deps = self._deps(queue, reads, writes)
        i = self.dma_i[queue]
        self.dma_i[queue] += 1
        key = ("d" + queue, i % self.NDSEM)
        val = 16 * (i // self.NDSEM + 1)
        if val > 16 and deps.get(key, 0) < val - 16:
            deps[key] = val - 16
        tok = (key, val)
        waits = self._commit(queue, deps, tok, reads, writes)
        self.q[queue].append((waits, fn, (key, 16)))
        return tok

    def wait_all(self, eng, toks):
        deps = {}
        for k, v in toks:
            if deps.get(k, 0) < v:
                deps[k] = v
        waits = []
        for k, v in deps.items():
            if self.seen[eng].get(k, 0) < v:
                self.seen[eng][k] = v
                waits.append((k, v))
        self.q[eng].append((waits, None, None))

    def emit(self, block):
        sems = self.sems

        def replay(name):
            def body(e):
                for waits, fn, inc in self.q[name]:
                    for k, v in waits:
                        e.wait_ge(sems[k], v)
                    if fn is None:
                        continue
                    ins = fn(e)
                    ins.then_inc(sems[inc[0]], inc[1])
            return body

        block.tensor(replay("pe"))
        block.scalar(replay("act"))
        block.vector(replay("dve"))
        block.gpsimd(replay("pool"))
        block.sync(replay("sp"))


def toks(n, name):
    return [Tok("%s%d" % (name, i)) for i in range(n)]


def build_nc(ntiles=NT, stages=None):
    nc = bass.Bass("TRN2", target_bir_lowering=False)
    NTOK = NT * T
    xT = nc.dram_tensor("xT", [D, NTOK], F32, kind="ExternalInput").ap()
    oT = nc.dram_tensor("oT", [D, NTOK], F32, kind="ExternalOutput").ap()
    wsm_d = nc.dram_tensor("wsmall", [NSMALL, 128, 2048], F32, kind="ExternalInput").ap()
    wbg_d = nc.dram_tensor("wbig", [NBIG, 128, DFF], F32, kind="ExternalInput").ap()
    gains_d = nc.dram_tensor("gains", [128, 4 * DC], F32, kind="ExternalInput").ap()
    vecs_d = nc.dram_tensor("vecs", [128, 8 * 4], F32, kind="ExternalInput").ap()
    cw_d = nc.dram_tensor("cw", [128, 4 * 31], F32, kind="ExternalInput").ap()
    ident_d = nc.dram_tensor("ident", [128, 32], F32, kind="ExternalInput").ap()
    aq_d = nc.dram_tensor("aq", [128, 3 * 16], F32, kind="ExternalInput").ap()
    sc_d = nc.dram_tensor("sc", [128, 5 * 512], F32, kind="ExternalInput").ap()
    cq_d = nc.dram_tensor("cq", [128, 2 * 512], F32, kind="ExternalInput").ap()
    idf_d = nc.dram_tensor("idf", [128, 256], F32, kind="ExternalInput").ap()
    wsm_bf = nc.dram_tensor("wsm_bf", [NSMALL, 128, 2048], BF16, kind="Internal").ap()
    wbg_bf = nc.dram_tensor("wbg_bf", [NBIG, 128, DFF], BF16, kind="Internal").ap()

    with contextlib.ExitStack() as st:
        def sb(name, shape, dt):
            return st.enter_context(nc.sbuf_tensor(name, shape, dt))

        def ps(name):
            return st.enter_context(nc.psum_tensor(name, [128, 512], F32))

        xbs = [sb("xb%d" % i, [128, DC, T], F32) for i in range(3)]
        hb = sb("hb", [128, DC, T], BF16)
        sq = sb("sq", [128, 4, T], BF16)
        stt_ = sb("stt", [128, 3, T], F32)
        sta = sb("sta", [128, 2, T], F32)
        act = sb("act", [128, FC, T], BF16)
        ms = sb("ms", [128, 12, T], BF16)
        stmp2 = sb("stmp2", [128, 2, T], BF16)
        idf = sb("idf_s", [128, 2, 128], F32)
        stmp = sb("stmp", [128, 2, T], BF16)
        ws = sb("ws", [128, RS, 2048], BF16)
        wb = sb("wb", [128, RB, DFF], BF16)
        diag = sb("diag", [128, 4 * 31, 32], BF16)
        abuf = sb("abuf", [128, 4, 30 + T], BF16)
        co = sb("co", [128, 4, T], F32)
        u32 = sb("u32", [128, 4, T], F32)
        s5tb = sb("s5tb", [128, 2, T], BF16)
        s5v = sb("s5v", [128, 2, T], F32)
        s5w = sb("s5w", [128, 2, T], F32)
        s5x = sb("s5x", [128, 4, T], BF16)
        ctabb = sb("ctabb", [128, 16, TF], BF16)
        stabb = sb("stabb", [128, 16, TF], BF16)
        s5wb = sb("s5wb", [128, 2, T], BF16)
        rtab = sb("rtab", [128, 16, TF], F32)
        btabF = sb("btabF", [128, 2, 16 * 128], BF16)
        ctb = sb("ctb", [128, 2, 512], BF16)
        gains = sb("gains_s", [128, 4 * DC], F32)
        vecs = sb("vecs_s", [128, 32], F32)
        cw = sb("cw_s", [128, 4 * 31], F32)
        ident = sb("ident_s", [128, 32], F32)
        ones_bf = sb("ones_bf", [128, 128], BF16)
        aq = sb("aq_s", [128, 48], F32)
        qs = sb("qs", [128, 12, 16], F32)
        carry = sb("carry", [128, 4, 8], F32)
        ctmp = sb("ctmp", [128, 4, 4], F32)
        btab = act[:, 0:4, :].rearrange("p a b -> p (a b)").bitcast(F32).rearrange("p (a b) -> p a b", a=2)
        xb0 = xbs[0]
        hbm = ms[:, 4:12, :]
        ctab = xbs[1][:, 0:4, :].rearrange("p a (b c) -> p (a b) c", c=TF)
        stab = xbs[1][:, 4:8, :].rearrange("p a (b c) -> p (a b) c", c=TF)
        pA = [ps("pA%d" % i) for i in range(4)]
        pB = [ps("pB%d" % i) for i in range(2)]
        pS = ps("pS")
        pX = ps("pX")

        P = Prog(nc, st)
        block = st.enter_context(nc.Block())

        t_xs = [toks(DC, "x%d_" % i) for i in range(3)]
        t_h = toks(DC, "h")
        t_sq = toks(4, "sq")
        t_st = toks(3, "st")
        t_sta = toks(2, "sta")
        t_act = toks(FC, "act")
        t_ms = toks(12, "ms")
        t_hm = t_ms[4:12]
        t_stmp2 = toks(2, "stmp2")
        t_stmp = toks(2, "stmp")
        t_ws2 = [toks(2, "ws%d_" % i) for i in range(RS)]
        t_wb2 = [toks(2, "wb%d_" % i) for i in range(RB)]
        t_diag = Tok("diag")
        t_abuf = toks(4, "abuf")
        t_halo = toks(4, "halo")
        t_co = toks(4, "co")
        t_u32 = toks(4, "u32")
        t_s5t = toks(4, "s5t")
        t_s5w = toks(2, "s5w")
        t_s5wb = toks(2, "s5wb")
        t_s5x = toks(4, "s5x")
        t_tab = Tok("tab")
        t_const = Tok("const")
        t_pA = toks(4, "pA")
        t_pB = toks(2, "pB")
        t_pS = Tok("pS")
        t_pX = Tok("pX")
        t_carry = toks(4, "carry")
        t_ctmp = Tok("ctmp")
        t_setup = Tok("setup")

        def g_ap(n, c):
            return gains[:, n * DC + c:n * DC + c + 1]

        def vec_ap(n, c):
            return vecs[:, n * 4 + c:n * 4 + c + 1]

        LEVEL = stages if stages is not None else 9

        for (dst, src) in ((gains, gains_d), (vecs, vecs_d), (cw, cw_d), (ident, ident_d), (aq, aq_d)):
            P.dma("sp", lambda e, dst=dst, src=src: e.dma_start(out=dst[:], in_=src), writes=[t_setup])
        P.dma("sp", lambda e: e.dma_start(out=idf[:].rearrange("p a b -> p (a b)"), in_=idf_d), writes=[t_setup])
        P.dma("sp", lambda e: e.dma_start(out=xb0[:, 0:5, :].rearrange("p a b -> p (a b)"), in_=sc_d), writes=[t_setup])
        P.dma("sp", lambda e: e.dma_start(out=xb0[:, 5:7, :].rearrange("p a b -> p (a b)"), in_=cq_d), writes=[t_setup])
        P.op("dve", lambda e: e.memset(ones_bf[:], 1.0), writes=[t_const])

        def S(fn, eng="dve"):
            P.op(eng, fn, reads=[t_setup, t_const], writes=[t_setup])

        for c in range(4):
            for k in range(31):
                S(lambda e, c=c, k=k: e.tensor_scalar(
                    out=diag[:, c * 31 + k, :], in0=ident[:], scalar1=cw[:, c * 31 + k:c * 31 + k + 1],
                    scalar2=None, op0=ALU.mult))

        PI = math.pi

        def lam_setup(are, aim, ldt, w):
            dt_, zr, th, r_, tmp, ang, sn, cs_, k_i = [w(i) for i in range(9)]
            S(lambda e: e.activation(out=dt_, in_=ldt, func=AF.Exp), "act")
            S(lambda e: e.tensor_tensor(out=zr, in0=are, in1=dt_, op=ALU.mult))
            S(lambda e: e.tensor_tensor(out=th, in0=aim, in1=dt_, op=ALU.mult))
            S(lambda e: e.activation(out=r_, in_=zr, func=AF.Exp), "act")
            for which, out_ in ((0, sn), (1, cs_)):
                off = 0.0 if which == 0 else PI / 2
                S(lambda e, off=off: e.tensor_scalar(out=ang, in0=th, scalar1=off, scalar2=None, op0=ALU.add))
                S(lambda e: e.tensor_scalar(out=tmp, in0=ang, scalar1=1.0 / (2 * PI), scalar2=None, op0=ALU.mult))
                S(lambda e: e.tensor_copy(out=k_i.bitcast(I32), in_=tmp))
                S(lambda e: e.tensor_copy(out=tmp, in_=k_i.bitcast(I32)))
                S(lambda e: e.scalar_tensor_tensor(out=ang, in0=tmp, scalar=-2 * PI, in1=ang, op0=ALU.mult, op1=ALU.add))
                S(lambda e: e.tensor_scalar(out=tmp, in0=ang, scalar1=-PI, scalar2=2 * PI, op0=ALU.is_lt, op1=ALU.mult))
                S(lambda e: e.tensor_tensor(out=ang, in0=ang, in1=tmp, op=ALU.add))
                S(lambda e: e.tensor_scalar(out=tmp, in0=ang, scalar1=PI, scalar2=-2 * PI, op0=ALU.is_gt, op1=ALU.mult))
                S(lambda e: e.tensor_tensor(out=ang, in0=ang, in1=tmp, op=ALU.add))
                S(lambda e: e.tensor_scalar(out=ang, in0=ang, scalar1=-3.1415925, scalar2=3.1415925, op0=ALU.max, op1=ALU.min))
                S(lambda e, out_=out_: e.activation(out=out_, in_=ang, func=AF.Sin), "act")
            return dict(r=r_, cos=cs_, sin=sn, dt=dt_)

        lq = lam_setup(aq[:, 0:16], aq[:, 16:32], aq[:, 32:48], lambda i: qs[:, i, :])
        S(lambda e: e.memset(ctab[:, :, 0:1], 1.0))
        S(lambda e: e.memset(stab[:, :, 0:1], 0.0))
        S(lambda e: e.tensor_copy(out=ctab[:, :, 1], in_=lq["cos"]))
        S(lambda e: e.tensor_copy(out=stab[:, :, 1], in_=lq["sin"]))
        k = 1
        while k < TF:
            for gc in range(16):
                er = ctab[:, gc, k:k + 1]
                ei = stab[:, gc, k:k + 1]
                n = min(k, TF - k)
                if n > 1:
                    src_c = ctab[:, gc, 1:n]
                    src_s = stab[:, gc, 1:n]
                    dst_c = ctab[:, gc, k + 1:k + n]
                    dst_s = stab[:, gc, k + 1:k + n]
                    tmpv = co[:, 0, 0:n - 1]
                    S(lambda e, src_s=src_s, ei=ei, tmpv=tmpv: e.tensor_scalar(out=tmpv, in0=src_s, scalar1=ei, scalar2=None, op0=ALU.mult))
                    S(lambda e, src_c=src_c, er=er, tmpv=tmpv, dst_c=dst_c: e.scalar_tensor_tensor(out=dst_c, in0=src_c, scalar=er, in1=tmpv, op0=ALU.mult, op1=ALU.subtract))
                    S(lambda e, src_s=src_s, er=er, tmpv=tmpv: e.tensor_scalar(out=tmpv, in0=src_s, scalar1=er, scalar2=None, op0=ALU.mult))
                    S(lambda e, src_c=src_c, ei=ei, tmpv=tmpv, dst_s=dst_s: e.scalar_tensor_tensor(out=dst_s, in0=src_c, scalar=ei, in1=tmpv, op0=ALU.mult, op1=ALU.add))
            if 2 * k < TF:
                for gc in range(16):
                    er = ctab[:, gc, k:k + 1]
                    ei = stab[:, gc, k:k + 1]
                    tmpv = co[:, 0, 0:1]
                    S(lambda e, ei=ei, tmpv=tmpv: e.tensor_scalar(out=tmpv, in0=ei, scalar1=ei, scalar2=None, op0=ALU.mult))
                    S(lambda e, er=er, tmpv=tmpv, gc=gc, k=k: e.scalar_tensor_tensor(out=ctab[:, gc, 2 * k:2 * k + 1], in0=er, scalar=er, in1=tmpv, op0=ALU.mult, op1=ALU.subtract))
                    S(lambda e, er=er, ei=ei, gc=gc, k=k: e.tensor_scalar(out=stab[:, gc, 2 * k:2 * k + 1], in0=er, scalar1=ei, scalar2=2.0, op0=ALU.mult, op1=ALU.mult))
            k *= 2
        S(lambda e: e.tensor_tensor(out=qs[:, 9, :], in0=ctab[:, :, TF - 1], in1=lq["cos"], op=ALU.mult))
        S(lambda e: e.tensor_tensor(out=qs[:, 11, :], in0=stab[:, :, TF - 1], in1=lq["sin"], op=ALU.mult))
        S(lambda e: e.tensor_tensor(out=qs[:, 9, :], in0=qs[:, 9, :], in1=qs[:, 11, :], op=ALU.subtract))
        S(lambda e: e.tensor_tensor(out=qs[:, 10, :], in0=ctab[:, :, TF - 1], in1=lq["sin"], op=ALU.mult))
        S(lambda e: e.tensor_tensor(out=qs[:, 11, :], in0=stab[:, :, TF - 1], in1=lq["cos"], op=ALU.mult))
        S(lambda e: e.tensor_tensor(out=qs[:, 10, :], in0=qs[:, 10, :], in1=qs[:, 11, :], op=ALU.add))
        S(lambda e: e.tensor_tensor(out=qs[:, 9, :], in0=qs[:, 9, :], in1=lq["r"], op=ALU.mult))
        S(lambda e: e.tensor_tensor(out=qs[:, 10, :], in0=qs[:, 10, :], in1=lq["r"], op=ALU.mult))
        rEr = qs[:, 9, :]
        rEi = qs[:, 10, :]
        S(lambda e: e.tensor_copy(out=ctabb[:], in_=ctab))
        S(lambda e: e.tensor_copy(out=stabb[:], in_=stab))
        S(lambda e: e.memset(rtab[:], 1.0))
        for gc in range(16):
            S(lambda e, gc=gc: e.tensor_scalar(out=rtab[:, gc, :], in0=rtab[:, gc, :], scalar1=qs[:, 3, gc:gc + 1], scalar2=None, op0=ALU.mult))
        S(lambda e: e.memset(rtab[:, :, 0:1], 0.0))

        def cv(i):
            return co[:, i, :] if i < 4 else (u32[:, i - 4, :] if i < 8 else s5v[:, 0, :])
        lc = lam_setup(xb0[:, 0, :], xb0[:, 1, :], xb0[:, 2, :], cv)
        lr_, li_ = xb0[:, 0, :], xb0[:, 1, :]
        br_, bi_ = xb0[:, 3, :], xb0[:, 4, :]
        ar_, ai_, den, cr_, ci_, tq = cv(0), cv(1), cv(2), cv(4), cv(5), s5v[:, 1, :]
        S(lambda e: e.tensor_tensor(out=ar_, in0=lc["r"], in1=lc["cos"], op=ALU.mult))
        S(lambda e: e.tensor_tensor(out=ai_, in0=lc["r"], in1=lc["sin"], op=ALU.mult))
        S(lambda e: e.tensor_scalar(out=ar_, in0=ar_, scalar1=-1.0, scalar2=None, op0=ALU.add))
        S(lambda e: e.tensor_tensor(out=den, in0=lr_, in1=lr_, op=ALU.mult))
        S(lambda e: e.tensor_tensor(out=tq, in0=li_, in1=li_, op=ALU.mult))
        S(lambda e: e.tensor_tensor(out=den, in0=den, in1=tq, op=ALU.add))
        S(lambda e: e.reciprocal(out=den, in_=den))
        S(lambda e: e.tensor_tensor(out=cr_, in0=ar_, in1=lr_, op=ALU.mult))
        S(lambda e: e.tensor_tensor(out=tq, in0=ai_, in1=li_, op=ALU.mult))
        S(lambda e: e.tensor_tensor(out=cr_, in0=cr_, in1=tq, op=ALU.add))
        S(lambda e: e.tensor_tensor(out=cr_, in0=cr_, in1=den, op=ALU.mult))
        S(lambda e: e.tensor_tensor(out=ci_, in0=ai_, in1=lr_, op=ALU.mult))
        S(lambda e: e.tensor_tensor(out=tq, in0=ar_, in1=li_, op=ALU.mult))
        S(lambda e: e.tensor_tensor(out=ci_, in0=ci_, in1=tq, op=ALU.subtract))
        S(lambda e: e.tensor_tensor(out=ci_, in0=ci_, in1=den, op=ALU.mult))
        S(lambda e: e.tensor_tensor(out=tq, in0=ci_, in1=bi_, op=ALU.mult))
        S(lambda e: e.tensor_tensor(out=den, in0=cr_, in1=br_, op=ALU.mult))
        S(lambda e: e.tensor_tensor(out=btab[:, 0, :], in0=den, in1=tq, op=ALU.subtract))
        S(lambda e: e.tensor_tensor(out=tq, in0=ci_, in1=br_, op=ALU.mult))
        S(lambda e: e.tensor_tensor(out=den, in0=cr_, in1=bi_, op=ALU.mult))
        S(lambda e: e.tensor_tensor(out=btab[:, 1, :], in0=den, in1=tq, op=ALU.add))
        for ri in range(2):
            for gc in range(16):
                uc_, j_ = divmod(gc, 4)
                S(lambda e, ri=ri, gc=gc, uc_=uc_, j_=j_: e.tensor_scalar(
                    out=btabF[:, ri, gc * 128:(gc + 1) * 128], in0=btab[:, ri, uc_ * 128:(uc_ + 1) * 128],
                    scalar1=vecs[:, 28 + j_:29 + j_], scalar2=None, op0=ALU.mult))
        S(lambda e: e.tensor_copy(out=ctb[:, 0, :], in_=xb0[:, 5, :]))
        S(lambda e: e.tensor_scalar(out=ctb[:, 1, :], in0=xb0[:, 6, :], scalar1=-1.0, scalar2=None, op0=ALU.mult))
        P.op("dve", lambda e: e.memset(ctmp[:], 0.0), reads=[t_setup],
             writes=[t_tab, t_diag, t_ctmp] + t_xs[0] + t_xs[1] + t_st + t_co + t_u32 + t_s5t + t_act[0:4])

        wstate = {"ns": 0, "nb": 0, "cs": 0, "cb": 0}
        total_small = ntiles * NSMALL
        total_big = ntiles * NBIG
        small_seq = []
        big_seq = []

        scr_s = {}
        scr_b = {}

        def issue_small(piece):
            n = wstate["ns"]
            slot = n % RS
            if piece not in scr_s:
                for hh in range(2):
                    P.dma("pool", lambda e, hh=hh, slot=slot, piece=piece: e.dma_start(
                        out=ws[:, slot, hh * 1024:(hh + 1) * 1024], in_=wsm_d[piece][:, hh * 1024:(hh + 1) * 1024]),
                        writes=[t_ws2[slot][hh]])
                tk = Tok("scr_s%d" % piece)
                P.dma("sp", lambda e, slot=slot, piece=piece: e.dma_start(out=wsm_bf[piece], in_=ws[:, slot, :]),
                      reads=t_ws2[slot], writes=[tk])
                scr_s[piece] = tk
            else:
                P.dma("sp", lambda e, slot=slot, piece=piece: e.dma_start(out=ws[:, slot, :], in_=wsm_bf[piece]),
                      reads=[scr_s[piece]], writes=t_ws2[slot])
            wstate["ns"] += 1

        def issue_big(piece):
            n = wstate["nb"]
            slot = n % RB
            HB = DFF // 2
            if piece not in scr_b:
                for hh in range(2):
                    P.dma("pool", lambda e, hh=hh, slot=slot, piece=piece: e.dma_start(
                        out=wb[:, slot, hh * HB:(hh + 1) * HB], in_=wbg_d[piece][:, hh * HB:(hh + 1) * HB]),
                        writes=[t_wb2[slot][hh]])
                tk = Tok("scr_b%d" % piece)
                P.dma("sp", lambda e, slot=slot, piece=piece: e.dma_start(out=wbg_bf[piece], in_=wb[:, slot, :]),
                      reads=t_wb2[slot], writes=[tk])
                scr_b[piece] = tk
            else:
                P.dma("sp", lambda e, slot=slot, piece=piece: e.dma_start(out=wb[:, slot, :], in_=wbg_bf[piece]),
                      reads=[scr_b[piece]], writes=t_wb2[slot])
            wstate["nb"] += 1

        def need_small(piece):
            n = wstate["cs"]
            assert small_plan[n] == piece, (n, small_plan[n], piece)
            while wstate["ns"] < min(n + RS, len(small_plan)):
                issue_small(small_plan[wstate["ns"]])
            wstate["cs"] += 1
            return n % RS

        def need_big(piece):
            n = wstate["cb"]
            assert big_plan[n] == piece, (n, big_plan[n], piece)
            while wstate["nb"] < min(n + RB, len(big_plan)):
                issue_big(big_plan[wstate["nb"]])
            wstate["cb"] += 1
            return n % RB

        def rms_stats(src_fn, src_toks, nchunks, dim, dst, t_dst):
            for c in range(nchunks):
                s = c % 4
                P.op("act", lambda e, c=c, s=s: e.activation(out=sq[:, s, :], in_=src_fn(c), func=AF.Square),
                     reads=[src_toks[c]], writes=[t_sq[s]])
                P.op("pe", lambda e, c=c, s=s: e.matmul(pS[:], lhsT=ones_bf[:], rhs=sq[:, s, :],
                                                        start=(c == 0), stop=(c == nchunks - 1)),
                     reads=[t_sq[s], t_const], writes=[t_pS])
            P.op("act", lambda e: e.activation(out=dst[:, 0, :], in_=pS[:], func=AF.Ln, scale=1.0 / dim, bias=EPS),
                 reads=[t_pS], writes=[t_dst[0]])
            P.op("act", lambda e: e.activation(out=dst[:, 1, :], in_=dst[:, 0, :], func=AF.Exp, scale=-0.5),
                 reads=[t_dst[0]], writes=[t_dst[1]])

        def norm_to_h(gi, xb, t_x, dst, t_dst, hdst=None, t_hdst=None):
            hdst = hb if hdst is None else hdst
            t_hdst = t_h if t_hdst is None else t_hdst
            rms_stats(lambda c: xb[:, c, :], t_x, DC, D, dst, t_dst)
            for c in range(DC):
                P.op("dve", lambda e, c=c: e.scalar_tensor_tensor(
                    out=hdst[:, c, :], in0=xb[:, c, :], scalar=g_ap(gi, c), in1=dst[:, 1, :],
                    op0=ALU.mult, op1=ALU.mult), reads=[t_x[c], t_dst[1]], writes=[t_hdst[c]])

        def ffn_A(fc, sbase):
            slot = need_small(sbase + fc)
            b = 2 * (fc % 2)

            def mm(e, slot=slot, b=b):
                for kc in range(DC):
                    e.matmul(pA[b][:], lhsT=ws[:, slot, kc * 128:(kc + 1) * 128], rhs=hb[:, kc, :],
                             start=(kc == 0), stop=(kc == DC - 1))
                for kc in range(DC):
                    ins = e.matmul(pA[b + 1][:], lhsT=ws[:, slot, 1024 + kc * 128:1024 + (kc + 1) * 128],
                                   rhs=hb[:, kc, :], start=(kc == 0), stop=(kc == DC - 1))
                return ins
            P.op("pe", mm, reads=t_ws2[slot] + t_h, writes=[t_pA[b], t_pA[b + 1]])
            s = fc % 2
            P.op("act", lambda e, s=s, b=b: e.activation(out=stmp[:, s, :], in_=pA[b][:], func=AF.Silu),
                 reads=[t_pA[b]], writes=[t_stmp[s]])
            P.op("act", lambda e, s=s, b=b: e.activation(out=stmp2[:, s, :], in_=pA[b + 1][:], func=AF.Copy),
                 reads=[t_pA[b + 1]], writes=[t_stmp2[s]])
            P.op("pool", lambda e, s=s, fc=fc: e.tensor_tensor(out=act[:, fc, :], in0=stmp2[:, s, :], in1=stmp[:, s, :], op=ALU.mult),
                 reads=[t_stmp2[s], t_stmp[s]], writes=[t_act[fc]])

        def ffn_B(dc, bbase, xb, t_x):
            slot = need_big(bbase + dc)
            b = 2 * (dc % 2)

            def mm2(e, slot=slot, b=b, dc=dc):
                e.matmul(pA[b][:], lhsT=idf[:, 1, :], rhs=xb[:, dc, :], start=True, stop=False)
                for fc in range(FC):
                    ins = e.matmul(pA[b][:], lhsT=wb[:, slot, fc * 128:(fc + 1) * 128], rhs=act[:, fc, :],
                                   start=False, stop=(fc == FC - 1))
                return ins
            P.op("pe", mm2, reads=t_wb2[slot] + t_act + [t_x[dc], t_tab], writes=[t_pA[b]])
            P.op("act", lambda e, dc=dc, b=b: e.activation(out=xb[:, dc, :], in_=pA[b][:], func=AF.Copy, scale=0.5),
                 reads=[t_pA[b]], writes=[t_x[dc]])

        def ffn_gen(gi, xb, t_x, sbase, bbase):
            norm_to_h(gi, xb, t_x, sta, t_sta)
            yield
            for fc in range(FC):
                ffn_A(fc, sbase)
                yield
            for dc in range(DC):
                ffn_B(dc, bbase, xb, t_x)
                yield

        def final_store(ti, xb, t_x):
            rms_stats(lambda c: xb[:, c, :], t_x, DC, D, sta, t_sta)
            for c in range(DC):
                P.op("dve", lambda e, c=c: e.scalar_tensor_tensor(
                    out=xb[:, c, :], in0=xb[:, c, :], scalar=g_ap(3, c), in1=sta[:, 1, :],
                    op0=ALU.mult, op1=ALU.mult), reads=[t_x[c], t_sta[1]], writes=[t_x[c]])
            cols = slice(ti * T, (ti + 1) * T)
            out_toks.append(P.dma("sp", lambda e, cols=cols, xb=xb: e.dma_start(out=oT_v[:, :, cols], in_=xb[:]), reads=t_x))
            if ti + 3 < ntiles:
                load_x(ti + 3)

        def ffn_pair_gen(t2, t1):
            xb2, tx2 = xbs[t2 % 3], t_xs[t2 % 3]
            xb1, tx1 = xbs[t1 % 3], t_xs[t1 % 3]
            norm_to_h(2, xb2, tx2, sta, t_sta)
            yield
            for fc in range(FC):
                ffn_A(fc, 33)
                yield
            norm_to_h(0, xb1, tx1, sta, t_sta)
            yield
            for dc in range(DC):
                ffn_B(dc, 8, xb2, tx2)
                yield
            for fc in range(2):
                ffn_A(fc, 0)
                yield
            final_store(t2, xb2, tx2)
            yield
            for fc in range(2, FC):
                ffn_A(fc, 0)
                yield
            for dc in range(DC):
                ffn_B(dc, 0, xb1, tx1)
                yield

        UBF = (0, 1, 2, 3)
        GLB = UBF
        MIX = tuple(range(4, 12))

        def conv_gen(ti):
            seq_end = (ti % TPS == TPS - 1)
            for c in range(4):
                def mmc(e, c=c):
                    for k in range(31):
                        for i in range(4):
                            ins = e.matmul(pS[32 * i:32 * i + 32, :], lhsT=diag[32 * i:32 * i + 32, c * 31 + k, :],
                                           rhs=abuf[32 * i:32 * i + 32, c, k:k + T],
                                           start=(k == 0), stop=(k == 30), tile_position=(32 * i, 32 * i))
                    return ins
                P.op("pe", mmc, reads=[t_diag, t_abuf[c], t_halo[c]], writes=[t_pS])
                P.op("act", lambda e, c=c: e.activation(out=co[:, c, :], in_=pS[:], func=AF.Identity, bias=vec_ap(0, c)),
                     reads=[t_pS, t_tab], writes=[t_co[c]])
                if not seq_end:
                    P.op("pool", lambda e, c=c: e.tensor_copy(out=abuf[:, c, 0:30], in_=abuf[:, c, T:T + 30]),
                         reads=[t_abuf[c]], writes=[t_halo[c]])
                P.op("act", lambda e, c=c: e.activation(out=sq[:, c, :], in_=co[:, c, :], func=AF.Square),
                     reads=[t_co[c]], writes=[t_sq[c]])
                P.op("act", lambda e, c=c: e.activation(out=ms[:, MIX[c], :], in_=co[:, c, :], func=AF.Copy),
                     reads=[t_co[c]], writes=[t_ms[MIX[c]]])
                yield
            for c in range(4):
                P.op("pe", lambda e, c=c: e.matmul(pS[:], lhsT=ones_bf[:], rhs=ms[:, MIX[c], :], start=(c == 0), stop=(c == 3)),
                     reads=[t_ms[MIX[c]], t_const], writes=[t_pS])
            P.op("act", lambda e: e.activation(out=stt_[:, 0, :], in_=pS[:], func=AF.Copy, scale=1.0 / 512),
                 reads=[t_pS], writes=[t_st[0]])
            P.op("dve", lambda e: e.tensor_tensor(out=stt_[:, 2, :], in0=stt_[:, 0, :], in1=stt_[:, 0, :], op=ALU.mult),
                 reads=[t_st[0]], writes=[t_st[2]])
            yield
            for c in range(4):
                P.op("pe", lambda e, c=c: e.matmul(pS[:], lhsT=ones_bf[:], rhs=sq[:, c, :], start=(c == 0), stop=(c == 3)),
                     reads=[t_sq[c], t_const], writes=[t_pS])
            P.op("dve", lambda e: e.scalar_tensor_tensor(out=stt_[:, 2, :], in0=pS[:], scalar=1.0 / 512, in1=stt_[:, 2, :],
                                                         op0=ALU.mult, op1=ALU.subtract),
                 reads=[t_pS, t_st[2]], writes=[t_st[2]])
            P.op("act", lambda e: e.activation(out=stt_[:, 2, :], in_=stt_[:, 2, :], func=AF.Ln, bias=EPS),
                 reads=[t_st[2]], writes=[t_st[2]])
            P.op("act", lambda e: e.activation(out=stt_[:, 1, :], in_=stt_[:, 2, :], func=AF.Exp, scale=-0.5),
                 reads=[t_st[2]], writes=[t_st[1]])
            P.op("dve", lambda e: e.tensor_tensor(out=stt_[:, 2, :], in0=stt_[:, 0, :], in1=stt_[:, 1, :], op=ALU.mult),
                 reads=[t_st[0], t_st[1], t_st[2]], writes=[t_st[2]])
            yield
            for c in range(4):
                P.op("dve", lambda e, c=c: e.tensor_tensor(out=co[:, c, :], in0=co[:, c, :], in1=stt_[:, 1, :], op=ALU.mult),
                     reads=[t_co[c], t_st[1]], writes=[t_co[c]])
                P.op("dve", lambda e, c=c: e.tensor_tensor(out=co[:, c, :], in0=co[:, c, :], in1=stt_[:, 2, :], op=ALU.subtract),
                     reads=[t_co[c], t_st[2]], writes=[t_co[c]])
                P.op("act", lambda e, c=c: e.activation(out=co[:, c, :], in_=co[:, c, :], func=AF.Silu,
                                                        scale=vec_ap(1, c), bias=vec_ap(2, c)),
                     reads=[t_co[c], t_tab], writes=[t_co[c]])
                yield
            rms_stats(lambda c: co[:, c, :], t_co, 4, 512, stt_, t_st)
            yield
            for c in range(4):
                P.op("dve", lambda e, c=c: e.scalar_tensor_tensor(
                    out=ms[:, MIX[c], :], in0=co[:, c, :], scalar=vec_ap(3, c), in1=stt_[:, 1, :],
                    op0=ALU.mult, op1=ALU.mult), reads=[t_co[c], t_st[1], t_tab], writes=[t_ms[MIX[c]]])

        def mixer_gen(ti, xb, t_x):
            seq_start = (ti % TPS == 0)
            seq_end = (ti % TPS == TPS - 1)
            norm_to_h(1, xb, t_x, stt_, t_st, hbm, t_hm)
            if seq_start:
                for c in range(4):
                    P.op("dve", lambda e, c=c: e.memset(abuf[:, c, 0:30], 0.0), writes=[t_halo[c]])
            yield
            for m, pair in enumerate(WIN_PAIRS):
                slot = need_small(22 + m)
                for o2, oc in enumerate(pair):
                    b = o2

                    def mm(e, slot=slot, o2=o2, b=b):
                        for kc in range(DC):
                            ins = e.matmul(pB[b][:], lhsT=ws[:, slot, (o2 * 8 + kc) * 128:(o2 * 8 + kc + 1) * 128],
                                           rhs=hbm[:, kc, :], start=(kc == 0), stop=(kc == DC - 1))
                        return ins
                    P.op("pe", mm, reads=t_ws2[slot] + t_hm, writes=[t_pB[b]])
                    if 4 <= oc < 8:
                        c = oc - 4
                        P.op("act", lambda e, b=b, c=c: e.activation(out=s5x[:, c % 2, :], in_=pB[b][:], func=AF.Sigmoid),
                             reads=[t_pB[b]], writes=[t_s5x[c % 2]])
                    elif oc < 4:
                        c = oc
                        P.op("dve", lambda e, b=b, c=c: e.tensor_tensor(out=abuf[:, c, 30:30 + T], in0=pB[b][:], in1=s5x[:, c % 2, :], op=ALU.mult),
                             reads=[t_pB[b], t_s5x[c % 2]], writes=[t_abuf[c]])
                    else:
                        c = oc - 8
                        P.op("act", lambda e, b=b, c=c: e.activation(out=u32[:, c, :], in_=pB[b][:], func=AF.Copy),
                             reads=[t_pB[b]], writes=[t_u32[c]])
                        P.op("act", lambda e, c=c: e.activation(out=ms[:, UBF[c], :], in_=u32[:, c, :], func=AF.Copy),
                             reads=[t_u32[c]], writes=[t_ms[UBF[c]]])
            yield

            cg = conv_gen(ti)
            steps = [(uc, f) for uc in range(4) for f in range(NF)]
            NS = len(steps)
            D_ = lambda fn, r, w: P.op("dve", fn, reads=r, writes=w)
            G_ = lambda fn, r, w: P.op("dve", fn, reads=r, writes=w)
            tA, tB, vr, vi = s5tb[:, 0, :], s5tb[:, 1, :], s5v[:, 0, :], s5v[:, 1, :]
            wr, wi = s5w[:, 0, :], s5w[:, 1, :]
            bre, bim = pB[0], pB[1]

            def tabs(uc):
                return (ctabb[:, 4 * uc:4 * uc + 4, :].rearrange("p a b -> p (a b)"),
                        stabb[:, 4 * uc:4 * uc + 4, :].rearrange("p a b -> p (a b)"),
                        rtab[:, 4 * uc:4 * uc + 4, :].rearrange("p a b -> p (a b)"))

            def mmb_op(si):
                uc, f = steps[si]
                fs = slice(f * TF, (f + 1) * TF)

                def mmb(e, uc=uc, fs=fs):
                    for ri in range(2):
                        for j in range(4):
                            gc = 4 * uc + j
                            ins = e.matmul(pB[ri][:, j * TF:(j + 1) * TF],
                                           lhsT=btabF[:, ri, gc * 128:(gc + 1) * 128],
                                           rhs=ms[:, UBF[uc], fs], start=True, stop=True)
                    return ins
                P.op("pe", mmb, reads=[t_tab, t_ms[UBF[uc]]], writes=[t_pB[0], t_pB[1]])

            def fwd(si):
                uc, f = steps[si]
                cT, sT, rT = tabs(uc)
                D_(lambda e: e.tensor_tensor(out=tA, in0=bre[:], in1=cT, op=ALU.mult), [t_pB[0], t_tab], [t_s5t[0]])
                D_(lambda e: e.tensor_tensor(out=tB, in0=bim[:], in1=sT, op=ALU.mult), [t_pB[1], t_tab], [t_s5t[1]])
                D_(lambda e: e.tensor_tensor(out=vr, in0=tA, in1=tB, op=ALU.add), [t_s5t[0], t_s5t[1]], [t_s5t[2]])
                D_(lambda e: e.tensor_tensor(out=tA, in0=bim[:], in1=cT, op=ALU.mult), [t_pB[1], t_tab], [t_s5t[0]])
                D_(lambda e: e.tensor_tensor(out=tB, in0=bre[:], in1=sT, op=ALU.mult), [t_pB[0], t_tab], [t_s5t[1]])
                D_(lambda e: e.tensor_tensor(out=vi, in0=tA, in1=tB, op=ALU.subtract), [t_s5t[0], t_s5t[1]], [t_s5t[3]])

            def scans(si):
                uc, f = steps[si]
                cT, sT, rT = tabs(uc)
                first = seq_start and f == 0
                if not first:
                    vr0 = s5v[:, 0, :].rearrange("p (a b) -> p a b", a=4)[:, :, 0]
                    vi0 = s5v[:, 1, :].rearrange("p (a b) -> p a b", a=4)[:, :, 0]
                    D_(lambda e: e.tensor_tensor(out=vr0, in0=vr0, in1=carry[:, uc, 0:4], op=ALU.add), [t_s5t[2], t_carry[uc]], [t_s5t[2]])
                    D_(lambda e: e.tensor_tensor(out=vi0, in0=vi0, in1=carry[:, uc, 4:8], op=ALU.add), [t_s5t[3], t_carry[uc]], [t_s5t[3]])
                D_(lambda e: e.tensor_tensor_scan(out=wr, data0=rT, data1=vr, initial=0.0, op0=ALU.mult, op1=ALU.add), [t_s5t[2], t_tab], [t_s5w[0]])
                D_(lambda e: e.tensor_tensor_scan(out=wi, data0=rT, data1=vi, initial=0.0, op0=ALU.mult, op1=ALU.add), [t_s5t[3], t_tab], [t_s5w[1]])
                P.op("act", lambda e: e.activation(out=s5wb[:, 0, :], in_=wr, func=AF.Copy), reads=[t_s5w[0]], writes=[t_s5wb[0]])
                P.op("act", lambda e: e.activation(out=s5wb[:, 1, :], in_=wi, func=AF.Copy), reads=[t_s5w[1]], writes=[t_s5wb[1]])
                last = seq_end and f == NF - 1
                if not last:
                    wr1 = s5w[:, 0, :].rearrange("p (a b) -> p a b", a=4)[:, :, TF - 1]
                    wi1 = s5w[:, 1, :].rearrange("p (a b) -> p a b", a=4)[:, :, TF - 1]
                    er4 = rEr[:, 4 * uc:4 * uc + 4]
                    ei4 = rEi[:, 4 * uc:4 * uc + 4]
                    D_(lambda e: e.tensor_tensor(out=ctmp[:, 0, :], in0=wr1, in1=er4, op=ALU.mult), [t_s5w[0], t_tab, t_ctmp], [t_ctmp])
                    D_(lambda e: e.tensor_tensor(out=ctmp[:, 1, :], in0=wi1, in1=ei4, op=ALU.mult), [t_s5w[1], t_tab, t_ctmp], [t_ctmp])
                    D_(lambda e: e.tensor_tensor(out=ctmp[:, 2, :], in0=wi1, in1=er4, op=ALU.mult), [t_s5w[1], t_tab, t_ctmp], [t_ctmp])
                    D_(lambda e: e.tensor_tensor(out=ctmp[:, 3, :], in0=wr1, in1=ei4, op=ALU.mult), [t_s5w[0], t_tab, t_ctmp], [t_ctmp])
                    D_(lambda e: e.tensor_tensor(out=carry[:, uc, 0:4], in0=ctmp[:, 0, :], in1=ctmp[:, 1, :], op=ALU.subtract), [t_ctmp], [t_carry[uc]])
                    D_(lambda e: e.tensor_tensor(out=carry[:, uc, 4:8], in0=ctmp[:, 2, :], in1=ctmp[:, 3, :], op=ALU.add), [t_ctmp], [t_carry[uc]])

            def combos(si):
                uc, f = steps[si]
                cT, sT, rT = tabs(uc)
                xs = 2 * (si % 2)
                xr, xi = s5x[:, xs, :], s5x[:, xs + 1, :]
                wrb, wib = s5wb[:, 0, :], s5wb[:, 1, :]
                D_(lambda e: e.tensor_tensor(out=tA, in0=wrb, in1=cT, op=ALU.mult), [t_s5wb[0], t_tab], [t_s5t[0]])
                D_(lambda e: e.tensor_tensor(out=tB, in0=wib, in1=sT, op=ALU.mult), [t_s5wb[1], t_tab], [t_s5t[1]])
                D_(lambda e: e.tensor_tensor(out=xr, in0=tA, in1=tB, op=ALU.subtract), [t_s5t[0], t_s5t[1]], [t_s5x[xs]])
                D_(lambda e: e.tensor_tensor(out=tA, in0=wib, in1=cT, op=ALU.mult), [t_s5wb[1], t_tab], [t_s5t[0]])
                D_(lambda e: e.tensor_tensor(out=tB, in0=wrb, in1=sT, op=ALU.mult), [t_s5wb[0], t_tab], [t_s5t[1]])
                D_(lambda e: e.tensor_tensor(out=xi, in0=tA, in1=tB, op=ALU.add), [t_s5t[0], t_s5t[1]], [t_s5x[xs + 1]])

            def mmy_op(si):
                uc, f = steps[si]
                xs = 2 * (si % 2)
                fs = slice(f * TF, (f + 1) * TF)

                def mmy(e):
                    for j in range(4):
                        for ri in range(2):
                            gc = 4 * uc + j
                            ins = e.matmul(pX[32 * j:32 * j + 32, fs],
                                           lhsT=ctb[:, ri, gc * 32:(gc + 1) * 32],
                                           rhs=s5x[:, xs + ri, j * TF:(j + 1) * TF],
                                           start=(ri == 0), stop=(ri == 1), tile_position=(0, 32 * j))
                    return ins
                P.op("pe", mmy, reads=[t_tab, t_s5x[xs], t_s5x[xs + 1]], writes=[t_pX])
                if f == NF - 1:
                    yv = u32[:, uc, :]
                    P.op("dve", lambda e: e.scalar_tensor_tensor(
                        out=yv, in0=yv, scalar=vec_ap(4, uc), in1=pX[:], op0=ALU.mult, op1=ALU.add),
                        reads=[t_u32[uc], t_pX, t_tab], writes=[t_u32[uc]])
                    s = uc % 2
                    tq_ = sta[:, s, :]
                    P.op("act", lambda e: e.activation(out=tq_, in_=yv, func=AF.Square),
                         reads=[t_u32[uc]], writes=[t_sta[s]])
                    P.op("dve", lambda e: e.tensor_scalar(out=tq_, in0=tq_, scalar1=0.044715, scalar2=1.0, op0=ALU.mult, op1=ALU.add),
                         reads=[t_sta[s]], writes=[t_sta[s]])
                    P.op("dve", lambda e: e.tensor_tensor(out=tq_, in0=tq_, in1=yv, op=ALU.mult),
                         reads=[t_sta[s], t_u32[uc]], writes=[t_sta[s]])
                    P.op("act", lambda e: e.activation(out=tq_, in_=tq_, func=AF.Sigmoid, scale=1.5957691216057308),
                         reads=[t_sta[s]], writes=[t_sta[s]])
                    P.op("dve", lambda e: e.tensor_tensor(out=yv, in0=yv, in1=tq_, op=ALU.mult),
                         reads=[t_sta[s], t_u32[uc]], writes=[t_u32[uc]])
                    P.op("act", lambda e: e.activation(out=ms[:, GLB[uc], :], in_=yv, func=AF.Copy),
                         reads=[t_u32[uc]], writes=[t_ms[GLB[uc]]])

            mmb_op(0)
            fwd(0)
            scans(0)
            for si in range(NS):
                combos(si)
                if si + 1 < NS:
                    mmb_op(si + 1)
                yield
                if si + 1 < NS:
                    fwd(si + 1)
                    scans(si + 1)
                try:
                    next(cg)
                except StopIteration:
                    pass
                yield
                mmy_op(si)
            for _ in cg:
                pass
            slot = need_small(28)
            for oc in range(4):
                b = oc % 2

                def mmg(e, slot=slot, oc=oc, b=b):
                    for kc in range(4):
                        ins = e.matmul(pB[b][:], lhsT=ws[:, slot, (oc * 4 + kc) * 128:(oc * 4 + kc + 1) * 128],
                                       rhs=ms[:, GLB[kc], :], start=(kc == 0), stop=(kc == 3))
                    return ins
                P.op("pe", mmg, reads=t_ws2[slot] + [t_ms[g] for g in GLB], writes=[t_pB[b]])
                s = oc % 2
                tq_ = sta[:, s, :]
                P.op("act", lambda e, b=b, oc=oc, tq_=tq_: e.activation(out=tq_, in_=pB[b][:], func=AF.Sigmoid, bias=vec_ap(5, oc)),
                     reads=[t_pB[b], t_tab], writes=[t_sta[s]])
                P.op("dve", lambda e, oc=oc, tq_=tq_: e.tensor_tensor(out=u32[:, oc, :], in0=u32[:, oc, :], in1=tq_, op=ALU.mult),
                     reads=[t_sta[s], t_u32[oc]], writes=[t_u32[oc]])
            yield
            rms_stats(lambda c: u32[:, c, :], t_u32, 4, 512, stt_, t_st)
            for c in range(4):
                P.op("dve", lambda e, c=c: e.scalar_tensor_tensor(
                    out=ms[:, MIX[4 + c], :], in0=u32[:, c, :], scalar=vec_ap(6, c), in1=stt_[:, 1, :],
                    op0=ALU.mult, op1=ALU.mult), reads=[t_u32[c], t_st[1], t_tab], writes=[t_ms[MIX[4 + c]]])
            yield
            for m in range(4):
                slot = need_small(29 + m)
                for d2 in range(2):
                    dc = 2 * m + d2
                    b = dc % 2

                    def mmo(e, slot=slot, d2=d2, b=b, dc=dc):
                        e.matmul(pB[b][:], lhsT=idf[:, 0, :], rhs=xb[:, dc, :], start=True, stop=False)
                        for kc in range(8):
                            ins = e.matmul(pB[b][:], lhsT=ws[:, slot, (d2 * 8 + kc) * 128:(d2 * 8 + kc + 1) * 128],
                                           rhs=ms[:, MIX[kc], :], start=False, stop=(kc == 7))
                        return ins
                    P.op("pe", mmo, reads=t_ws2[slot] + [t_ms[g] for g in MIX] + [t_x[dc], t_tab], writes=[t_pB[b]])
                    P.op("act", lambda e, dc=dc, b=b: e.activation(out=xb[:, dc, :], in_=pB[b][:], func=AF.Copy),
                         reads=[t_pB[b]], writes=[t_x[dc]])
                yield

        small_plan, big_plan = [], []
        MIXP = list(range(22, 28)) + [28] + list(range(29, 33))
        small_plan += list(range(0, 22))
        big_plan += list(range(0, 8))
        for ti in range(ntiles):
            pass

        out_toks = []
        xT_v = xT.rearrange("(c p) t -> p c t", p=128)
        oT_v = oT.rearrange("(c p) t -> p c t", p=128)

        def load_x(ti):
            cols = slice(ti * T, (ti + 1) * T)
            b = ti % 3
            P.dma("sp", lambda e, cols=cols, b=b: e.dma_start(out=xbs[b][:], in_=xT_v[:, :, cols]), writes=t_xs[b])

        def finalize_gen(ti):
            xb, t_x = xbs[ti % 3], t_xs[ti % 3]
            for _ in ffn_gen(2, xb, t_x, 33, 8):
                yield
            final_store(ti, xb, t_x)
            yield

        def chain(*gens):
            for g in gens:
                if g is not None:
                    for _ in g:
                        yield

        def interleave(gm, ga, per):
            k = 0
            m_done = a_done = False
            while not (m_done and a_done):
                if not m_done:
                    try:
                        next(gm)
                    except StopIteration:
                        m_done = True
                n = per(k) if not m_done else 10 ** 9
                k += 1
                for _ in range(n):
                    if a_done:
                        break
                    try:
                        next(ga)
                    except StopIteration:
                        a_done = True

        def per_full(k):
            if k < 2:
                return 3
            if k < 2 + 32:
                return 1 if k % 2 == 0 else 2
            return 2

        def per_half(k):
            if k < 2:
                return 1
            if k < 2 + 32:
                return 1 if k % 2 == 1 else 0
            return 2

        def run_schedule(dry):
            load_x(0)
            for _ in ffn_gen(0, xbs[0], t_xs[0], 0, 0):
                pass
            if ntiles > 1:
                load_x(1)
            if ntiles > 2:
                load_x(2)
            for ti in range(ntiles):
                b = ti % 3
                has2 = ti >= 1
                has1 = ti + 1 < ntiles
                if has2 and has1:
                    ga = ffn_pair_gen(ti - 1, ti + 1)
                elif has2:
                    ga = finalize_gen(ti - 1)
                elif has1:
                    ga = ffn_gen(0, xbs[(ti + 1) % 3], t_xs[(ti + 1) % 3], 0, 0)
                else:
                    ga = None
                interleave(mixer_gen(ti, xbs[b], t_xs[b]), ga if ga is not None else iter(()), per_full if (has2 and has1) else per_half)
            for _ in finalize_gen(ntiles - 1):
                pass

        real_need_small, real_need_big = need_small, need_big
        real_P = P
        plan_s, plan_b = [], []

        class _Null:
            def op(self, *a, **k):
                return None

            def dma(self, *a, **k):
                return ("x", 0)
        P = _Null()

        def need_small(piece):
            plan_s.append(piece)
            return 0

        def need_big(piece):
            plan_b.append(piece)
            return 0
        run_schedule(True)
        small_plan, big_plan = plan_s, plan_b
        out_toks.clear()
        P = real_P
        need_small, need_big = real_need_small, real_need_big
        run_schedule(False)
        P.wait_all("sp", out_toks)
        P.emit(block)
    return nc


def _host_layout(inp):
    f = lambda a: np.ascontiguousarray(a, dtype=np.float32)
    L = 0
    small = np.empty((NSMALL, 128, 2048), np.float32)

    def w13(w1, w3):
        a = w1.reshape(8, 128, FC, 128).transpose(2, 1, 0, 3)
        b = w3.reshape(8, 128, FC, 128).transpose(2, 1, 0, 3)
        return np.stack([a, b], axis=2).reshape(FC, 128, 2048)
    small[0:22] = w13(inp["ffn1_w1"][L], inp["ffn1_w3"][L])
    win = inp["w_in"][L].reshape(8, 128, 12, 128)
    for m, pair in enumerate(WIN_PAIRS):
        small[22 + m] = np.stack([win[:, :, oc, :].transpose(1, 0, 2) for oc in pair], axis=1).reshape(128, 2048)
    glu = inp["ssm_glu_w"][L].reshape(4, 128, 4, 128)
    small[28] = glu.transpose(1, 2, 0, 3).reshape(128, 2048)
    wo = inp["w_out"][L].reshape(8, 128, 4, 2, 128)
    small[29:33] = wo.transpose(2, 1, 3, 0, 4).reshape(4, 128, 2048)
    small[33:55] = w13(inp["ffn2_w1"][L], inp["ffn2_w3"][L])
    big = np.empty((NBIG, 128, DFF), np.float32)
    big[0:8] = inp["ffn1_w2"][L].reshape(FC, 128, 8, 128).transpose(2, 1, 0, 3).reshape(8, 128, DFF)
    big[8:16] = inp["ffn2_w2"][L].reshape(FC, 128, 8, 128).transpose(2, 1, 0, 3).reshape(8, 128, DFF)

    def pc(v, n):
        return v.reshape(n, 128).T
    gains = np.concatenate([pc(inp["norm_ffn1"][L], 8), pc(inp["norm_mix"][L], 8),
                            pc(inp["norm_ffn2"][L], 8), pc(inp["norm_final"], 8)], axis=1)
    vecs = np.concatenate([pc(inp["conv_b"][L], 4), pc(inp["conv_ln_g"][L], 4), pc(inp["conv_ln_b"][L], 4),
                           pc(inp["conv_out_g"][L], 4), pc(inp["ssm_D"][L], 4), pc(inp["ssm_glu_b"][L], 4),
                           pc(inp["ssm_out_g"][L], 4),
                           (np.arange(128)[:, None] // 32 == np.arange(4)[None, :]).astype(np.float32)], axis=1)
    cw = inp["conv_w"][L].reshape(31, 4, 128).transpose(2, 1, 0).reshape(128, 124)
    ident = (np.arange(128)[:, None] % 32 == np.arange(32)[None, :]).astype(np.float32)
    A_re, A_im, ldt = inp["ssm_A_re"][L], inp["ssm_A_im"][L], inp["ssm_log_dt"][L]

    def ql(a):
        return a.reshape(16, 2, 64).transpose(1, 2, 0).reshape(128, 16)
    aq = np.concatenate([ql(A_re), ql(A_im), ql(np.broadcast_to(ldt[:, None], (32, 64)))], axis=1)
    B_re, B_im = inp["ssm_B_re"][L], inp["ssm_B_im"][L]
    sc = np.zeros((5, 128, 4, 128), np.float32)
    for g in range(32):
        uc, g8 = divmod(g, 8)
        rows = slice(16 * g8, 16 * g8 + 16)
        sc[0, rows, uc, :] = np.tile(A_re[g], 2)[None, :]
        sc[1, rows, uc, :] = np.tile(A_im[g], 2)[None, :]
        sc[2, rows, uc, :] = ldt[g]
        qo = 64 * (g % 2)
        sc[3, rows, uc, qo:qo + 64] = B_re[g].T
        sc[4, rows, uc, qo:qo + 64] = B_im[g].T
    sc = sc.transpose(1, 0, 2, 3).reshape(128, 5 * 512)
    C_re, C_im = inp["ssm_C_re"][L], inp["ssm_C_im"][L]
    cq = np.zeros((2, 128, 16, 32), np.float32)
    for g in range(32):
        gc, g2 = divmod(g, 2)
        cq[0, 64 * g2:64 * g2 + 64, gc, 16 * g2:16 * g2 + 16] = C_re[g].T
        cq[1, 64 * g2:64 * g2 + 64, gc, 16 * g2:16 * g2 + 16] = C_im[g].T
    cq = cq.transpose(1, 0, 2, 3).reshape(128, 2 * 512)
    idf = np.concatenate([np.eye(128, dtype=np.float32), 2.0 * np.eye(128, dtype=np.float32)], axis=1)
    return dict(wsmall=small, wbig=big, gains=f(gains), vecs=f(vecs), cw=f(cw), ident=ident, idf=idf,
                aq=f(aq), sc=f(sc), cq=f(cq))


_NC_CACHE = {}


def kernel(**inputs):
    inp = {k: np.asarray(v) for k, v in inputs.items()}
    x = inp["x"]
    shared = _host_layout(inp)
    in_maps = []
    for c in range(NCORES):
        xc = x[NSEQ * c:NSEQ * (c + 1)].reshape(NSEQ * SEQ, D)
        m = dict(shared)
        m["xT"] = np.ascontiguousarray(xc.T)
        in_maps.append(m)
    if "nc" not in _NC_CACHE:
        _NC_CACHE["nc"] = build_nc()
    res = run_bass_kernel_spmd(_NC_CACHE["nc"], in_maps, core_ids=list(range(NCORES)))
    out = np.empty((16, SEQ, D), np.float32)
    for c in range(NCORES):
        o = res.results[c]["oT"]
        out[NSEQ * c:NSEQ * (c + 1)] = np.ascontiguousarray(o.T).reshape(NSEQ, SEQ, D)
    return out
```

```python
import contextlib
import math
import numpy as np
import concourse.bass as bass
import concourse.mybir as mybir
from concourse.bass_utils import run_bass_kernel_spmd

F32 = mybir.dt.float32
BF16 = mybir.dt.bfloat16
I32 = mybir.dt.int32
AF = mybir.ActivationFunctionType
ALU = mybir.AluOpType

NCORES = 8
D = 1024
DC = 8
DFF = 2816
FC = 22
T = 512
SEQ = 4096
TPS = SEQ // T
NSEQ = 2
NT = TPS * NSEQ
TF = 128
NF = T // TF
EPS = 1e-6
RS = 3
RB = 2
NSMALL = 55
NBIG = 16
WIN_PAIRS = [(4, 0), (5, 1), (6, 2), (7, 3), (8, 9), (10, 11)]


class Tok:
    __slots__ = ("w", "r", "name")

    def __init__(self, name=""):
        self.w = None
        self.r = {}
        self.name = name


class Prog:
    ENG = ("pe", "act", "dve", "pool", "sp")
    NDSEM = 10

    def __init__(self, nc, stack):
        self.nc = nc
        self.q = {e: [] for e in self.ENG}
        self.cnt = {e: 0 for e in self.ENG}
        self.seen = {e: {} for e in self.ENG}
        self.sems = {}
        for e in self.ENG:
            self.sems[e] = stack.enter_context(nc.semaphore("s_" + e))
        for qn in ("sp", "pool"):
            for i in range(self.NDSEM):
                self.sems[("d" + qn, i)] = stack.enter_context(nc.semaphore("d%s_%d" % (qn, i)))
        self.dma_i = {"sp": 0, "pool": 0}

    def _deps(self, eng, reads, writes):
        deps = {}

        def add(t):
            if t is None:
                return
            k, v = t
            if deps.get(k, 0) < v:
                deps[k] = v

        for b in reads:
            add(b.w)
        for b in writes:
            if b.w is not None and not (b.w[0] == eng == "pe"):
                add(b.w)
            for k, v in b.r.items():
                if not (k == eng == "pe"):
                    add((k, v))
        return deps

    def _commit(self, eng, deps, tok, reads, writes):
        waits = []
        seen = self.seen[eng]
        for k, v in deps.items():
            if seen.get(k, 0) < v:
                seen[k] = v
                waits.append((k, v))
        k, v = tok
        for b in reads:
            if b.r.get(k, 0) < v:
                b.r[k] = v
        for b in writes:
            b.w = tok
            b.r = {}
        return waits

    def op(self, eng, fn, reads=(), writes=()):
        deps = self._deps(eng, reads, writes)
        self.cnt[eng] += 1
        tok = (eng, self.cnt[eng])
        waits = self._commit(eng, deps, tok, reads, writes)
        self.q[eng].append((waits, fn, (eng, 1)))
        return tok

    def dma(self, queue, fn, reads=(), writes=()):
        deps = self._deps(queue, reads, writes)
        i = self.dma_i[queue]
        self.dma_i[queue] += 1
        key = ("d" + queue, i % self.NDSEM)
        val = 16 * (i // self.NDSEM + 1)
        if val > 16 and deps.get(key, 0) < val - 16:
            deps[key] = val - 16
        tok = (key, val)
        waits = self._commit(queue, deps, tok, reads, writes)
        self.q[queue].append((waits, fn, (key, 16)))
        return tok

    def wait_all(self, eng, toks):
        deps = {}
        for k, v in toks:
            if deps.get(k, 0) < v:
                deps[k] = v
        waits = []
        for k, v in deps.items():
            if self.seen[eng].get(k, 0) < v:
                self.seen[eng][k] = v
                waits.append((k, v))
        self.q[eng].append((waits, None, None))

    def emit(self, block):
        sems = self.sems

        def replay(name):
            def body(e):
                for waits, fn, inc in self.q[name]:
                    for k, v in waits:
                        e.wait_ge(sems[k], v)
                    if fn is None:
                        continue
                    ins = fn(e)
                    ins.then_inc(sems[inc[0]], inc[1])
            return body

        block.tensor(replay("pe"))
        block.scalar(replay("act"))
        block.vector(replay("dve"))
        block.gpsimd(replay("pool"))
        block.sync(replay("sp"))


def toks(n, name):
    return [Tok("%s%d" % (name, i)) for i in range(n)]


def build_nc(ntiles=NT, stages=None):
    nc = bass.Bass("TRN2", target_bir_lowering=False)
    NTOK = NT * T
    xT = nc.dram_tensor("xT", [D, NTOK], F32, kind="ExternalInput").ap()
    oT = nc.dram_tensor("oT", [D, NTOK], F32, kind="ExternalOutput").ap()
    wsm_d = nc.dram_tensor("wsmall", [NSMALL, 128, 2048], F32, kind="ExternalInput").ap()
    wbg_d = nc.dram_tensor("wbig", [NBIG, 128, DFF], F32, kind="ExternalInput").ap()
    gains_d = nc.dram_tensor("gains", [128, 4 * DC], F32, kind="ExternalInput").ap()
    vecs_d = nc.dram_tensor("vecs", [128, 8 * 4], F32, kind="ExternalInput").ap()
    cw_d = nc.dram_tensor("cw", [128, 4 * 31], F32, kind="ExternalInput").ap()
    ident_d = nc.dram_tensor("ident", [128, 32], F32, kind="ExternalInput").ap()
    aq_d = nc.dram_tensor("aq", [128, 3 * 16], F32, kind="ExternalInput").ap()
    sc_d = nc.dram_tensor("sc", [128, 5 * 512], F32, kind="ExternalInput").ap()
    cq_d = nc.dram_tensor("cq", [128, 2 * 512], F32, kind="ExternalInput").ap()
    idf_d = nc.dram_tensor("idf", [128, 256], F32, kind="ExternalInput").ap()
    wsm_bf = nc.dram_tensor("wsm_bf", [NSMALL, 128, 2048], BF16, kind="Internal").ap()
    wbg_bf = nc.dram_tensor("wbg_bf", [NBIG, 128, DFF], BF16, kind="Internal").ap()

    with contextlib.ExitStack() as st:
        def sb(name, shape, dt):
            return st.enter_context(nc.sbuf_tensor(name, shape, dt))

        def ps(name):
            return st.enter_context(nc.psum_tensor(name, [128, 512], F32))

        xbs = [sb("xb%d" % i, [128, DC, T], F32) for i in range(3)]
        hb = sb("hb", [128, DC, T], BF16)
        sq = sb("sq", [128, 4, T], BF16)
        stt_ = sb("stt", [128, 3, T], F32)
        sta = sb("sta", [128, 2, T], F32)
        act = sb("act", [128, FC, T], BF16)
        ms = sb("ms", [128, 12, T], BF16)
        stmp2 = sb("stmp2", [128, 2, T], BF16)
        idf = sb("idf_s", [128, 2, 128], F32)
        stmp = sb("stmp", [128, 2, T], BF16)
        ws = sb("ws", [128, RS, 2048], BF16)
        wb = sb("wb", [128, RB, DFF], BF16)
        diag = sb("diag", [128, 4 * 31, 32], BF16)
        abuf = sb("abuf", [128, 4, 30 + T], BF16)
        co = sb("co", [128, 4, T], F32)
        u32 = sb("u32", [128, 4, T], F32)
        s5tb = sb("s5tb", [128, 2, T], BF16)
        s5v = sb("s5v", [128, 2, T], F32)
        s5w = sb("s5w", [128, 2, T], F32)
        s5x = sb("s5x", [128, 4, T], BF16)
        ctabb = sb("ctabb", [128, 16, TF], BF16)
        stabb = sb("stabb", [128, 16, TF], BF16)
        s5wb = sb("s5wb", [128, 2, T], BF16)
        rtab = sb("rtab", [128, 16, TF], F32)
        btabF = sb("btabF", [128, 2, 16 * 128], BF16)
        ctb = sb("ctb", [128, 2, 512], BF16)
        gains = sb("gains_s", [128, 4 * DC], F32)
        vecs = sb("vecs_s", [128, 32], F32)
        cw = sb("cw_s", [128, 4 * 31], F32)
        ident = sb("ident_s", [128, 32], F32)
        ones_bf = sb("ones_bf", [128, 128], BF16)
        aq = sb("aq_s", [128, 48], F32)
        qs = sb("qs", [128, 12, 16], F32)
        carry = sb("carry", [128, 4, 8], F32)
        ctmp = sb("ctmp", [128, 4, 4], F32)
        btab = act[:, 0:4, :].rearrange("p a b -> p (a b)").bitcast(F32).rearrange("p (a b) -> p a b", a=2)
        xb0 = xbs[0]
        hbm = ms[:, 4:12, :]
        ctab = xbs[1][:, 0:4, :].rearrange("p a (b c) -> p (a b) c", c=TF)
        stab = xbs[1][:, 4:8, :].rearrange("p a (b c) -> p (a b) c", c=TF)
        pA = [ps("pA%d" % i) for i in range(4)]
        pB = [ps("pB%d" % i) for i in range(2)]
        pS = ps("pS")
        pX = ps("pX")

        P = Prog(nc, st)
        block = st.enter_context(nc.Block())

        t_xs = [toks(DC, "x%d_" % i) for i in range(3)]
        t_h = toks(DC, "h")
        t_sq = toks(4, "sq")
        t_st = toks(3, "st")
        t_sta = toks(2, "sta")
        t_act = toks(FC, "act")
        t_ms = toks(12, "ms")
        t_hm = t_ms[4:12]
        t_stmp2 = toks(2, "stmp2")
        t_stmp = toks(2, "stmp")
        t_ws2 = [toks(2, "ws%d_" % i) for i in range(RS)]
        t_wb2 = [toks(2, "wb%d_" % i) for i in range(RB)]
        t_diag = Tok("diag")
        t_abuf = toks(4, "abuf")
        t_halo = toks(4, "halo")
        t_co = toks(4, "co")
        t_u32 = toks(4, "u32")
        t_s5t = toks(4, "s5t")
        t_s5w = toks(2, "s5w")
        t_s5wb = toks(2, "s5wb")
        t_s5x = toks(4, "s5x")
        t_tab = Tok("tab")
        t_const = Tok("const")
        t_pA = toks(4, "pA")
        t_pB = toks(2, "pB")
        t_pS = Tok("pS")
        t_pX = Tok("pX")
        t_carry = toks(4, "carry")
        t_ctmp = Tok("ctmp")
        t_setup = Tok("setup")

        def g_ap(n, c):
            return gains[:, n * DC + c:n * DC + c + 1]

        def vec_ap(n, c):
            return vecs[:, n * 4 + c:n * 4 + c + 1]

        LEVEL = stages if stages is not None else 9

        for (dst, src) in ((gains, gains_d), (vecs, vecs_d), (cw, cw_d), (ident, ident_d), (aq, aq_d)):
            P.dma("sp", lambda e, dst=dst, src=src: e.dma_start(out=dst[:], in_=src), writes=[t_setup])
        P.dma("sp", lambda e: e.dma_start(out=idf[:].rearrange("p a b -> p (a b)"), in_=idf_d), writes=[t_setup])
        P.dma("sp", lambda e: e.dma_start(out=xb0[:, 0:5, :].rearrange("p a b -> p (a b)"), in_=sc_d), writes=[t_setup])
        P.dma("sp", lambda e: e.dma_start(out=xb0[:, 5:7, :].rearrange("p a b -> p (a b)"), in_=cq_d), writes=[t_setup])
        P.op("dve", lambda e: e.memset(ones_bf[:], 1.0), writes=[t_const])

        def S(fn, eng="dve"):
            P.op(eng, fn, reads=[t_setup, t_const], writes=[t_setup])

        for c in range(4):
            for k in range(31):
                S(lambda e, c=c, k=k: e.tensor_scalar(
                    out=diag[:, c * 31 + k, :], in0=ident[:], scalar1=cw[:, c * 31 + k:c * 31 + k + 1],
                    scalar2=None, op0=ALU.mult))

        PI = math.pi

        def lam_setup(are, aim, ldt, w):
            dt_, zr, th, r_, tmp, ang, sn, cs_, k_i = [w(i) for i in range(9)]
            S(lambda e: e.activation(out=dt_, in_=ldt, func=AF.Exp), "act")
            S(lambda e: e.tensor_tensor(out=zr, in0=are, in1=dt_, op=ALU.mult))
            S(lambda e: e.tensor_tensor(out=th, in0=aim, in1=dt_, op=ALU.mult))
            S(lambda e: e.activation(out=r_, in_=zr, func=AF.Exp), "act")
            for which, out_ in ((0, sn), (1, cs_)):
                off = 0.0 if which == 0 else PI / 2
                S(lambda e, off=off: e.tensor_scalar(out=ang, in0=th, scalar1=off, scalar2=None, op0=ALU.add))
                S(lambda e: e.tensor_scalar(out=tmp, in0=ang, scalar1=1.0 / (2 * PI), scalar2=None, op0=ALU.mult))
                S(lambda e: e.tensor_copy(out=k_i.bitcast(I32), in_=tmp))
                S(lambda e: e.tensor_copy(out=tmp, in_=k_i.bitcast(I32)))
                S(lambda e: e.scalar_tensor_tensor(out=ang, in0=tmp, scalar=-2 * PI, in1=ang, op0=ALU.mult, op1=ALU.add))
                S(lambda e: e.tensor_scalar(out=tmp, in0=ang, scalar1=-PI, scalar2=2 * PI, op0=ALU.is_lt, op1=ALU.mult))
                S(lambda e: e.tensor_tensor(out=ang, in0=ang, in1=tmp, op=ALU.add))
                S(lambda e: e.tensor_scalar(out=tmp, in0=ang, scalar1=PI, scalar2=-2 * PI, op0=ALU.is_gt, op1=ALU.mult))
                S(lambda e: e.tensor_tensor(out=ang, in0=ang, in1=tmp, op=ALU.add))
                S(lambda e: e.tensor_scalar(out=ang, in0=ang, scalar1=-3.1415925, scalar2=3.1415925, op0=ALU.max, op1=ALU.min))
                S(lambda e, out_=out_: e.activation(out=out_, in_=ang, func=AF.Sin), "act")
            return dict(r=r_, cos=cs_, sin=sn, dt=dt_)

        lq = lam_setup(aq[:, 0:16], aq[:, 16:32], aq[:, 32:48], lambda i: qs[:, i, :])
        S(lambda e: e.memset(ctab[:, :, 0:1], 1.0))
        S(lambda e: e.memset(stab[:, :, 0:1], 0.0))
        S(lambda e: e.tensor_copy(out=ctab[:, :, 1], in_=lq["cos"]))
        S(lambda e: e.tensor_copy(out=stab[:, :, 1], in_=lq["sin"]))
        k = 1
        while k < TF:
            for gc in range(16):
                er = ctab[:, gc, k:k + 1]
                ei = stab[:, gc, k:k + 1]
                n = min(k, TF - k)
                if n > 1:
                    src_c = ctab[:, gc, 1:n]
                    src_s = stab[:, gc, 1:n]
                    dst_c = ctab[:, gc, k + 1:k + n]
                    dst_s = stab[:, gc, k + 1:k + n]
                    tmpv = co[:, 0, 0:n - 1]
                    S(lambda e, src_s=src_s, ei=ei, tmpv=tmpv: e.tensor_scalar(out=tmpv, in0=src_s, scalar1=ei, scalar2=None, op0=ALU.mult))
                    S(lambda e, src_c=src_c, er=er, tmpv=tmpv, dst_c=dst_c: e.scalar_tensor_tensor(out=dst_c, in0=src_c, scalar=er, in1=tmpv, op0=ALU.mult, op1=ALU.subtract))
                    S(lambda e, src_s=src_s, er=er, tmpv=tmpv: e.tensor_scalar(out=tmpv, in0=src_s, scalar1=er, scalar2=None, op0=ALU.mult))
                    S(lambda e, src_c=src_c, ei=ei, tmpv=tmpv, dst_s=dst_s: e.scalar_tensor_tensor(out=dst_s, in0=src_c, scalar=ei, in1=tmpv, op0=ALU.mult, op1=ALU.add))
            if 2 * k < TF:
                for gc in range(16):
                    er = ctab[:, gc, k:k + 1]
                    ei = stab[:, gc, k:k + 1]
                    tmpv = co[:, 0, 0:1]
                    S(lambda e, ei=ei, tmpv=tmpv: e.tensor_scalar(out=tmpv, in0=ei, scalar1=ei, scalar2=None, op0=ALU.mult))
                    S(lambda e, er=er, tmpv=tmpv, gc=gc, k=k: e.scalar_tensor_tensor(out=ctab[:, gc, 2 * k:2 * k + 1], in0=er, scalar=er, in1=tmpv, op0=ALU.mult, op1=ALU.subtract))
                    S(lambda e, er=er, ei=ei, gc=gc, k=k: e.tensor_scalar(out=stab[:, gc, 2 * k:2 * k + 1], in0=er, scalar1=ei, scalar2=2.0, op0=ALU.mult, op1=ALU.mult))
            k *= 2
        S(lambda e: e.tensor_tensor(out=qs[:, 9, :], in0=ctab[:, :, TF - 1], in1=lq["cos"], op=ALU.mult))
        S(lambda e: e.tensor_tensor(out=qs[:, 11, :], in0=stab[:, :, TF - 1], in1=lq["sin"], op=ALU.mult))
        S(lambda e: e.tensor_tensor(out=qs[:, 9, :], in0=qs[:, 9, :], in1=qs[:, 11, :], op=ALU.subtract))
        S(lambda e: e.tensor_tensor(out=qs[:, 10, :], in0=ctab[:, :, TF - 1], in1=lq["sin"], op=ALU.mult))
        S(lambda e: e.tensor_tensor(out=qs[:, 11, :], in0=stab[:, :, TF - 1], in1=lq["cos"], op=ALU.mult))
        S(lambda e: e.tensor_tensor(out=qs[:, 10, :], in0=qs[:, 10, :], in1=qs[:, 11, :], op=ALU.add))
        S(lambda e: e.tensor_tensor(out=qs[:, 9, :], in0=qs[:, 9, :], in1=lq["r"], op=ALU.mult))
        S(lambda e: e.tensor_tensor(out=qs[:, 10, :], in0=qs[:, 10, :], in1=lq["r"], op=ALU.mult))
        rEr = qs[:, 9, :]
        rEi = qs[:, 10, :]
        S(lambda e: e.tensor_copy(out=ctabb[:], in_=ctab))
        S(lambda e: e.tensor_copy(out=stabb[:], in_=stab))
        S(lambda e: e.memset(rtab[:], 1.0))
        for gc in range(16):
            S(lambda e, gc=gc: e.tensor_scalar(out=rtab[:, gc, :], in0=rtab[:, gc, :], scalar1=qs[:, 3, gc:gc + 1], scalar2=None, op0=ALU.mult))
        S(lambda e: e.memset(rtab[:, :, 0:1], 0.0))

        def cv(i):
            return co[:, i, :] if i < 4 else (u32[:, i - 4, :] if i < 8 else s5v[:, 0, :])
        lc = lam_setup(xb0[:, 0, :], xb0[:, 1, :], xb0[:, 2, :], cv)
        lr_, li_ = xb0[:, 0, :], xb0[:, 1, :]
        br_, bi_ = xb0[:, 3, :], xb0[:, 4, :]
        ar_, ai_, den, cr_, ci_, tq = cv(0), cv(1), cv(2), cv(4), cv(5), s5v[:, 1, :]
        S(lambda e: e.tensor_tensor(out=ar_, in0=lc["r"], in1=lc["cos"], op=ALU.mult))
        S(lambda e: e.tensor_tensor(out=ai_, in0=lc["r"], in1=lc["sin"], op=ALU.mult))
        S(lambda e: e.tensor_scalar(out=ar_, in0=ar_, scalar1=-1.0, scalar2=None, op0=ALU.add))
        S(lambda e: e.tensor_tensor(out=den, in0=lr_, in1=lr_, op=ALU.mult))
        S(lambda e: e.tensor_tensor(out=tq, in0=li_, in1=li_, op=ALU.mult))
        S(lambda e: e.tensor_tensor(out=den, in0=den, in1=tq, op=ALU.add))
        S(lambda e: e.reciprocal(out=den, in_=den))
        S(lambda e: e.tensor_tensor(out=cr_, in0=ar_, in1=lr_, op=ALU.mult))
        S(lambda e: e.tensor_tensor(out=tq, in0=ai_, in1=li_, op=ALU.mult))
        S(lambda e: e.tensor_tensor(out=cr_, in0=cr_, in1=tq, op=ALU.add))
        S(lambda e: e.tensor_tensor(out=cr_, in0=cr_, in1=den, op=ALU.mult))
        S(lambda e: e.tensor_tensor(out=ci_, in0=ai_, in1=lr_, op=ALU.mult))
        S(lambda e: e.tensor_tensor(out=tq, in0=ar_, in1=li_, op=ALU.mult))
        S(lambda e: e.tensor_tensor(out=ci_, in0=ci_, in1=tq, op=ALU.subtract))
        S(lambda e: e.tensor_tensor(out=ci_, in0=ci_, in1=den, op=ALU.mult))
        S(lambda e: e.tensor_tensor(out=tq, in0=ci_, in1=bi_, op=ALU.mult))
        S(lambda e: e.tensor_tensor(out=den, in0=cr_, in1=br_, op=ALU.mult))
        S(lambda e: e.tensor_tensor(out=btab[:, 0, :], in0=den, in1=tq, op=ALU.subtract))
        S(lambda e: e.tensor_tensor(out=tq, in0=ci_, in1=br_, op=ALU.mult))
        S(lambda e: e.tensor_tensor(out=den, in0=cr_, in1=bi_, op=ALU.mult))
        S(lambda e: e.tensor_tensor(out=btab[:, 1, :], in0=den, in1=tq, op=ALU.add))
        for ri in range(2):
            for gc in range(16):
                uc_, j_ = divmod(gc, 4)
                S(lambda e, ri=ri, gc=gc, uc_=uc_, j_=j_: e.tensor_scalar(
                    out=btabF[:, ri, gc * 128:(gc + 1) * 128], in0=btab[:, ri, uc_ * 128:(uc_ + 1) * 128],
                    scalar1=vecs[:, 28 + j_:29 + j_], scalar2=None, op0=ALU.mult))
        S(lambda e: e.tensor_copy(out=ctb[:, 0, :], in_=xb0[:, 5, :]))
        S(lambda e: e.tensor_scalar(out=ctb[:, 1, :], in0=xb0[:, 6, :], scalar1=-1.0, scalar2=None, op0=ALU.mult))
        P.op("dve", lambda e: e.memset(ctmp[:], 0.0), reads=[t_setup],
             writes=[t_tab, t_diag, t_ctmp] + t_xs[0] + t_xs[1] + t_st + t_co + t_u32 + t_s5t + t_act[0:4])

        wstate = {"ns": 0, "nb": 0, "cs": 0, "cb": 0}
        total_small = ntiles * NSMALL
        total_big = ntiles * NBIG
        small_seq = []
        big_seq = []

        scr_s = {}
        scr_b = {}

        def issue_small(piece):
            n = wstate["ns"]
            slot = n % RS
            if piece not in scr_s:
                for hh in range(2):
                    P.dma("pool", lambda e, hh=hh, slot=slot, piece=piece: e.dma_start(
                        out=ws[:, slot, hh * 1024:(hh + 1) * 1024], in_=wsm_d[piece][:, hh * 1024:(hh + 1) * 1024]),
                        writes=[t_ws2[slot][hh]])
                tk = Tok("scr_s%d" % piece)
                P.dma("sp", lambda e, slot=slot, piece=piece: e.dma_start(out=wsm_bf[piece], in_=ws[:, slot, :]),
                      reads=t_ws2[slot], writes=[tk])
                scr_s[piece] = tk
            else:
                P.dma("sp", lambda e, slot=slot, piece=piece: e.dma_start(out=ws[:, slot, :], in_=wsm_bf[piece]),
                      reads=[scr_s[piece]], writes=t_ws2[slot])
            wstate["ns"] += 1

        def issue_big(piece):
            n = wstate["nb"]
            slot = n % RB
            HB = DFF // 2
            if piece not in scr_b:
                for hh in range(2):
                    P.dma("pool", lambda e, hh=hh, slot=slot, piece=piece: e.dma_start(
                        out=wb[:, slot, hh * HB:(hh + 1) * HB], in_=wbg_d[piece][:, hh * HB:(hh + 1) * HB]),
                        writes=[t_wb2[slot][hh]])
                tk = Tok("scr_b%d" % piece)
                P.dma("sp", lambda e, slot=slot, piece=piece: e.dma_start(out=wbg_bf[piece], in_=wb[:, slot, :]),
                      reads=t_wb2[slot], writes=[tk])
                scr_b[piece] = tk
            else:
                P.dma("sp", lambda e, slot=slot, piece=piece: e.dma_start(out=wb[:, slot, :], in_=wbg_bf[piece]),
                      reads=[scr_b[piece]], writes=t_wb2[slot])
            wstate["nb"] += 1

        def need_small(piece):
            n = wstate["cs"]
            assert small_plan[n] == piece, (n, small_plan[n], piece)
            while wstate["ns"] < min(n + RS, len(small_plan)):
                issue_small(small_plan[wstate["ns"]])
            wstate["cs"] += 1
            return n % RS

        def need_big(piece):
            n = wstate["cb"]
            assert big_plan[n] == piece, (n, big_plan[n], piece)
            while wstate["nb"] < min(n + RB, len(big_plan)):
                issue_big(big_plan[wstate["nb"]])
            wstate["cb"] += 1
            return n % RB

        def rms_stats(src_fn, src_toks, nchunks, dim, dst, t_dst):
            for c in range(nchunks):
                s = c % 4
                P.op("act", lambda e, c=c, s=s: e.activation(out=sq[:, s, :], in_=src_fn(c), func=AF.Square),
                     reads=[src_toks[c]], writes=[t_sq[s]])
                P.op("pe", lambda e, c=c, s=s: e.matmul(pS[:], lhsT=ones_bf[:], rhs=sq[:, s, :],
                                                        start=(c == 0), stop=(c == nchunks - 1)),
                     reads=[t_sq[s], t_const], writes=[t_pS])
            P.op("act", lambda e: e.activation(out=dst[:, 0, :], in_=pS[:], func=AF.Ln, scale=1.0 / dim, bias=EPS),
                 reads=[t_pS], writes=[t_dst[0]])
            P.op("act", lambda e: e.activation(out=dst[:, 1, :], in_=dst[:, 0, :], func=AF.Exp, scale=-0.5),
                 reads=[t_dst[0]], writes=[t_dst[1]])

        def norm_to_h(gi, xb, t_x, dst, t_dst, hdst=None, t_hdst=None):
            hdst = hb if hdst is None else hdst
            t_hdst = t_h if t_hdst is None else t_hdst
            rms_stats(lambda c: xb[:, c, :], t_x, DC, D, dst, t_dst)
            for c in range(DC):
                P.op("dve", lambda e, c=c: e.scalar_tensor_tensor(
                    out=hdst[:, c, :], in0=xb[:, c, :], scalar=g_ap(gi, c), in1=dst[:, 1, :],
                    op0=ALU.mult, op1=ALU.mult), reads=[t_x[c], t_dst[1]], writes=[t_hdst[c]])

        def ffn_A(fc, sbase):
            slot = need_small(sbase + fc)
            b = 2 * (fc % 2)

            def mm(e, slot=slot, b=b):
                for kc in range(DC):
                    e.matmul(pA[b][:], lhsT=ws[:, slot, kc * 128:(kc + 1) * 128], rhs=hb[:, kc, :],
                             start=(kc == 0), stop=(kc == DC - 1))
                for kc in range(DC):
                    ins = e.matmul(pA[b + 1][:], lhsT=ws[:, slot, 1024 + kc * 128:1024 + (kc + 1) * 128],
                                   rhs=hb[:, kc, :], start=(kc == 0), stop=(kc == DC - 1))
                return ins
            P.op("pe", mm, reads=t_ws2[slot] + t_h, writes=[t_pA[b], t_pA[b + 1]])
            s = fc % 2
            P.op("act", lambda e, s=s, b=b: e.activation(out=stmp[:, s, :], in_=pA[b][:], func=AF.Silu),
                 reads=[t_pA[b]], writes=[t_stmp[s]])
            P.op("act", lambda e, s=s, b=b: e.activation(out=stmp2[:, s, :], in_=pA[b + 1][:], func=AF.Copy),
                 reads=[t_pA[b + 1]], writes=[t_stmp2[s]])
            P.op("pool", lambda e, s=s, fc=fc: e.tensor_tensor(out=act[:, fc, :], in0=stmp2[:, s, :], in1=stmp[:, s, :], op=ALU.mult),
                 reads=[t_stmp2[s], t_stmp[s]], writes=[t_act[fc]])

        def ffn_B(dc, bbase, xb, t_x):
            slot = need_big(bbase + dc)
            b = 2 * (dc % 2)

            def mm2(e, slot=slot, b=b, dc=dc):
                e.matmul(pA[b][:], lhsT=idf[:, 1, :], rhs=xb[:, dc, :], start=True, stop=False)
                for fc in range(FC):
                    ins = e.matmul(pA[b][:], lhsT=wb[:, slot, fc * 128:(fc + 1) * 128], rhs=act[:, fc, :],
                                   start=False, stop=(fc == FC - 1))
                return ins
            P.op("pe", mm2, reads=t_wb2[slot] + t_act + [t_x[dc], t_tab], writes=[t_pA[b]])
            P.op("act", lambda e, dc=dc, b=b: e.activation(out=xb[:, dc, :], in_=pA[b][:], func=AF.Copy, scale=0.5),
                 reads=[t_pA[b]], writes=[t_x[dc]])

        def ffn_gen(gi, xb, t_x, sbase, bbase):
            norm_to_h(gi, xb, t_x, sta, t_sta)
            yield
            for fc in range(FC):
                ffn_A(fc, sbase)
                yield
            for dc in range(DC):
                ffn_B(dc, bbase, xb, t_x)
                yield

        def final_store(ti, xb, t_x):
            rms_stats(lambda c: xb[:, c, :], t_x, DC, D, sta, t_sta)
            for c in range(DC):
                P.op("dve", lambda e, c=c: e.scalar_tensor_tensor(
                    out=xb[:, c, :], in0=xb[:, c, :], scalar=g_ap(3, c), in1=sta[:, 1, :],
                    op0=ALU.mult, op1=ALU.mult), reads=[t_x[c], t_sta[1]], writes=[t_x[c]])
            cols = slice(ti * T, (ti + 1) * T)
            out_toks.append(P.dma("sp", lambda e, cols=cols, xb=xb: e.dma_start(out=oT_v[:, :, cols], in_=xb[:]), reads=t_x))
            if ti + 3 < ntiles:
                load_x(ti + 3)

        def ffn_pair_gen(t2, t1):
            xb2, tx2 = xbs[t2 % 3], t_xs[t2 % 3]
            xb1, tx1 = xbs[t1 % 3], t_xs[t1 % 3]
            norm_to_h(2, xb2, tx2, sta, t_sta)
            yield
            for fc in range(FC):
                ffn_A(fc, 33)
                yield
            norm_to_h(0, xb1, tx1, sta, t_sta)
            yield
            for dc in range(DC):
                ffn_B(dc, 8, xb2, tx2)
                yield
            for fc in range(2):
                ffn_A(fc, 0)
                yield
            final_store(t2, xb2, tx2)
            yield
            for fc in range(2, FC):
                ffn_A(fc, 0)
                yield
            for dc in range(DC):
                ffn_B(dc, 0, xb1, tx1)
                yield

        UBF = (0, 1, 2, 3)
        GLB = UBF
        MIX = tuple(range(4, 12))

        def conv_gen(ti):
            seq_end = (ti % TPS == TPS - 1)
            for c in range(4):
                def mmc(e, c=c):
                    for k in range(31):
                        for i in range(4):
                            ins = e.matmul(pS[32 * i:32 * i + 32, :], lhsT=diag[32 * i:32 * i + 32, c * 31 + k, :],
                                           rhs=abuf[32 * i:32 * i + 32, c, k:k + T],
                                           start=(k == 0), stop=(k == 30), tile_position=(32 * i, 32 * i))
                    return ins
                P.op("pe", mmc, reads=[t_diag, t_abuf[c], t_halo[c]], writes=[t_pS])
                P.op("act", lambda e, c=c: e.activation(out=co[:, c, :], in_=pS[:], func=AF.Identity, bias=vec_ap(0, c)),
                     reads=[t_pS, t_tab], writes=[t_co[c]])
                if not seq_end:
                    P.op("pool", lambda e, c=c: e.tensor_copy(out=abuf[:, c, 0:30], in_=abuf[:, c, T:T + 30]),
                         reads=[t_abuf[c]], writes=[t_halo[c]])
                P.op("act", lambda e, c=c: e.activation(out=sq[:, c, :], in_=co[:, c, :], func=AF.Square),
                     reads=[t_co[c]], writes=[t_sq[c]])
                P.op("act", lambda e, c=c: e.activation(out=ms[:, MIX[c], :], in_=co[:, c, :], func=AF.Copy),
                     reads=[t_co[c]], writes=[t_ms[MIX[c]]])
                yield
            for c in range(4):
                P.op("pe", lambda e, c=c: e.matmul(pS[:], lhsT=ones_bf[:], rhs=ms[:, MIX[c], :], start=(c == 0), stop=(c == 3)),
                     reads=[t_ms[MIX[c]], t_const], writes=[t_pS])
            P.op("act", lambda e: e.activation(out=stt_[:, 0, :], in_=pS[:], func=AF.Copy, scale=1.0 / 512),
                 reads=[t_pS], writes=[t_st[0]])
            P.op("dve", lambda e: e.tensor_tensor(out=stt_[:, 2, :], in0=stt_[:, 0, :], in1=stt_[:, 0, :], op=ALU.mult),
                 reads=[t_st[0]], writes=[t_st[2]])
            yield
            for c in range(4):
                P.op("pe", lambda e, c=c: e.matmul(pS[:], lhsT=ones_bf[:], rhs=sq[:, c, :], start=(c == 0), stop=(c == 3)),
                     reads=[t_sq[c], t_const], writes=[t_pS])
            P.op("dve", lambda e: e.scalar_tensor_tensor(out=stt_[:, 2, :], in0=pS[:], scalar=1.0 / 512, in1=stt_[:, 2, :],
                                                         op0=ALU.mult, op1=ALU.subtract),
                 reads=[t_pS, t_st[2]], writes=[t_st[2]])
            P.op("act", lambda e: e.activation(out=stt_[:, 2, :], in_=stt_[:, 2, :], func=AF.Ln, bias=EPS),
                 reads=[t_st[2]], writes=[t_st[2]])
            P.op("act", lambda e: e.activation(out=stt_[:, 1, :], in_=stt_[:, 2, :], func=AF.Exp, scale=-0.5),
                 reads=[t_st[2]], writes=[t_st[1]])
            P.op("dve", lambda e: e.tensor_tensor(out=stt_[:, 2, :], in0=stt_[:, 0, :], in1=stt_[:, 1, :], op=ALU.mult),
                 reads=[t_st[0], t_st[1], t_st[2]], writes=[t_st[2]])
            yield
            for c in range(4):
                P.op("dve", lambda e, c=c: e.tensor_tensor(out=co[:, c, :], in0=co[:, c, :], in1=stt_[:, 1, :], op=ALU.mult),
                     reads=[t_co[c], t_st[1]], writes=[t_co[c]])
                P.op("dve", lambda e, c=c: e.tensor_tensor(out=co[:, c, :], in0=co[:, c, :], in1=stt_[:, 2, :], op=ALU.subtract),
                     reads=[t_co[c], t_st[2]], writes=[t_co[c]])
                P.op("act", lambda e, c=c: e.activation(out=co[:, c, :], in_=co[:, c, :], func=AF.Silu,
                                                        scale=vec_ap(1, c), bias=vec_ap(2, c)),
                     reads=[t_co[c], t_tab], writes=[t_co[c]])
                yield
            rms_stats(lambda c: co[:, c, :], t_co, 4, 512, stt_, t_st)
            yield
            for c in range(4):
                P.op("dve", lambda e, c=c: e.scalar_tensor_tensor(
                    out=ms[:, MIX[c], :], in0=co[:, c, :], scalar=vec_ap(3, c), in1=stt_[:, 1, :],
                    op0=ALU.mult, op1=ALU.mult), reads=[t_co[c], t_st[1], t_tab], writes=[t_ms[MIX[c]]])

        def mixer_gen(ti, xb, t_x):
            seq_start = (ti % TPS == 0)
            seq_end = (ti % TPS == TPS - 1)
            norm_to_h(1, xb, t_x, stt_, t_st, hbm, t_hm)
            if seq_start:
                for c in range(4):
                    P.op("dve", lambda e, c=c: e.memset(abuf[:, c, 0:30], 0.0), writes=[t_halo[c]])
            yield
            for m, pair in enumerate(WIN_PAIRS):
                slot = need_small(22 + m)
                for o2, oc in enumerate(pair):
                    b = o2

                    def mm(e, slot=slot, o2=o2, b=b):
                        for kc in range(DC):
                            ins = e.matmul(pB[b][:], lhsT=ws[:, slot, (o2 * 8 + kc) * 128:(o2 * 8 + kc + 1) * 128],
                                           rhs=hbm[:, kc, :], start=(kc == 0), stop=(kc == DC - 1))
                        return ins
                    P.op("pe", mm, reads=t_ws2[slot] + t_hm, writes=[t_pB[b]])
                    if 4 <= oc < 8:
                        c = oc - 4
                        P.op("act", lambda e, b=b, c=c: e.activation(out=s5x[:, c % 2, :], in_=pB[b][:], func=AF.Sigmoid),
                             reads=[t_pB[b]], writes=[t_s5x[c % 2]])
                    elif oc < 4:
                        c = oc
                        P.op("dve", lambda e, b=b, c=c: e.tensor_tensor(out=abuf[:, c, 30:30 + T], in0=pB[b][:], in1=s5x[:, c % 2, :], op=ALU.mult),
                             reads=[t_pB[b], t_s5x[c % 2]], writes=[t_abuf[c]])
                    else:
                        c = oc - 8
                        P.op("act", lambda e, b=b, c=c: e.activation(out=u32[:, c, :], in_=pB[b][:], func=AF.Copy),
                             reads=[t_pB[b]], writes=[t_u32[c]])
                        P.op("act", lambda e, c=c: e.activation(out=ms[:, UBF[c], :], in_=u32[:, c, :], func=AF.Copy),
                             reads=[t_u32[c]], writes=[t_ms[UBF[c]]])
            yield

            cg = conv_gen(ti)
            steps = [(uc, f) for uc in range(4) for f in range(NF)]
            NS = len(steps)
            D_ = lambda fn, r, w: P.op("dve", fn, reads=r, writes=w)
            G_ = lambda fn, r, w: P.op("dve", fn, reads=r, writes=w)
            tA, tB, vr, vi = s5tb[:, 0, :], s5tb[:, 1, :], s5v[:, 0, :], s5v[:, 1, :]
            wr, wi = s5w[:, 0, :], s5w[:, 1, :]
            bre, bim = pB[0], pB[1]

            def tabs(uc):
                return (ctabb[:, 4 * uc:4 * uc + 4, :].rearrange("p a b -> p (a b)"),
                        stabb[:, 4 * uc:4 * uc + 4, :].rearrange("p a b -> p (a b)"),
                        rtab[:, 4 * uc:4 * uc + 4, :].rearrange("p a b -> p (a b)"))

            def mmb_op(si):
                uc, f = steps[si]
                fs = slice(f * TF, (f + 1) * TF)

                def mmb(e, uc=uc, fs=fs):
                    for ri in range(2):
                        for j in range(4):
                            gc = 4 * uc + j
                            ins = e.matmul(pB[ri][:, j * TF:(j + 1) * TF],
                                           lhsT=btabF[:, ri, gc * 128:(gc + 1) * 128],
                                           rhs=ms[:, UBF[uc], fs], start=True, stop=True)
                    return ins
                P.op("pe", mmb, reads=[t_tab, t_ms[UBF[uc]]], writes=[t_pB[0], t_pB[1]])

            def fwd(si):
                uc, f = steps[si]
                cT, sT, rT = tabs(uc)
                D_(lambda e: e.tensor_tensor(out=tA, in0=bre[:], in1=cT, op=ALU.mult), [t_pB[0], t_tab], [t_s5t[0]])
                D_(lambda e: e.tensor_tensor(out=tB, in0=bim[:], in1=sT, op=ALU.mult), [t_pB[1], t_tab], [t_s5t[1]])
                D_(lambda e: e.tensor_tensor(out=vr, in0=tA, in1=tB, op=ALU.add), [t_s5t[0], t_s5t[1]], [t_s5t[2]])
                D_(lambda e: e.tensor_tensor(out=tA, in0=bim[:], in1=cT, op=ALU.mult), [t_pB[1], t_tab], [t_s5t[0]])
                D_(lambda e: e.tensor_tensor(out=tB, in0=bre[:], in1=sT, op=ALU.mult), [t_pB[0], t_tab], [t_s5t[1]])
                D_(lambda e: e.tensor_tensor(out=vi, in0=tA, in1=tB, op=ALU.subtract), [t_s5t[0], t_s5t[1]], [t_s5t[3]])

            def scans(si):
                uc, f = steps[si]
                cT, sT, rT = tabs(uc)
                first = seq_start and f == 0
                if not first:
                    vr0 = s5v[:, 0, :].rearrange("p (a b) -> p a b", a=4)[:, :, 0]
                    vi0 = s5v[:, 1, :].rearrange("p (a b) -> p a b", a=4)[:, :, 0]
                    D_(lambda e: e.tensor_tensor(out=vr0, in0=vr0, in1=carry[:, uc, 0:4], op=ALU.add), [t_s5t[2], t_carry[uc]], [t_s5t[2]])
                    D_(lambda e: e.tensor_tensor(out=vi0, in0=vi0, in1=carry[:, uc, 4:8], op=ALU.add), [t_s5t[3], t_carry[uc]], [t_s5t[3]])
                D_(lambda e: e.tensor_tensor_scan(out=wr, data0=rT, data1=vr, initial=0.0, op0=ALU.mult, op1=ALU.add), [t_s5t[2], t_tab], [t_s5w[0]])
                D_(lambda e: e.tensor_tensor_scan(out=wi, data0=rT, data1=vi, initial=0.0, op0=ALU.mult, op1=ALU.add), [t_s5t[3], t_tab], [t_s5w[1]])
                P.op("act", lambda e: e.activation(out=s5wb[:, 0, :], in_=wr, func=AF.Copy), reads=[t_s5w[0]], writes=[t_s5wb[0]])
                P.op("act", lambda e: e.activation(out=s5wb[:, 1, :], in_=wi, func=AF.Copy), reads=[t_s5w[1]], writes=[t_s5wb[1]])
                last = seq_end and f == NF - 1
                if not last:
                    wr1 = s5w[:, 0, :].rearrange("p (a b) -> p a b", a=4)[:, :, TF - 1]
                    wi1 = s5w[:, 1, :].rearrange("p (a b) -> p a b", a=4)[:, :, TF - 1]
                    er4 = rEr[:, 4 * uc:4 * uc + 4]
                    ei4 = rEi[:, 4 * uc:4 * uc + 4]
                    D_(lambda e: e.tensor_tensor(out=ctmp[:, 0, :], in0=wr1, in1=er4, op=ALU.mult), [t_s5w[0], t_tab, t_ctmp], [t_ctmp])
                    D_(lambda e: e.tensor_tensor(out=ctmp[:, 1, :], in0=wi1, in1=ei4, op=ALU.mult), [t_s5w[1], t_tab, t_ctmp], [t_ctmp])
                    D_(lambda e: e.tensor_tensor(out=ctmp[:, 2, :], in0=wi1, in1=er4, op=ALU.mult), [t_s5w[1], t_tab, t_ctmp], [t_ctmp])
                    D_(lambda e: e.tensor_tensor(out=ctmp[:, 3, :], in0=wr1, in1=ei4, op=ALU.mult), [t_s5w[0], t_tab, t_ctmp], [t_ctmp])
                    D_(lambda e: e.tensor_tensor(out=carry[:, uc, 0:4], in0=ctmp[:, 0, :], in1=ctmp[:, 1, :], op=ALU.subtract), [t_ctmp], [t_carry[uc]])
                    D_(lambda e: e.tensor_tensor(out=carry[:, uc, 4:8], in0=ctmp[:, 2, :], in1=ctmp[:, 3, :], op=ALU.add), [t_ctmp], [t_carry[uc]])

            def combos(si):
                uc, f = steps[si]
                cT, sT, rT = tabs(uc)
                xs = 2 * (si % 2)
                xr, xi = s5x[:, xs, :], s5x[:, xs + 1, :]
                wrb, wib = s5wb[:, 0, :], s5wb[:, 1, :]
                D_(lambda e: e.tensor_tensor(out=tA, in0=wrb, in1=cT, op=ALU.mult), [t_s5wb[0], t_tab], [t_s5t[0]])
                D_(lambda e: e.tensor_tensor(out=tB, in0=wib, in1=sT, op=ALU.mult), [t_s5wb[1], t_tab], [t_s5t[1]])
                D_(lambda e: e.tensor_tensor(out=xr, in0=tA, in1=tB, op=ALU.subtract), [t_s5t[0], t_s5t[1]], [t_s5x[xs]])
                D_(lambda e: e.tensor_tensor(out=tA, in0=wib, in1=cT, op=ALU.mult), [t_s5wb[1], t_tab], [t_s5t[0]])
                D_(lambda e: e.tensor_tensor(out=tB, in0=wrb, in1=sT, op=ALU.mult), [t_s5wb[0], t_tab], [t_s5t[1]])
                D_(lambda e: e.tensor_tensor(out=xi, in0=tA, in1=tB, op=ALU.add), [t_s5t[0], t_s5t[1]], [t_s5x[xs + 1]])

            def mmy_op(si):
                uc, f = steps[si]
                xs = 2 * (si % 2)
                fs = slice(f * TF, (f + 1) * TF)

                def mmy(e):
                    for j in range(4):
                        for ri in range(2):
                            gc = 4 * uc + j
                            ins = e.matmul(pX[32 * j:32 * j + 32, fs],
                                           lhsT=ctb[:, ri, gc * 32:(gc + 1) * 32],
                                           rhs=s5x[:, xs + ri, j * TF:(j + 1) * TF],
                                           start=(ri == 0), stop=(ri == 1), tile_position=(0, 32 * j))
                    return ins
                P.op("pe", mmy, reads=[t_tab, t_s5x[xs], t_s5x[xs + 1]], writes=[t_pX])
                if f == NF - 1:
                    yv = u32[:, uc, :]
                    P.op("dve", lambda e: e.scalar_tensor_tensor(
                        out=yv, in0=yv, scalar=vec_ap(4, uc), in1=pX[:], op0=ALU.mult, op1=ALU.add),
                        reads=[t_u32[uc], t_pX, t_tab], writes=[t_u32[uc]])
                    s = uc % 2
                    tq_ = sta[:, s, :]
                    P.op("act", lambda e: e.activation(out=tq_, in_=yv, func=AF.Square),
                         reads=[t_u32[uc]], writes=[t_sta[s]])
                    P.op("dve", lambda e: e.tensor_scalar(out=tq_, in0=tq_, scalar1=0.044715, scalar2=1.0, op0=ALU.mult, op1=ALU.add),
                         reads=[t_sta[s]], writes=[t_sta[s]])
                    P.op("dve", lambda e: e.tensor_tensor(out=tq_, in0=tq_, in1=yv, op=ALU.mult),
                         reads=[t_sta[s], t_u32[uc]], writes=[t_sta[s]])
                    P.op("act", lambda e: e.activation(out=tq_, in_=tq_, func=AF.Sigmoid, scale=1.5957691216057308),
                         reads=[t_sta[s]], writes=[t_sta[s]])
                    P.op("dve", lambda e: e.tensor_tensor(out=yv, in0=yv, in1=tq_, op=ALU.mult),
                         reads=[t_sta[s], t_u32[uc]], writes=[t_u32[uc]])
                    P.op("act", lambda e: e.activation(out=ms[:, GLB[uc], :], in_=yv, func=AF.Copy),
                         reads=[t_u32[uc]], writes=[t_ms[GLB[uc]]])

            mmb_op(0)
            fwd(0)
            scans(0)
            for si in range(NS):
                combos(si)
                if si + 1 < NS:
                    mmb_op(si + 1)
                yield
                if si + 1 < NS:
                    fwd(si + 1)
                    scans(si + 1)
                try:
                    next(cg)
                except StopIteration:
                    pass
                yield
                if si >= 1:
                    mmy_op(si - 1)
            mmy_op(NS - 1)
            for _ in cg:
                pass
            slot = need_small(28)
            for oc in range(4):
                b = oc % 2

                def mmg(e, slot=slot, oc=oc, b=b):
                    for kc in range(4):
                        ins = e.matmul(pB[b][:], lhsT=ws[:, slot, (oc * 4 + kc) * 128:(oc * 4 + kc + 1) * 128],
                                       rhs=ms[:, GLB[kc], :], start=(kc == 0), stop=(kc == 3))
                    return ins
                P.op("pe", mmg, reads=t_ws2[slot] + [t_ms[g] for g in GLB], writes=[t_pB[b]])
                s = oc % 2
                tq_ = sta[:, s, :]
                P.op("act", lambda e, b=b, oc=oc, tq_=tq_: e.activation(out=tq_, in_=pB[b][:], func=AF.Sigmoid, bias=vec_ap(5, oc)),
                     reads=[t_pB[b], t_tab], writes=[t_sta[s]])
                P.op("dve", lambda e, oc=oc, tq_=tq_: e.tensor_tensor(out=u32[:, oc, :], in0=u32[:, oc, :], in1=tq_, op=ALU.mult),
                     reads=[t_sta[s], t_u32[oc]], writes=[t_u32[oc]])
            yield
            rms_stats(lambda c: u32[:, c, :], t_u32, 4, 512, stt_, t_st)
            for c in range(4):
                P.op("dve", lambda e, c=c: e.scalar_tensor_tensor(
                    out=ms[:, MIX[4 + c], :], in0=u32[:, c, :], scalar=vec_ap(6, c), in1=stt_[:, 1, :],
                    op0=ALU.mult, op1=ALU.mult), reads=[t_u32[c], t_st[1], t_tab], writes=[t_ms[MIX[4 + c]]])
            yield
            for m in range(4):
                slot = need_small(29 + m)
                for d2 in range(2):
                    dc = 2 * m + d2
                    b = dc % 2

                    def mmo(e, slot=slot, d2=d2, b=b, dc=dc):
                        e.matmul(pB[b][:], lhsT=idf[:, 0, :], rhs=xb[:, dc, :], start=True, stop=False)
                        for kc in range(8):
                            ins = e.matmul(pB[b][:], lhsT=ws[:, slot, (d2 * 8 + kc) * 128:(d2 * 8 + kc + 1) * 128],
                                           rhs=ms[:, MIX[kc], :], start=False, stop=(kc == 7))
                        return ins
                    P.op("pe", mmo, reads=t_ws2[slot] + [t_ms[g] for g in MIX] + [t_x[dc], t_tab], writes=[t_pB[b]])
                    P.op("act", lambda e, dc=dc, b=b: e.activation(out=xb[:, dc, :], in_=pB[b][:], func=AF.Copy),
                         reads=[t_pB[b]], writes=[t_x[dc]])
                yield

        small_plan, big_plan = [], []
        MIXP = list(range(22, 28)) + [28] + list(range(29, 33))
        small_plan += list(range(0, 22))
        big_plan += list(range(0, 8))
        for ti in range(ntiles):
            pass

        out_toks = []
        xT_v = xT.rearrange("(c p) t -> p c t", p=128)
        oT_v = oT.rearrange("(c p) t -> p c t", p=128)

        def load_x(ti):
            cols = slice(ti * T, (ti + 1) * T)
            b = ti % 3
            P.dma("sp", lambda e, cols=cols, b=b: e.dma_start(out=xbs[b][:], in_=xT_v[:, :, cols]), writes=t_xs[b])

        def finalize_gen(ti):
            xb, t_x = xbs[ti % 3], t_xs[ti % 3]
            for _ in ffn_gen(2, xb, t_x, 33, 8):
                yield
            final_store(ti, xb, t_x)
            yield

        def chain(*gens):
            for g in gens:
                if g is not None:
                    for _ in g:
                        yield

        def interleave(gm, ga, per):
            k = 0
            m_done = a_done = False
            while not (m_done and a_done):
                if not m_done:
                    try:
                        next(gm)
                    except StopIteration:
                        m_done = True
                n = per(k) if not m_done else 10 ** 9
                k += 1
                for _ in range(n):
                    if a_done:
                        break
                    try:
                        next(ga)
                    except StopIteration:
                        a_done = True

        def per_full(k):
            if k < 2:
                return 3
            if k < 2 + 32:
                return 1 if k % 2 == 0 else 2
            return 2

        def per_half(k):
            if k < 2:
                return 1
            if k < 2 + 32:
                return 1 if k % 2 == 1 else 0
            return 2

        def run_schedule(dry):
            load_x(0)
            for _ in ffn_gen(0, xbs[0], t_xs[0], 0, 0):
                pass
            if ntiles > 1:
                load_x(1)
            if ntiles > 2:
                load_x(2)
            for ti in range(ntiles):
                b = ti % 3
                has2 = ti >= 1
                has1 = ti + 1 < ntiles
                if has2 and has1:
                    ga = ffn_pair_gen(ti - 1, ti + 1)
                elif has2:
                    ga = finalize_gen(ti - 1)
                elif has1:
                    ga = ffn_gen(0, xbs[(ti + 1) % 3], t_xs[(ti + 1) % 3], 0, 0)
                else:
                    ga = None
                interleave(mixer_gen(ti, xbs[b], t_xs[b]), ga if ga is not None else iter(()), per_full if (has2 and has1) else per_half)
            for _ in finalize_gen(ntiles - 1):
                pass

        real_need_small, real_need_big = need_small, need_big
        real_P = P
        plan_s, plan_b = [], []

        class _Null:
            def op(self, *a, **k):
                return None

            def dma(self, *a, **k):
                return ("x", 0)
        P = _Null()

        def need_small(piece):
            plan_s.append(piece)
            return 0

        def need_big(piece):
            plan_b.append(piece)
            return 0
        run_schedule(True)
        small_plan, big_plan = plan_s, plan_b
        out_toks.clear()
        P = real_P
        need_small, need_big = real_need_small, real_need_big
        run_schedule(False)
        P.wait_all("sp", out_toks)
        P.emit(block)
    return nc


def _host_layout(inp):
    f = lambda a: np.ascontiguousarray(a, dtype=np.float32)
    L = 0
    small = np.empty((NSMALL, 128, 2048), np.float32)

    def w13(w1, w3):
        a = w1.reshape(8, 128, FC, 128).transpose(2, 1, 0, 3)
        b = w3.reshape(8, 128, FC, 128).transpose(2, 1, 0, 3)
        return np.stack([a, b], axis=2).reshape(FC, 128, 2048)
    small[0:22] = w13(inp["ffn1_w1"][L], inp["ffn1_w3"][L])
    win = inp["w_in"][L].reshape(8, 128, 12, 128)
    for m, pair in enumerate(WIN_PAIRS):
        small[22 + m] = np.stack([win[:, :, oc, :].transpose(1, 0, 2) for oc in pair], axis=1).reshape(128, 2048)
    glu = inp["ssm_glu_w"][L].reshape(4, 128, 4, 128)
    small[28] = glu.transpose(1, 2, 0, 3).reshape(128, 2048)
    wo = inp["w_out"][L].reshape(8, 128, 4, 2, 128)
    small[29:33] = wo.transpose(2, 1, 3, 0, 4).reshape(4, 128, 2048)
    small[33:55] = w13(inp["ffn2_w1"][L], inp["ffn2_w3"][L])
    big = np.empty((NBIG, 128, DFF), np.float32)
    big[0:8] = inp["ffn1_w2"][L].reshape(FC, 128, 8, 128).transpose(2, 1, 0, 3).reshape(8, 128, DFF)
    big[8:16] = inp["ffn2_w2"][L].reshape(FC, 128, 8, 128).transpose(2, 1, 0, 3).reshape(8, 128, DFF)

    def pc(v, n):
        return v.reshape(n, 128).T
    gains = np.concatenate([pc(inp["norm_ffn1"][L], 8), pc(inp["norm_mix"][L], 8),
                            pc(inp["norm_ffn2"][L], 8), pc(inp["norm_final"], 8)], axis=1)
    vecs = np.concatenate([pc(inp["conv_b"][L], 4), pc(inp["conv_ln_g"][L], 4), pc(inp["conv_ln_b"][L], 4),
                           pc(inp["conv_out_g"][L], 4), pc(inp["ssm_D"][L], 4), pc(inp["ssm_glu_b"][L], 4),
                           pc(inp["ssm_out_g"][L], 4),
                           (np.arange(128)[:, None] // 32 == np.arange(4)[None, :]).astype(np.float32)], axis=1)
    cw = inp["conv_w"][L].reshape(31, 4, 128).transpose(2, 1, 0).reshape(128, 124)
    ident = (np.arange(128)[:, None] % 32 == np.arange(32)[None, :]).astype(np.float32)
    A_re, A_im, ldt = inp["ssm_A_re"][L], inp["ssm_A_im"][L], inp["ssm_log_dt"][L]

    def ql(a):
        return a.reshape(16, 2, 64).transpose(1, 2, 0).reshape(128, 16)
    aq = np.concatenate([ql(A_re), ql(A_im), ql(np.broadcast_to(ldt[:, None], (32, 64)))], axis=1)
    B_re, B_im = inp["ssm_B_re"][L], inp["ssm_B_im"][L]
    sc = np.zeros((5, 128, 4, 128), np.float32)
    for g in range(32):
        uc, g8 = divmod(g, 8)
        rows = slice(16 * g8, 16 * g8 + 16)
        sc[0, rows, uc, :] = np.tile(A_re[g], 2)[None, :]
        sc[1, rows, uc, :] = np.tile(A_im[g], 2)[None, :]
        sc[2, rows, uc, :] = ldt[g]
        qo = 64 * (g % 2)
        sc[3, rows, uc, qo:qo + 64] = B_re[g].T
        sc[4, rows, uc, qo:qo + 64] = B_im[g].T
    sc = sc.transpose(1, 0, 2, 3).reshape(128, 5 * 512)
    C_re, C_im = inp["ssm_C_re"][L], inp["ssm_C_im"][L]
    cq = np.zeros((2, 128, 16, 32), np.float32)
    for g in range(32):
        gc, g2 = divmod(g, 2)
        cq[0, 64 * g2:64 * g2 + 64, gc, 16 * g2:16 * g2 + 16] = C_re[g].T
        cq[1, 64 * g2:64 * g2 + 64, gc, 16 * g2:16 * g2 + 16] = C_im[g].T
    cq = cq.transpose(1, 0, 2, 3).reshape(128, 2 * 512)
    idf = np.concatenate([np.eye(128, dtype=np.float32), 2.0 * np.eye(128, dtype=np.float32)], axis=1)
    return dict(wsmall=small, wbig=big, gains=f(gains), vecs=f(vecs), cw=f(cw), ident=ident, idf=idf,
                aq=f(aq), sc=f(sc), cq=f(cq))


_NC_CACHE = {}


def kernel(**inputs):
    inp = {k: np.asarray(v) for k, v in inputs.items()}
    x = inp["x"]
    shared = _host_layout(inp)
    in_maps = []
    for c in range(NCORES):
        xc = x[NSEQ * c:NSEQ * (c + 1)].reshape(NSEQ * SEQ, D)
        m = dict(shared)
        m["xT"] = np.ascontiguousarray(xc.T)
        in_maps.append(m)
    if "nc" not in _NC_CACHE:
        _NC_CACHE["nc"] = build_nc()
    res = run_bass_kernel_spmd(_NC_CACHE["nc"], in_maps, core_ids=list(range(NCORES)))
    out = np.empty((16, SEQ, D), np.float32)
    for c in range(NCORES):
        o = res.results[c]["oT"]
        out[NSEQ * c:NSEQ * (c + 1)] = np.ascontiguousarray(o.T).reshape(NSEQ, SEQ, D)
    return out
```

```python
import contextlib
import math
import numpy as np
import concourse.bass as bass
import concourse.mybir as mybir
from concourse.bass_utils import run_bass_kernel_spmd

F32 = mybir.dt.float32
BF16 = mybir.dt.bfloat16
I32 = mybir.dt.int32
AF = mybir.ActivationFunctionType
ALU = mybir.AluOpType

NCORES = 8
D = 1024
DC = 8
DFF = 2816
FC = 22
T = 512
SEQ = 4096
TPS = SEQ // T
NSEQ = 2
NT = TPS * NSEQ
TF = 128
NF = T // TF
EPS = 1e-6
RS = 3
RB = 2
NSMALL = 55
NBIG = 16
WIN_PAIRS = [(4, 0), (5, 1), (6, 2), (7, 3), (8, 9), (10, 11)]


class Tok:
    __slots__ = ("w", "r", "name")

    def __init__(self, name=""):
        self.w = None
        self.r = {}
        self.name = name


class Prog:
    ENG = ("pe", "act", "dve", "pool", "sp")
    NDSEM = 10

    def __init__(self, nc, stack):
        self.nc = nc
        self.q = {e: [] for e in self.ENG}
        self.cnt = {e: 0 for e in self.ENG}
        self.seen = {e: {} for e in self.ENG}
        self.sems = {}
        for e in self.ENG:
            self.sems[e] = stack.enter_context(nc.semaphore("s_" + e))
        for qn in ("sp", "pool"):
            for i in range(self.NDSEM):
                self.sems[("d" + qn, i)] = stack.enter_context(nc.semaphore("d%s_%d" % (qn, i)))
        self.dma_i = {"sp": 0, "pool": 0}

    def _deps(self, eng, reads, writes):
        deps = {}

        def add(t):
            if t is None:
                return
            k, v = t
            if deps.get(k, 0) < v:
                deps[k] = v

        for b in reads:
            add(b.w)
        for b in writes:
            if b.w is not None and not (b.w[0] == eng == "pe"):
                add(b.w)
            for k, v in b.r.items():
                if not (k == eng == "pe"):
                    add((k, v))
        return deps

    def _commit(self, eng, deps, tok, reads, writes):
        waits = []
        seen = self.seen[eng]
        for k, v in deps.items():
            if seen.get(k, 0) < v:
                seen[k] = v
                waits.append((k, v))
        k, v = tok
        for b in reads:
            if b.r.get(k, 0) < v:
                b.r[k] = v
        for b in writes:
            b.w = tok
            b.r = {}
        return waits

    def op(self, eng, fn, reads=(), writes=()):
        deps = self._deps(eng, reads, writes)
        self.cnt[eng] += 1
        tok = (eng, self.cnt[eng])
        waits = self._commit(eng, deps, tok, reads, writes)
        self.q[eng].append((waits, fn, (eng, 1)))
        return tok

    def dma(self, queue, fn, reads=(), writes=()):
        deps = self._deps(queue, reads, writes)
        i = self.dma_i[queue]
        self.dma_i[queue] += 1
        key = ("d" + queue, i % self.NDSEM)
        val = 16 * (i // self.NDSEM + 1)
        if val > 16 and deps.get(key, 0) < val - 16:
            deps[key] = val - 16
        tok = (key, val)
        waits = self._commit(queue, deps, tok, reads, writes)
        self.q[queue].append((waits, fn, (key, 16)))
        return tok

    def wait_all(self, eng, toks):
        deps = {}
        for k, v in toks:
            if deps.get(k, 0) < v:
                deps[k] = v
        waits = []
        for k, v in deps.items():
            if self.seen[eng].get(k, 0) < v:
                self.seen[eng][k] = v
                waits.append((k, v))
        self.q[eng].append((waits, None, None))

    def emit(self, block):
        sems = self.sems

        def replay(name):
            def body(e):
                for waits, fn, inc in self.q[name]:
                    for k, v in waits:
                        e.wait_ge(sems[k], v)
                    if fn is None:
                        continue
                    ins = fn(e)
                    ins.then_inc(sems[inc[0]], inc[1])
            return body

        block.tensor(replay("pe"))
        block.scalar(replay("act"))
        block.vector(replay("dve"))
        block.gpsimd(replay("pool"))
        block.sync(replay("sp"))


def toks(n, name):
    return [Tok("%s%d" % (name, i)) for i in range(n)]


def build_nc(ntiles=NT, stages=None):
    nc = bass.Bass("TRN2", target_bir_lowering=False)
    NTOK = NT * T
    xT = nc.dram_tensor("xT", [D, NTOK], F32, kind="ExternalInput").ap()
    oT = nc.dram_tensor("oT", [D, NTOK], F32, kind="ExternalOutput").ap()
    wsm_d = nc.dram_tensor("wsmall", [NSMALL, 128, 2048], F32, kind="ExternalInput").ap()
    wbg_d = nc.dram_tensor("wbig", [NBIG, 128, DFF], F32, kind="ExternalInput").ap()
    gains_d = nc.dram_tensor("gains", [128, 4 * DC], F32, kind="ExternalInput").ap()
    vecs_d = nc.dram_tensor("vecs", [128, 8 * 4], F32, kind="ExternalInput").ap()
    cw_d = nc.dram_tensor("cw", [128, 4 * 31], F32, kind="ExternalInput").ap()
    ident_d = nc.dram_tensor("ident", [128, 32], F32, kind="ExternalInput").ap()
    aq_d = nc.dram_tensor("aq", [128, 3 * 16], F32, kind="ExternalInput").ap()
    sc_d = nc.dram_tensor("sc", [128, 5 * 512], F32, kind="ExternalInput").ap()
    cq_d = nc.dram_tensor("cq", [128, 2 * 512], F32, kind="ExternalInput").ap()
    idf_d = nc.dram_tensor("idf", [128, 256], F32, kind="ExternalInput").ap()
    wsm_bf = nc.dram_tensor("wsm_bf", [NSMALL, 128, 2048], BF16, kind="Internal").ap()
    wbg_bf = nc.dram_tensor("wbg_bf", [NBIG, 128, DFF], BF16, kind="Internal").ap()

    with contextlib.ExitStack() as st:
        def sb(name, shape, dt):
            return st.enter_context(nc.sbuf_tensor(name, shape, dt))

        def ps(name):
            return st.enter_context(nc.psum_tensor(name, [128, 512], F32))

        xbs = [sb("xb%d" % i, [128, DC, T], F32) for i in range(3)]
        hb = sb("hb", [128, DC, T], BF16)
        sq = sb("sq", [128, 4, T], BF16)
        stt_ = sb("stt", [128, 3, T], F32)
        sta = sb("sta", [128, 2, T], F32)
        act = sb("act", [128, FC, T], BF16)
        ms = sb("ms", [128, 12, T], BF16)
        stmp2 = sb("stmp2", [128, 2, T], BF16)
        idf = sb("idf_s", [128, 2, 128], F32)
        stmp = sb("stmp", [128, 2, T], BF16)
        ws = sb("ws", [128, RS, 2048], BF16)
        wb = sb("wb", [128, RB, DFF], BF16)
        diag = sb("diag", [128, 4 * 31, 32], BF16)
        abuf = sb("abuf", [128, 4, 30 + T], BF16)
        co = sb("co", [128, 4, T], F32)
        u32 = sb("u32", [128, 4, T], F32)
        s5tb = sb("s5tb", [128, 2, T], BF16)
        s5v = sb("s5v", [128, 2, T], F32)
        s5w = sb("s5w", [128, 2, T], F32)
        s5x = sb("s5x", [128, 4, T], BF16)
        ctabb = sb("ctabb", [128, 16, TF], BF16)
        stabb = sb("stabb", [128, 16, TF], BF16)
        s5wb = sb("s5wb", [128, 2, T], BF16)
        rtab = sb("rtab", [128, 16, TF], F32)
        btabF = sb("btabF", [128, 2, 16 * 128], BF16)
        ctb = sb("ctb", [128, 2, 512], BF16)
        gains = sb("gains_s", [128, 4 * DC], F32)
        vecs = sb("vecs_s", [128, 32], F32)
        cw = sb("cw_s", [128, 4 * 31], F32)
        ident = sb("ident_s", [128, 32], F32)
        ones_bf = sb("ones_bf", [128, 128], BF16)
        aq = sb("aq_s", [128, 48], F32)
        qs = sb("qs", [128, 12, 16], F32)
        carry = sb("carry", [128, 4, 8], F32)
        ctmp = sb("ctmp", [128, 4, 4], F32)
        btab = act[:, 0:4, :].rearrange("p a b -> p (a b)").bitcast(F32).rearrange("p (a b) -> p a b", a=2)
        xb0 = xbs[0]
        hbm = ms[:, 4:12, :]
        ctab = xbs[1][:, 0:4, :].rearrange("p a (b c) -> p (a b) c", c=TF)
        stab = xbs[1][:, 4:8, :].rearrange("p a (b c) -> p (a b) c", c=TF)
        pA = [ps("pA%d" % i) for i in range(4)]
        pB = [ps("pB%d" % i) for i in range(2)]
        pS = ps("pS")
        pX = ps("pX")

        P = Prog(nc, st)
        block = st.enter_context(nc.Block())

        t_xs = [toks(DC, "x%d_" % i) for i in range(3)]
        t_h = toks(DC, "h")
        t_sq = toks(4, "sq")
        t_st = toks(3, "st")
        t_sta = toks(2, "sta")
        t_act = toks(FC, "act")
        t_ms = toks(12, "ms")
        t_hm = t_ms[4:12]
        t_stmp2 = toks(2, "stmp2")
        t_stmp = toks(2, "stmp")
        t_ws2 = [toks(2, "ws%d_" % i) for i in range(RS)]
        t_wb2 = [toks(2, "wb%d_" % i) for i in range(RB)]
        t_diag = Tok("diag")
        t_abuf = toks(4, "abuf")
        t_halo = toks(4, "halo")
        t_co = toks(4, "co")
        t_u32 = toks(4, "u32")
        t_s5t = toks(4, "s5t")
        t_s5w = toks(2, "s5w")
        t_s5wb = toks(2, "s5wb")
        t_s5x = toks(4, "s5x")
        t_tab = Tok("tab")
        t_const = Tok("const")
        t_pA = toks(4, "pA")
        t_pB = toks(2, "pB")
        t_pS = Tok("pS")
        t_pX = Tok("pX")
        t_carry = toks(4, "carry")
        t_ctmp = Tok("ctmp")
        t_setup = Tok("setup")

        def g_ap(n, c):
            return gains[:, n * DC + c:n * DC + c + 1]

        def vec_ap(n, c):
            return vecs[:, n * 4 + c:n * 4 + c + 1]

        LEVEL = stages if stages is not None else 9

        for (dst, src) in ((gains, gains_d), (vecs, vecs_d), (cw, cw_d), (ident, ident_d), (aq, aq_d)):
            P.dma("sp", lambda e, dst=dst, src=src: e.dma_start(out=dst[:], in_=src), writes=[t_setup])
        P.dma("sp", lambda e: e.dma_start(out=idf[:].rearrange("p a b -> p (a b)"), in_=idf_d), writes=[t_setup])
        P.dma("sp", lambda e: e.dma_start(out=xb0[:, 0:5, :].rearrange("p a b -> p (a b)"), in_=sc_d), writes=[t_setup])
        P.dma("sp", lambda e: e.dma_start(out=xb0[:, 5:7, :].rearrange("p a b -> p (a b)"), in_=cq_d), writes=[t_setup])
        P.op("dve", lambda e: e.memset(ones_bf[:], 1.0), writes=[t_const])

        def S(fn, eng="dve"):
            P.op(eng, fn, reads=[t_setup, t_const], writes=[t_setup])

        for c in range(4):
            for k in range(31):
                S(lambda e, c=c, k=k: e.tensor_scalar(
                    out=diag[:, c * 31 + k, :], in0=ident[:], scalar1=cw[:, c * 31 + k:c * 31 + k + 1],
                    scalar2=None, op0=ALU.mult))

        PI = math.pi

        def lam_setup(are, aim, ldt, w):
            dt_, zr, th, r_, tmp, ang, sn, cs_, k_i = [w(i) for i in range(9)]
            S(lambda e: e.activation(out=dt_, in_=ldt, func=AF.Exp), "act")
            S(lambda e: e.tensor_tensor(out=zr, in0=are, in1=dt_, op=ALU.mult))
            S(lambda e: e.tensor_tensor(out=th, in0=aim, in1=dt_, op=ALU.mult))
            S(lambda e: e.activation(out=r_, in_=zr, func=AF.Exp), "act")
            for which, out_ in ((0, sn), (1, cs_)):
                off = 0.0 if which == 0 else PI / 2
                S(lambda e, off=off: e.tensor_scalar(out=ang, in0=th, scalar1=off, scalar2=None, op0=ALU.add))
                S(lambda e: e.tensor_scalar(out=tmp, in0=ang, scalar1=1.0 / (2 * PI), scalar2=None, op0=ALU.mult))
                S(lambda e: e.tensor_copy(out=k_i.bitcast(I32), in_=tmp))
                S(lambda e: e.tensor_copy(out=tmp, in_=k_i.bitcast(I32)))
                S(lambda e: e.scalar_tensor_tensor(out=ang, in0=tmp, scalar=-2 * PI, in1=ang, op0=ALU.mult, op1=ALU.add))
                S(lambda e: e.tensor_scalar(out=tmp, in0=ang, scalar1=-PI, scalar2=2 * PI, op0=ALU.is_lt, op1=ALU.mult))
                S(lambda e: e.tensor_tensor(out=ang, in0=ang, in1=tmp, op=ALU.add))
                S(lambda e: e.tensor_scalar(out=tmp, in0=ang, scalar1=PI, scalar2=-2 * PI, op0=ALU.is_gt, op1=ALU.mult))
                S(lambda e: e.tensor_tensor(out=ang, in0=ang, in1=tmp, op=ALU.add))
                S(lambda e: e.tensor_scalar(out=ang, in0=ang, scalar1=-3.1415925, scalar2=3.1415925, op0=ALU.max, op1=ALU.min))
                S(lambda e, out_=out_: e.activation(out=out_, in_=ang, func=AF.Sin), "act")
            return dict(r=r_, cos=cs_, sin=sn, dt=dt_)

        lq = lam_setup(aq[:, 0:16], aq[:, 16:32], aq[:, 32:48], lambda i: qs[:, i, :])
        S(lambda e: e.memset(ctab[:, :, 0:1], 1.0))
        S(lambda e: e.memset(stab[:, :, 0:1], 0.0))
        S(lambda e: e.tensor_copy(out=ctab[:, :, 1], in_=lq["cos"]))
        S(lambda e: e.tensor_copy(out=stab[:, :, 1], in_=lq["sin"]))
        k = 1
        while k < TF:
            for gc in range(16):
                er = ctab[:, gc, k:k + 1]
                ei = stab[:, gc, k:k + 1]
                n = min(k, TF - k)
                if n > 1:
                    src_c = ctab[:, gc, 1:n]
                    src_s = stab[:, gc, 1:n]
                    dst_c = ctab[:, gc, k + 1:k + n]
                    dst_s = stab[:, gc, k + 1:k + n]
                    tmpv = co[:, 0, 0:n - 1]
                    S(lambda e, src_s=src_s, ei=ei, tmpv=tmpv: e.tensor_scalar(out=tmpv, in0=src_s, scalar1=ei, scalar2=None, op0=ALU.mult))
                    S(lambda e, src_c=src_c, er=er, tmpv=tmpv, dst_c=dst_c: e.scalar_tensor_tensor(out=dst_c, in0=src_c, scalar=er, in1=tmpv, op0=ALU.mult, op1=ALU.subtract))
                    S(lambda e, src_s=src_s, er=er, tmpv=tmpv: e.tensor_scalar(out=tmpv, in0=src_s, scalar1=er, scalar2=None, op0=ALU.mult))
                    S(lambda e, src_c=src_c, ei=ei, tmpv=tmpv, dst_s=dst_s: e.scalar_tensor_tensor(out=dst_s, in0=src_c, scalar=ei, in1=tmpv, op0=ALU.mult, op1=ALU.add))
            if 2 * k < TF:
                for gc in range(16):
                    er = ctab[:, gc, k:k + 1]
                    ei = stab[:, gc, k:k + 1]
                    tmpv = co[:, 0, 0:1]
                    S(lambda e, ei=ei, tmpv=tmpv: e.tensor_scalar(out=tmpv, in0=ei, scalar1=ei, scalar2=None, op0=ALU.mult))
                    S(lambda e, er=er, tmpv=tmpv, gc=gc, k=k: e.scalar_tensor_tensor(out=ctab[:, gc, 2 * k:2 * k + 1], in0=er, scalar=er, in1=tmpv, op0=ALU.mult, op1=ALU.subtract))
                    S(lambda e, er=er, ei=ei, gc=gc, k=k: e.tensor_scalar(out=stab[:, gc, 2 * k:2 * k + 1], in0=er, scalar1=ei, scalar2=2.0, op0=ALU.mult, op1=ALU.mult))
            k *= 2
        S(lambda e: e.tensor_tensor(out=qs[:, 9, :], in0=ctab[:, :, TF - 1], in1=lq["cos"], op=ALU.mult))
        S(lambda e: e.tensor_tensor(out=qs[:, 11, :], in0=stab[:, :, TF - 1], in1=lq["sin"], op=ALU.mult))
        S(lambda e: e.tensor_tensor(out=qs[:, 9, :], in0=qs[:, 9, :], in1=qs[:, 11, :], op=ALU.subtract))
        S(lambda e: e.tensor_tensor(out=qs[:, 10, :], in0=ctab[:, :, TF - 1], in1=lq["sin"], op=ALU.mult))
        S(lambda e: e.tensor_tensor(out=qs[:, 11, :], in0=stab[:, :, TF - 1], in1=lq["cos"], op=ALU.mult))
        S(lambda e: e.tensor_tensor(out=qs[:, 10, :], in0=qs[:, 10, :], in1=qs[:, 11, :], op=ALU.add))
        S(lambda e: e.tensor_tensor(out=qs[:, 9, :], in0=qs[:, 9, :], in1=lq["r"], op=ALU.mult))
        S(lambda e: e.tensor_tensor(out=qs[:, 10, :], in0=qs[:, 10, :], in1=lq["r"], op=ALU.mult))
        rEr = qs[:, 9, :]
        rEi = qs[:, 10, :]
        S(lambda e: e.tensor_copy(out=ctabb[:], in_=ctab))
        S(lambda e: e.tensor_copy(out=stabb[:], in_=stab))
        S(lambda e: e.memset(rtab[:], 1.0))
        for gc in range(16):
            S(lambda e, gc=gc: e.tensor_scalar(out=rtab[:, gc, :], in0=rtab[:, gc, :], scalar1=qs[:, 3, gc:gc + 1], scalar2=None, op0=ALU.mult))
        S(lambda e: e.memset(rtab[:, :, 0:1], 0.0))

        def cv(i):
            return co[:, i, :] if i < 4 else (u32[:, i - 4, :] if i < 8 else s5v[:, 0, :])
        lc = lam_setup(xb0[:, 0, :], xb0[:, 1, :], xb0[:, 2, :], cv)
        lr_, li_ = xb0[:, 0, :], xb0[:, 1, :]
        br_, bi_ = xb0[:, 3, :], xb0[:, 4, :]
        ar_, ai_, den, cr_, ci_, tq = cv(0), cv(1), cv(2), cv(4), cv(5), s5v[:, 1, :]
        S(lambda e: e.tensor_tensor(out=ar_, in0=lc["r"], in1=lc["cos"], op=ALU.mult))
        S(lambda e: e.tensor_tensor(out=ai_, in0=lc["r"], in1=lc["sin"], op=ALU.mult))
        S(lambda e: e.tensor_scalar(out=ar_, in0=ar_, scalar1=-1.0, scalar2=None, op0=ALU.add))
        S(lambda e: e.tensor_tensor(out=den, in0=lr_, in1=lr_, op=ALU.mult))
        S(lambda e: e.tensor_tensor(out=tq, in0=li_, in1=li_, op=ALU.mult))
        S(lambda e: e.tensor_tensor(out=den, in0=den, in1=tq, op=ALU.add))
        S(lambda e: e.reciprocal(out=den, in_=den))
        S(lambda e: e.tensor_tensor(out=cr_, in0=ar_, in1=lr_, op=ALU.mult))
        S(lambda e: e.tensor_tensor(out=tq, in0=ai_, in1=li_, op=ALU.mult))
        S(lambda e: e.tensor_tensor(out=cr_, in0=cr_, in1=tq, op=ALU.add))
        S(lambda e: e.tensor_tensor(out=cr_, in0=cr_, in1=den, op=ALU.mult))
        S(lambda e: e.tensor_tensor(out=ci_, in0=ai_, in1=lr_, op=ALU.mult))
        S(lambda e: e.tensor_tensor(out=tq, in0=ar_, in1=li_, op=ALU.mult))
        S(lambda e: e.tensor_tensor(out=ci_, in0=ci_, in1=tq, op=ALU.subtract))
        S(lambda e: e.tensor_tensor(out=ci_, in0=ci_, in1=den, op=ALU.mult))
        S(lambda e: e.tensor_tensor(out=tq, in0=ci_, in1=bi_, op=ALU.mult))
        S(lambda e: e.tensor_tensor(out=den, in0=cr_, in1=br_, op=ALU.mult))
        S(lambda e: e.tensor_tensor(out=btab[:, 0, :], in0=den, in1=tq, op=ALU.subtract))
        S(lambda e: e.tensor_tensor(out=tq, in0=ci_, in1=br_, op=ALU.mult))
        S(lambda e: e.tensor_tensor(out=den, in0=cr_, in1=bi_, op=ALU.mult))
        S(lambda e: e.tensor_tensor(out=btab[:, 1, :], in0=den, in1=tq, op=ALU.add))
        for ri in range(2):
            for gc in range(16):
                uc_, j_ = divmod(gc, 4)
                S(lambda e, ri=ri, gc=gc, uc_=uc_, j_=j_: e.tensor_scalar(
                    out=btabF[:, ri, gc * 128:(gc + 1) * 128], in0=btab[:, ri, uc_ * 128:(uc_ + 1) * 128],
                    scalar1=vecs[:, 28 + j_:29 + j_], scalar2=None, op0=ALU.mult))
        S(lambda e: e.tensor_copy(out=ctb[:, 0, :], in_=xb0[:, 5, :]))
        S(lambda e: e.tensor_scalar(out=ctb[:, 1, :], in0=xb0[:, 6, :], scalar1=-1.0, scalar2=None, op0=ALU.mult))
        P.op("dve", lambda e: e.memset(ctmp[:], 0.0), reads=[t_setup],
             writes=[t_tab, t_diag, t_ctmp] + t_xs[0] + t_xs[1] + t_st + t_co + t_u32 + t_s5t + t_act[0:4])

        wstate = {"ns": 0, "nb": 0, "cs": 0, "cb": 0}
        total_small = ntiles * NSMALL
        total_big = ntiles * NBIG
        small_seq = []
        big_seq = []

        scr_s = {}
        scr_b = {}

        def issue_small(piece):
            n = wstate["ns"]
            slot = n % RS
            if piece not in scr_s:
                for hh in range(2):
                    P.dma("pool", lambda e, hh=hh, slot=slot, piece=piece: e.dma_start(
                        out=ws[:, slot, hh * 1024:(hh + 1) * 1024], in_=wsm_d[piece][:, hh * 1024:(hh + 1) * 1024]),
                        writes=[t_ws2[slot][hh]])
                tk = Tok("scr_s%d" % piece)
                P.dma("sp", lambda e, slot=slot, piece=piece: e.dma_start(out=wsm_bf[piece], in_=ws[:, slot, :]),
                      reads=t_ws2[slot], writes=[tk])
                scr_s[piece] = tk
            else:
                P.dma("sp", lambda e, slot=slot, piece=piece: e.dma_start(out=ws[:, slot, :], in_=wsm_bf[piece]),
                      reads=[scr_s[piece]], writes=t_ws2[slot])
            wstate["ns"] += 1

        def issue_big(piece):
            n = wstate["nb"]
            slot = n % RB
            HB = DFF // 2
            if piece not in scr_b:
                for hh in range(2):
                    P.dma("pool", lambda e, hh=hh, slot=slot, piece=piece: e.dma_start(
                        out=wb[:, slot, hh * HB:(hh + 1) * HB], in_=wbg_d[piece][:, hh * HB:(hh + 1) * HB]),
                        writes=[t_wb2[slot][hh]])
                tk = Tok("scr_b%d" % piece)
                P.dma("sp", lambda e, slot=slot, piece=piece: e.dma_start(out=wbg_bf[piece], in_=wb[:, slot, :]),
                      reads=t_wb2[slot], writes=[tk])
                scr_b[piece] = tk
            else:
                P.dma("sp", lambda e, slot=slot, piece=piece: e.dma_start(out=wb[:, slot, :], in_=wbg_bf[piece]),
                      reads=[scr_b[piece]], writes=t_wb2[slot])
            wstate["nb"] += 1

        def need_small(piece):
            n = wstate["cs"]
            assert small_plan[n] == piece, (n, small_plan[n], piece)
            while wstate["ns"] < min(n + RS, len(small_plan)):
                issue_small(small_plan[wstate["ns"]])
            wstate["cs"] += 1
            return n % RS

        def need_big(piece):
            n = wstate["cb"]
            assert big_plan[n] == piece, (n, big_plan[n], piece)
            while wstate["nb"] < min(n + RB, len(big_plan)):
                issue_big(big_plan[wstate["nb"]])
            wstate["cb"] += 1
            return n % RB

        def rms_stats(src_fn, src_toks, nchunks, dim, dst, t_dst):
            for c in range(nchunks):
                s = c % 4
                P.op("act", lambda e, c=c, s=s: e.activation(out=sq[:, s, :], in_=src_fn(c), func=AF.Square),
                     reads=[src_toks[c]], writes=[t_sq[s]])
                P.op("pe", lambda e, c=c, s=s: e.matmul(pS[:], lhsT=ones_bf[:], rhs=sq[:, s, :],
                                                        start=(c == 0), stop=(c == nchunks - 1)),
                     reads=[t_sq[s], t_const], writes=[t_pS])
            P.op("act", lambda e: e.activation(out=dst[:, 0, :], in_=pS[:], func=AF.Ln, scale=1.0 / dim, bias=EPS),
                 reads=[t_pS], writes=[t_dst[0]])
            P.op("act", lambda e: e.activation(out=dst[:, 1, :], in_=dst[:, 0, :], func=AF.Exp, scale=-0.5),
                 reads=[t_dst[0]], writes=[t_dst[1]])

        def norm_to_h(gi, xb, t_x, dst, t_dst, hdst=None, t_hdst=None):
            hdst = hb if hdst is None else hdst
            t_hdst = t_h if t_hdst is None else t_hdst
            rms_stats(lambda c: xb[:, c, :], t_x, DC, D, dst, t_dst)
            for c in range(DC):
                P.op("dve", lambda e, c=c: e.scalar_tensor_tensor(
                    out=hdst[:, c, :], in0=xb[:, c, :], scalar=g_ap(gi, c), in1=dst[:, 1, :],
                    op0=ALU.mult, op1=ALU.mult), reads=[t_x[c], t_dst[1]], writes=[t_hdst[c]])

        def ffn_A(fc, sbase):
            slot = need_small(sbase + fc)
            b = 2 * (fc % 2)

            def mm(e, slot=slot, b=b):
                for kc in range(DC):
                    e.matmul(pA[b][:], lhsT=ws[:, slot, kc * 128:(kc + 1) * 128], rhs=hb[:, kc, :],
                             start=(kc == 0), stop=(kc == DC - 1))
                for kc in range(DC):
                    ins = e.matmul(pA[b + 1][:], lhsT=ws[:, slot, 1024 + kc * 128:1024 + (kc + 1) * 128],
                                   rhs=hb[:, kc, :], start=(kc == 0), stop=(kc == DC - 1))
                return ins
            P.op("pe", mm, reads=t_ws2[slot] + t_h, writes=[t_pA[b], t_pA[b + 1]])
            s = fc % 2
            P.op("act", lambda e, s=s, b=b: e.activation(out=stmp[:, s, :], in_=pA[b][:], func=AF.Silu),
                 reads=[t_pA[b]], writes=[t_stmp[s]])
            P.op("act", lambda e, s=s, b=b: e.activation(out=stmp2[:, s, :], in_=pA[b + 1][:], func=AF.Copy),
                 reads=[t_pA[b + 1]], writes=[t_stmp2[s]])
            P.op("pool", lambda e, s=s, fc=fc: e.tensor_tensor(out=act[:, fc, :], in0=stmp2[:, s, :], in1=stmp[:, s, :], op=ALU.mult),
                 reads=[t_stmp2[s], t_stmp[s]], writes=[t_act[fc]])

        def ffn_B(dc, bbase, xb, t_x):
            slot = need_big(bbase + dc)
            b = 2 * (dc % 2)

            def mm2(e, slot=slot, b=b, dc=dc):
                e.matmul(pA[b][:], lhsT=idf[:, 1, :], rhs=xb[:, dc, :], start=True, stop=False)
                for fc in range(FC):
                    ins = e.matmul(pA[b][:], lhsT=wb[:, slot, fc * 128:(fc + 1) * 128], rhs=act[:, fc, :],
                                   start=False, stop=(fc == FC - 1))
                return ins
            P.op("pe", mm2, reads=t_wb2[slot] + t_act + [t_x[dc], t_tab], writes=[t_pA[b]])
            P.op("act", lambda e, dc=dc, b=b: e.activation(out=xb[:, dc, :], in_=pA[b][:], func=AF.Copy, scale=0.5),
                 reads=[t_pA[b]], writes=[t_x[dc]])

        def ffn_gen(gi, xb, t_x, sbase, bbase):
            norm_to_h(gi, xb, t_x, sta, t_sta)
            yield
            for fc in range(FC):
                ffn_A(fc, sbase)
                yield
            for dc in range(DC):
                ffn_B(dc, bbase, xb, t_x)
                yield

        def final_store(ti, xb, t_x):
            rms_stats(lambda c: xb[:, c, :], t_x, DC, D, sta, t_sta)
            for c in range(DC):
                P.op("dve", lambda e, c=c: e.scalar_tensor_tensor(
                    out=xb[:, c, :], in0=xb[:, c, :], scalar=g_ap(3, c), in1=sta[:, 1, :],
                    op0=ALU.mult, op1=ALU.mult), reads=[t_x[c], t_sta[1]], writes=[t_x[c]])
            cols = slice(ti * T, (ti + 1) * T)
            out_toks.append(P.dma("sp", lambda e, cols=cols, xb=xb: e.dma_start(out=oT_v[:, :, cols], in_=xb[:]), reads=t_x))
            if ti + 3 < ntiles:
                load_x(ti + 3)

        def ffn_pair_gen(t2, t1):
            xb2, tx2 = xbs[t2 % 3], t_xs[t2 % 3]
            xb1, tx1 = xbs[t1 % 3], t_xs[t1 % 3]
            norm_to_h(2, xb2, tx2, sta, t_sta)
            yield
            for fc in range(FC):
                ffn_A(fc, 33)
                yield
            norm_to_h(0, xb1, tx1, sta, t_sta)
            yield
            for dc in range(DC):
                ffn_B(dc, 8, xb2, tx2)
                yield
            for fc in range(2):
                ffn_A(fc, 0)
                yield
            final_store(t2, xb2, tx2)
            yield
            for fc in range(2, FC):
                ffn_A(fc, 0)
                yield
            for dc in range(DC):
                ffn_B(dc, 0, xb1, tx1)
                yield

        UBF = (0, 1, 2, 3)
        GLB = UBF
        MIX = tuple(range(4, 12))

        def conv_gen(ti):
            seq_end = (ti % TPS == TPS - 1)
            for c in range(4):
                def mmc(e, c=c):
                    for k in range(31):
                        for i in range(4):
                            ins = e.matmul(pS[32 * i:32 * i + 32, :], lhsT=diag[32 * i:32 * i + 32, c * 31 + k, :],
                                           rhs=abuf[32 * i:32 * i + 32, c, k:k + T],
                                           start=(k == 0), stop=(k == 30), tile_position=(32 * i, 32 * i))
                    return ins
                P.op("pe", mmc, reads=[t_diag, t_abuf[c], t_halo[c]], writes=[t_pS])
                P.op("act", lambda e, c=c: e.activation(out=co[:, c, :], in_=pS[:], func=AF.Identity, bias=vec_ap(0, c)),
                     reads=[t_pS, t_tab], writes=[t_co[c]])
                if not seq_end:
                    P.op("pool", lambda e, c=c: e.tensor_copy(out=abuf[:, c, 0:30], in_=abuf[:, c, T:T + 30]),
                         reads=[t_abuf[c]], writes=[t_halo[c]])
                P.op("act", lambda e, c=c: e.activation(out=sq[:, c, :], in_=co[:, c, :], func=AF.Square),
                     reads=[t_co[c]], writes=[t_sq[c]])
                P.op("act", lambda e, c=c: e.activation(out=ms[:, MIX[c], :], in_=co[:, c, :], func=AF.Copy),
                     reads=[t_co[c]], writes=[t_ms[MIX[c]]])
                yield
            for c in range(4):
                P.op("pe", lambda e, c=c: e.matmul(pS[:], lhsT=ones_bf[:], rhs=ms[:, MIX[c], :], start=(c == 0), stop=(c == 3)),
                     reads=[t_ms[MIX[c]], t_const], writes=[t_pS])
            P.op("act", lambda e: e.activation(out=stt_[:, 0, :], in_=pS[:], func=AF.Copy, scale=1.0 / 512),
                 reads=[t_pS], writes=[t_st[0]])
            P.op("dve", lambda e: e.tensor_tensor(out=stt_[:, 2, :], in0=stt_[:, 0, :], in1=stt_[:, 0, :], op=ALU.mult),
                 reads=[t_st[0]], writes=[t_st[2]])
            yield
            for c in range(4):
                P.op("pe", lambda e, c=c: e.matmul(pS[:], lhsT=ones_bf[:], rhs=sq[:, c, :], start=(c == 0), stop=(c == 3)),
                     reads=[t_sq[c], t_const], writes=[t_pS])
            P.op("dve", lambda e: e.scalar_tensor_tensor(out=stt_[:, 2, :], in0=pS[:], scalar=1.0 / 512, in1=stt_[:, 2, :],
                                                         op0=ALU.mult, op1=ALU.subtract),
                 reads=[t_pS, t_st[2]], writes=[t_st[2]])
            P.op("act", lambda e: e.activation(out=stt_[:, 2, :], in_=stt_[:, 2, :], func=AF.Ln, bias=EPS),
                 reads=[t_st[2]], writes=[t_st[2]])
            P.op("act", lambda e: e.activation(out=stt_[:, 1, :], in_=stt_[:, 2, :], func=AF.Exp, scale=-0.5),
                 reads=[t_st[2]], writes=[t_st[1]])
            P.op("dve", lambda e: e.tensor_tensor(out=stt_[:, 2, :], in0=stt_[:, 0, :], in1=stt_[:, 1, :], op=ALU.mult),
                 reads=[t_st[0], t_st[1], t_st[2]], writes=[t_st[2]])
            yield
            for c in range(4):
                P.op("dve", lambda e, c=c: e.tensor_tensor(out=co[:, c, :], in0=co[:, c, :], in1=stt_[:, 1, :], op=ALU.mult),
                     reads=[t_co[c], t_st[1]], writes=[t_co[c]])
                P.op("dve", lambda e, c=c: e.tensor_tensor(out=co[:, c, :], in0=co[:, c, :], in1=stt_[:, 2, :], op=ALU.subtract),
                     reads=[t_co[c], t_st[2]], writes=[t_co[c]])
                P.op("act", lambda e, c=c: e.activation(out=co[:, c, :], in_=co[:, c, :], func=AF.Silu,
                                                        scale=vec_ap(1, c), bias=vec_ap(2, c)),
                     reads=[t_co[c], t_tab], writes=[t_co[c]])
                yield
            rms_stats(lambda c: co[:, c, :], t_co, 4, 512, stt_, t_st)
            yield
            for c in range(4):
                P.op("dve", lambda e, c=c: e.scalar_tensor_tensor(
                    out=ms[:, MIX[c], :], in0=co[:, c, :], scalar=vec_ap(3, c), in1=stt_[:, 1, :],
                    op0=ALU.mult, op1=ALU.mult), reads=[t_co[c], t_st[1], t_tab], writes=[t_ms[MIX[c]]])

        def mixer_gen(ti, xb, t_x):
            seq_start = (ti % TPS == 0)
            seq_end = (ti % TPS == TPS - 1)
            norm_to_h(1, xb, t_x, stt_, t_st, hbm, t_hm)
            if seq_start:
                for c in range(4):
                    P.op("dve", lambda e, c=c: e.memset(abuf[:, c, 0:30], 0.0), writes=[t_halo[c]])
            yield
            for m, pair in enumerate(WIN_PAIRS):
                slot = need_small(22 + m)
                for o2, oc in enumerate(pair):
                    b = o2

                    def mm(e, slot=slot, o2=o2, b=b):
                        for kc in range(DC):
                            ins = e.matmul(pB[b][:], lhsT=ws[:, slot, (o2 * 8 + kc) * 128:(o2 * 8 + kc + 1) * 128],
                                           rhs=hbm[:, kc, :], start=(kc == 0), stop=(kc == DC - 1))
                        return ins
                    P.op("pe", mm, reads=t_ws2[slot] + t_hm, writes=[t_pB[b]])
                    if 4 <= oc < 8:
                        c = oc - 4
                        P.op("act", lambda e, b=b, c=c: e.activation(out=s5x[:, c % 2, :], in_=pB[b][:], func=AF.Sigmoid),
                             reads=[t_pB[b]], writes=[t_s5x[c % 2]])
                    elif oc < 4:
                        c = oc
                        P.op("dve", lambda e, b=b, c=c: e.tensor_tensor(out=abuf[:, c, 30:30 + T], in0=pB[b][:], in1=s5x[:, c % 2, :], op=ALU.mult),
                             reads=[t_pB[b], t_s5x[c % 2]], writes=[t_abuf[c]])
                    else:
                        c = oc - 8
                        P.op("act", lambda e, b=b, c=c: e.activation(out=u32[:, c, :], in_=pB[b][:], func=AF.Copy),
                             reads=[t_pB[b]], writes=[t_u32[c]])
                        P.op("act", lambda e, c=c: e.activation(out=ms[:, UBF[c], :], in_=u32[:, c, :], func=AF.Copy),
                             reads=[t_u32[c]], writes=[t_ms[UBF[c]]])
            yield

            cg = conv_gen(ti)
            steps = [(uc, f) for uc in range(4) for f in range(NF)]
            NS = len(steps)
            D_ = lambda fn, r, w: P.op("dve", fn, reads=r, writes=w)
            G_ = lambda fn, r, w: P.op("dve", fn, reads=r, writes=w)
            tA, tB, vr, vi = s5tb[:, 0, :], s5tb[:, 1, :], s5v[:, 0, :], s5v[:, 1, :]
            wr, wi = s5w[:, 0, :], s5w[:, 1, :]
            bre, bim = pB[0], pB[1]

            def tabs(uc):
                return (ctabb[:, 4 * uc:4 * uc + 4, :].rearrange("p a b -> p (a b)"),
                        stabb[:, 4 * uc:4 * uc + 4, :].rearrange("p a b -> p (a b)"),
                        rtab[:, 4 * uc:4 * uc + 4, :].rearrange("p a b -> p (a b)"))

            def mmb_op(si):
                uc, f = steps[si]
                fs = slice(f * TF, (f + 1) * TF)

                def mmb(e, uc=uc, fs=fs):
                    for ri in range(2):
                        for j in range(4):
                            gc = 4 * uc + j
                            ins = e.matmul(pB[ri][:, j * TF:(j + 1) * TF],
                                           lhsT=btabF[:, ri, gc * 128:(gc + 1) * 128],
                                           rhs=ms[:, UBF[uc], fs], start=True, stop=True)
                    return ins
                P.op("pe", mmb, reads=[t_tab, t_ms[UBF[uc]]], writes=[t_pB[0], t_pB[1]])

            def fwd(si):
                uc, f = steps[si]
                cT, sT, rT = tabs(uc)
                D_(lambda e: e.tensor_tensor(out=tA, in0=bre[:], in1=cT, op=ALU.mult), [t_pB[0], t_tab], [t_s5t[0]])
                D_(lambda e: e.tensor_tensor(out=tB, in0=bim[:], in1=sT, op=ALU.mult), [t_pB[1], t_tab], [t_s5t[1]])
                D_(lambda e: e.tensor_tensor(out=vr, in0=tA, in1=tB, op=ALU.add), [t_s5t[0], t_s5t[1]], [t_s5t[2]])
                D_(lambda e: e.tensor_tensor(out=tA, in0=bim[:], in1=cT, op=ALU.mult), [t_pB[1], t_tab], [t_s5t[0]])
                D_(lambda e: e.tensor_tensor(out=tB, in0=bre[:], in1=sT, op=ALU.mult), [t_pB[0], t_tab], [t_s5t[1]])
                D_(lambda e: e.tensor_tensor(out=vi, in0=tA, in1=tB, op=ALU.subtract), [t_s5t[0], t_s5t[1]], [t_s5t[3]])

            def scans(si):
                uc, f = steps[si]
                cT, sT, rT = tabs(uc)
                first = seq_start and f == 0
                if not first:
                    vr0 = s5v[:, 0, :].rearrange("p (a b) -> p a b", a=4)[:, :, 0]
                    vi0 = s5v[:, 1, :].rearrange("p (a b) -> p a b", a=4)[:, :, 0]
                    D_(lambda e: e.tensor_tensor(out=vr0, in0=vr0, in1=carry[:, uc, 0:4], op=ALU.add), [t_s5t[2], t_carry[uc]], [t_s5t[2]])
                    D_(lambda e: e.tensor_tensor(out=vi0, in0=vi0, in1=carry[:, uc, 4:8], op=ALU.add), [t_s5t[3], t_carry[uc]], [t_s5t[3]])
                D_(lambda e: e.tensor_tensor_scan(out=wr, data0=rT, data1=vr, initial=0.0, op0=ALU.mult, op1=ALU.add), [t_s5t[2], t_tab], [t_s5w[0]])
                D_(lambda e: e.tensor_tensor_scan(out=wi, data0=rT, data1=vi, initial=0.0, op0=ALU.mult, op1=ALU.add), [t_s5t[3], t_tab], [t_s5w[1]])
                P.op("act", lambda e: e.activation(out=s5wb[:, 0, :], in_=wr, func=AF.Copy), reads=[t_s5w[0]], writes=[t_s5wb[0]])
                P.op("act", lambda e: e.activation(out=s5wb[:, 1, :], in_=wi, func=AF.Copy), reads=[t_s5w[1]], writes=[t_s5wb[1]])
                last = seq_end and f == NF - 1
                if not last:
                    wr1 = s5w[:, 0, :].rearrange("p (a b) -> p a b", a=4)[:, :, TF - 1]
                    wi1 = s5w[:, 1, :].rearrange("p (a b) -> p a b", a=4)[:, :, TF - 1]
                    er4 = rEr[:, 4 * uc:4 * uc + 4]
                    ei4 = rEi[:, 4 * uc:4 * uc + 4]
                    D_(lambda e: e.tensor_tensor(out=ctmp[:, 0, :], in0=wr1, in1=er4, op=ALU.mult), [t_s5w[0], t_tab, t_ctmp], [t_ctmp])
                    D_(lambda e: e.tensor_tensor(out=ctmp[:, 1, :], in0=wi1, in1=ei4, op=ALU.mult), [t_s5w[1], t_tab, t_ctmp], [t_ctmp])
                    D_(lambda e: e.tensor_tensor(out=ctmp[:, 2, :], in0=wi1, in1=er4, op=ALU.mult), [t_s5w[1], t_tab, t_ctmp], [t_ctmp])
                    D_(lambda e: e.tensor_tensor(out=ctmp[:, 3, :], in0=wr1, in1=ei4, op=ALU.mult), [t_s5w[0], t_tab, t_ctmp], [t_ctmp])
                    D_(lambda e: e.tensor_tensor(out=carry[:, uc, 0:4], in0=ctmp[:, 0, :], in1=ctmp[:, 1, :], op=ALU.subtract), [t_ctmp], [t_carry[uc]])
                    D_(lambda e: e.tensor_tensor(out=carry[:, uc, 4:8], in0=ctmp[:, 2, :], in1=ctmp[:, 3, :], op=ALU.add), [t_ctmp], [t_carry[uc]])

            def combos(si):
                uc, f = steps[si]
                cT, sT, rT = tabs(uc)
                xs = 2 * (si % 2)
                xr, xi = s5x[:, xs, :], s5x[:, xs + 1, :]
                wrb, wib = s5wb[:, 0, :], s5wb[:, 1, :]
                D_(lambda e: e.tensor_tensor(out=tA, in0=wrb, in1=cT, op=ALU.mult), [t_s5wb[0], t_tab], [t_s5t[0]])
                D_(lambda e: e.tensor_tensor(out=tB, in0=wib, in1=sT, op=ALU.mult), [t_s5wb[1], t_tab], [t_s5t[1]])
                D_(lambda e: e.tensor_tensor(out=xr, in0=tA, in1=tB, op=ALU.subtract), [t_s5t[0], t_s5t[1]], [t_s5x[xs]])
                D_(lambda e: e.tensor_tensor(out=tA, in0=wib, in1=cT, op=ALU.mult), [t_s5wb[1], t_tab], [t_s5t[0]])
                D_(lambda e: e.tensor_tensor(out=tB, in0=wrb, in1=sT, op=ALU.mult), [t_s5wb[0], t_tab], [t_s5t[1]])
                D_(lambda e: e.tensor_tensor(out=xi, in0=tA, in1=tB, op=ALU.add), [t_s5t[0], t_s5t[1]], [t_s5x[xs + 1]])

            def mmy_op(si):
                uc, f = steps[si]
                xs = 2 * (si % 2)
                fs = slice(f * TF, (f + 1) * TF)

                def mmy(e):
                    for j in range(4):
                        for ri in range(2):
                            gc = 4 * uc + j
                            ins = e.matmul(pX[32 * j:32 * j + 32, fs],
                                           lhsT=ctb[:, ri, gc * 32:(gc + 1) * 32],
                                           rhs=s5x[:, xs + ri, j * TF:(j + 1) * TF],
                                           start=(ri == 0), stop=(ri == 1), tile_position=(0, 32 * j))
                    return ins
                P.op("pe", mmy, reads=[t_tab, t_s5x[xs], t_s5x[xs + 1]], writes=[t_pX])
                if f == NF - 1:
                    yv = u32[:, uc, :]
                    P.op("dve", lambda e: e.scalar_tensor_tensor(
                        out=yv, in0=yv, scalar=vec_ap(4, uc), in1=pX[:], op0=ALU.mult, op1=ALU.add),
                        reads=[t_u32[uc], t_pX, t_tab], writes=[t_u32[uc]])
                    s = uc % 2
                    tq_ = sta[:, s, :]
                    P.op("act", lambda e: e.activation(out=tq_, in_=yv, func=AF.Square),
                         reads=[t_u32[uc]], writes=[t_sta[s]])
                    P.op("dve", lambda e: e.tensor_scalar(out=tq_, in0=tq_, scalar1=0.044715, scalar2=1.0, op0=ALU.mult, op1=ALU.add),
                         reads=[t_sta[s]], writes=[t_sta[s]])
                    P.op("dve", lambda e: e.tensor_tensor(out=tq_, in0=tq_, in1=yv, op=ALU.mult),
                         reads=[t_sta[s], t_u32[uc]], writes=[t_sta[s]])
                    P.op("act", lambda e: e.activation(out=tq_, in_=tq_, func=AF.Sigmoid, scale=1.5957691216057308),
                         reads=[t_sta[s]], writes=[t_sta[s]])
                    P.op("dve", lambda e: e.tensor_tensor(out=yv, in0=yv, in1=tq_, op=ALU.mult),
                         reads=[t_sta[s], t_u32[uc]], writes=[t_u32[uc]])
                    P.op("act", lambda e: e.activation(out=ms[:, GLB[uc], :], in_=yv, func=AF.Copy),
                         reads=[t_u32[uc]], writes=[t_ms[GLB[uc]]])

            mmb_op(0)
            fwd(0)
            scans(0)
            for si in range(NS):
                combos(si)
                if si + 1 < NS:
                    mmb_op(si + 1)
                yield
                if si + 1 < NS:
                    fwd(si + 1)
                    scans(si + 1)
                try:
                    next(cg)
                except StopIteration:
                    pass
                yield
                mmy_op(si)
            for _ in cg:
                pass
            slot = need_small(28)
            for oc in range(4):
                b = oc % 2

                def mmg(e, slot=slot, oc=oc, b=b):
                    for kc in range(4):
                        ins = e.matmul(pB[b][:], lhsT=ws[:, slot, (oc * 4 + kc) * 128:(oc * 4 + kc + 1) * 128],
                                       rhs=ms[:, GLB[kc], :], start=(kc == 0), stop=(kc == 3))
                    return ins
                P.op("pe", mmg, reads=t_ws2[slot] + [t_ms[g] for g in GLB], writes=[t_pB[b]])
                s = oc % 2
                tq_ = sta[:, s, :]
                P.op("act", lambda e, b=b, oc=oc, tq_=tq_: e.activation(out=tq_, in_=pB[b][:], func=AF.Sigmoid, bias=vec_ap(5, oc)),
                     reads=[t_pB[b], t_tab], writes=[t_sta[s]])
                P.op("dve", lambda e, oc=oc, tq_=tq_: e.tensor_tensor(out=u32[:, oc, :], in0=u32[:, oc, :], in1=tq_, op=ALU.mult),
                     reads=[t_sta[s], t_u32[oc]], writes=[t_u32[oc]])
            yield
            rms_stats(lambda c: u32[:, c, :], t_u32, 4, 512, stt_, t_st)
            for c in range(4):
                P.op("dve", lambda e, c=c: e.scalar_tensor_tensor(
                    out=ms[:, MIX[4 + c], :], in0=u32[:, c, :], scalar=vec_ap(6, c), in1=stt_[:, 1, :],
                    op0=ALU.mult, op1=ALU.mult), reads=[t_u32[c], t_st[1], t_tab], writes=[t_ms[MIX[4 + c]]])
            yield
            for m in range(4):
                slot = need_small(29 + m)
                for d2 in range(2):
                    dc = 2 * m + d2
                    b = dc % 2

                    def mmo(e, slot=slot, d2=d2, b=b, dc=dc):
                        e.matmul(pB[b][:], lhsT=idf[:, 0, :], rhs=xb[:, dc, :], start=True, stop=False)
                        for kc in range(8):
                            ins = e.matmul(pB[b][:], lhsT=ws[:, slot, (d2 * 8 + kc) * 128:(d2 * 8 + kc + 1) * 128],
                                           rhs=ms[:, MIX[kc], :], start=False, stop=(kc == 7))
                        return ins
                    P.op("pe", mmo, reads=t_ws2[slot] + [t_ms[g] for g in MIX] + [t_x[dc], t_tab], writes=[t_pB[b]])
                    P.op("act", lambda e, dc=dc, b=b: e.activation(out=xb[:, dc, :], in_=pB[b][:], func=AF.Copy),
                         reads=[t_pB[b]], writes=[t_x[dc]])
                yield

        small_plan, big_plan = [], []
        MIXP = list(range(22, 28)) + [28] + list(range(29, 33))
        small_plan += list(range(0, 22))
        big_plan += list(range(0, 8))
        for ti in range(ntiles):
            pass

        out_toks = []
        xT_v = xT.rearrange("(c p) t -> p c t", p=128)
        oT_v = oT.rearrange("(c p) t -> p c t", p=128)

        def load_x(ti):
            cols = slice(ti * T, (ti + 1) * T)
            b = ti % 3
            P.dma("sp", lambda e, cols=cols, b=b: e.dma_start(out=xbs[b][:], in_=xT_v[:, :, cols]), writes=t_xs[b])

        def finalize_gen(ti):
            xb, t_x = xbs[ti % 3], t_xs[ti % 3]
            for _ in ffn_gen(2, xb, t_x, 33, 8):
                yield
            final_store(ti, xb, t_x)
            yield

        def chain(*gens):
            for g in gens:
                if g is not None:
                    for _ in g:
                        yield

        def interleave(gm, ga, per):
            k = 0
            m_done = a_done = False
            try:
                next(ga)
            except StopIteration:
                a_done = True
            while not (m_done and a_done):
                if not m_done:
                    try:
                        next(gm)
                    except StopIteration:
                        m_done = True
                n = per(k) if not m_done else 10 ** 9
                k += 1
                for _ in range(n):
                    if a_done:
                        break
                    try:
                        next(ga)
                    except StopIteration:
                        a_done = True

        def per_full(k):
            if k < 2:
                return 3
            if k < 2 + 32:
                return 1 if k % 2 == 0 else 2
            return 2

        def per_half(k):
            if k < 2:
                return 1
            if k < 2 + 32:
                return 1 if k % 2 == 1 else 0
            return 2

        def run_schedule(dry):
            load_x(0)
            for _ in ffn_gen(0, xbs[0], t_xs[0], 0, 0):
                pass
            if ntiles > 1:
                load_x(1)
            if ntiles > 2:
                load_x(2)
            for ti in range(ntiles):
                b = ti % 3
                has2 = ti >= 1
                has1 = ti + 1 < ntiles
                if has2 and has1:
                    ga = ffn_pair_gen(ti - 1, ti + 1)
                elif has2:
                    ga = finalize_gen(ti - 1)
                elif has1:
                    ga = ffn_gen(0, xbs[(ti + 1) % 3], t_xs[(ti + 1) % 3], 0, 0)
                else:
                    ga = None
                interleave(mixer_gen(ti, xbs[b], t_xs[b]), ga if ga is not None else iter(()), per_full if (has2 and has1) else per_half)
            for _ in finalize_gen(ntiles - 1):
                pass

        real_need_small, real_need_big = need_small, need_big
        real_P = P
        plan_s, plan_b = [], []

        class _Null:
            def op(self, *a, **k):
                return None

            def dma(self, *a, **k):
                return ("x", 0)
        P = _Null()

        def need_small(piece):
            plan_s.append(piece)
            return 0

        def need_big(piece):
            plan_b.append(piece)
            return 0
        run_schedule(True)
        small_plan, big_plan = plan_s, plan_b
        out_toks.clear()
        P = real_P
        need_small, need_big = real_need_small, real_need_big
        run_schedule(False)
        P.wait_all("sp", out_toks)
        P.emit(block)
    return nc


def _host_layout(inp):
    f = lambda a: np.ascontiguousarray(a, dtype=np.float32)
    L = 0
    small = np.empty((NSMALL, 128, 2048), np.float32)

    def w13(w1, w3):
        a = w1.reshape(8, 128, FC, 128).transpose(2, 1, 0, 3)
        b = w3.reshape(8, 128, FC, 128).transpose(2, 1, 0, 3)
        return np.stack([a, b], axis=2).reshape(FC, 128, 2048)
    small[0:22] = w13(inp["ffn1_w1"][L], inp["ffn1_w3"][L])
    win = inp["w_in"][L].reshape(8, 128, 12, 128)
    for m, pair in enumerate(WIN_PAIRS):
        small[22 + m] = np.stack([win[:, :, oc, :].transpose(1, 0, 2) for oc in pair], axis=1).reshape(128, 2048)
    glu = inp["ssm_glu_w"][L].reshape(4, 128, 4, 128)
    small[28] = glu.transpose(1, 2, 0, 3).reshape(128, 2048)
    wo = inp["w_out"][L].reshape(8, 128, 4, 2, 128)
    small[29:33] = wo.transpose(2, 1, 3, 0, 4).reshape(4, 128, 2048)
    small[33:55] = w13(inp["ffn2_w1"][L], inp["ffn2_w3"][L])
    big = np.empty((NBIG, 128, DFF), np.float32)
    big[0:8] = inp["ffn1_w2"][L].reshape(FC, 128, 8, 128).transpose(2, 1, 0, 3).reshape(8, 128, DFF)
    big[8:16] = inp["ffn2_w2"][L].reshape(FC, 128, 8, 128).transpose(2, 1, 0, 3).reshape(8, 128, DFF)

    def pc(v, n):
        return v.reshape(n, 128).T
    gains = np.concatenate([pc(inp["norm_ffn1"][L], 8), pc(inp["norm_mix"][L], 8),
                            pc(inp["norm_ffn2"][L], 8), pc(inp["norm_final"], 8)], axis=1)
    vecs = np.concatenate([pc(inp["conv_b"][L], 4), pc(inp["conv_ln_g"][L], 4), pc(inp["conv_ln_b"][L], 4),
                           pc(inp["conv_out_g"][L], 4), pc(inp["ssm_D"][L], 4), pc(inp["ssm_glu_b"][L], 4),
                           pc(inp["ssm_out_g"][L], 4),
                           (np.arange(128)[:, None] // 32 == np.arange(4)[None, :]).astype(np.float32)], axis=1)
    cw = inp["conv_w"][L].reshape(31, 4, 128).transpose(2, 1, 0).reshape(128, 124)
    ident = (np.arange(128)[:, None] % 32 == np.arange(32)[None, :]).astype(np.float32)
    A_re, A_im, ldt = inp["ssm_A_re"][L], inp["ssm_A_im"][L], inp["ssm_log_dt"][L]

    def ql(a):
        return a.reshape(16, 2, 64).transpose(1, 2, 0).reshape(128, 16)
    aq = np.concatenate([ql(A_re), ql(A_im), ql(np.broadcast_to(ldt[:, None], (32, 64)))], axis=1)
    B_re, B_im = inp["ssm_B_re"][L], inp["ssm_B_im"][L]
    sc = np.zeros((5, 128, 4, 128), np.float32)
    for g in range(32):
        uc, g8 = divmod(g, 8)
        rows = slice(16 * g8, 16 * g8 + 16)
        sc[0, rows, uc, :] = np.tile(A_re[g], 2)[None, :]
        sc[1, rows, uc, :] = np.tile(A_im[g], 2)[None, :]
        sc[2, rows, uc, :] = ldt[g]
        qo = 64 * (g % 2)
        sc[3, rows, uc, qo:qo + 64] = B_re[g].T
        sc[4, rows, uc, qo:qo + 64] = B_im[g].T
    sc = sc.transpose(1, 0, 2, 3).reshape(128, 5 * 512)
    C_re, C_im = inp["ssm_C_re"][L], inp["ssm_C_im"][L]
    cq = np.zeros((2, 128, 16, 32), np.float32)
    for g in range(32):
        gc, g2 = divmod(g, 2)
        cq[0, 64 * g2:64 * g2 + 64, gc, 16 * g2:16 * g2 + 16] = C_re[g].T
        cq[1, 64 * g2:64 * g2 + 64, gc, 16 * g2:16 * g2 + 16] = C_im[g].T
    cq = cq.transpose(1, 0, 2, 3).reshape(128, 2 * 512)
    idf = np.concatenate([np.eye(128, dtype=np.float32), 2.0 * np.eye(128, dtype=np.float32)], axis=1)
    return dict(wsmall=small, wbig=big, gains=f(gains), vecs=f(vecs), cw=f(cw), ident=ident, idf=idf,
                aq=f(aq), sc=f(sc), cq=f(cq))


_NC_CACHE = {}


def kernel(**inputs):
    inp = {k: np.asarray(v) for k, v in inputs.items()}
    x = inp["x"]
    shared = _host_layout(inp)
    in_maps = []
    for c in range(NCORES):
        xc = x[NSEQ * c:NSEQ * (c + 1)].reshape(NSEQ * SEQ, D)
        m = dict(shared)
        m["xT"] = np.ascontiguousarray(xc.T)
        in_maps.append(m)
    if "nc" not in _NC_CACHE:
        _NC_CACHE["nc"] = build_nc()
    res = run_bass_kernel_spmd(_NC_CACHE["nc"], in_maps, core_ids=list(range(NCORES)))
    out = np.empty((16, SEQ, D), np.float32)
    for c in range(NCORES):
        o = res.results[c]["oT"]
        out[NSEQ * c:NSEQ * (c + 1)] = np.ascontiguousarray(o.T).reshape(NSEQ, SEQ, D)
    return out
```
